# Optimizing a Trainium2 kernel written in Bass

```python
import math
import jax, jax.numpy as jnp
from jax import lax
import numpy as np

D_MODEL = 1024
BATCH = 2
SEQ = 8192
DEPTH = 1

D_MIX = D_MODEL
ATT_HEADS = 8
HEAD_DIM = 64
D_ATT = ATT_HEADS * HEAD_DIM
D_CONV = D_MIX - D_ATT
DILATED_PATTERNS = ((128, 1), (512, 4), (2048, 16))
ROPE_THETA = 500000.0
ROT_DIM = HEAD_DIM // 4
CONV_WIDTH = 31
N_MEM = 256
XATT_HEADS = 4
XATT_HEAD_DIM = D_MODEL // XATT_HEADS
D_FF = 4 * D_MODEL
D_IN = 3 * D_ATT + 2 * D_CONV
EPS = 1e-6
NEG_INF = -1e30

kernel_name = "hybrid_dilated_swa_conformer_encoder"


def rmsnorm(x, g):
    xf = x.astype(jnp.float32)
    var = jnp.mean(xf * xf, axis=-1, keepdims=True)
    return (xf * lax.rsqrt(var + EPS) * g.astype(jnp.float32)).astype(x.dtype)


def layernorm(x, g, b):
    xf = x.astype(jnp.float32)
    mu = jnp.mean(xf, axis=-1, keepdims=True)
    var = jnp.mean(jnp.square(xf - mu), axis=-1, keepdims=True)
    y = (xf - mu) * lax.rsqrt(var + EPS) * g.astype(jnp.float32) + b.astype(jnp.float32)
    return y.astype(x.dtype)


def partial_rotary(t):
    S = t.shape[1]
    half = ROT_DIM // 2
    freqs = ROPE_THETA ** (-jnp.arange(0, ROT_DIM, 2, dtype=jnp.float32) / ROT_DIM)
    ang = jnp.arange(S, dtype=jnp.float32)[:, None] * freqs[None, :]
    cos = jnp.cos(ang)[None, :, None, :]
    sin = jnp.sin(ang)[None, :, None, :]
    tf = t.astype(jnp.float32)
    x1, x2, rest = tf[..., :half], tf[..., half:ROT_DIM], tf[..., ROT_DIM:]
    rot = jnp.concatenate([x1 * cos - x2 * sin, x2 * cos + x1 * sin, rest], axis=-1)
    return rot.astype(t.dtype)


def to_strided(t, d):
    B, S = t.shape[:2]
    rest = t.shape[2:]
    t = jnp.moveaxis(t.reshape(B, S // d, d, *rest), 2, 1)
    return t.reshape(B * d, S // d, *rest)


def from_strided(t, d, B):
    N, L = t.shape[:2]
    rest = t.shape[2:]
    t = jnp.moveaxis(t.reshape(B, d, L, *rest), 1, 2)
    return t.reshape(B, L * d, *rest)


def banded_attention(q, k, v, half):
    N, L, H, Dh = q.shape
    blk = half
    nb = -(-L // blk)
    Lp = nb * blk
    pad = Lp - L
    qb = jnp.pad(q, ((0, 0), (0, pad), (0, 0), (0, 0))).reshape(N, nb, blk, H, Dh)
    kp = jnp.pad(k, ((0, 0), (blk, blk + pad), (0, 0), (0, 0)))
    vp = jnp.pad(v, ((0, 0), (blk, blk + pad), (0, 0), (0, 0)))

    def window(t):
        return jnp.concatenate(
            [t[:, i * blk:i * blk + Lp].reshape(N, nb, blk, H, Dh) for i in range(3)], axis=2)

    kb, vb = window(kp), window(vp)
    s = jnp.einsum('nbqhd,nbkhd->nbhqk', qb, kb).astype(jnp.float32) * (Dh ** -0.5)
    qpos = jnp.arange(nb)[:, None] * blk + jnp.arange(blk)[None, :]
    kpos = jnp.arange(nb)[:, None] * blk - blk + jnp.arange(3 * blk)[None, :]
    valid = ((jnp.abs(kpos[:, None, :] - qpos[:, :, None]) <= half)
             & (kpos >= 0)[:, None, :] & (kpos < L)[:, None, :])
    s = jnp.where(valid[None, :, None], s, NEG_INF)
    m = jnp.max(s, axis=-1, keepdims=True)
    p = jnp.exp(s - m)
    den = jnp.sum(p, axis=-1, keepdims=True)
    o = jnp.einsum('nbhqk,nbkhd->nbqhd', (p / den).astype(v.dtype), vb)
    lse = (m + jnp.log(den))[..., 0]
    o = o.reshape(N, Lp, H, Dh)[:, :L]
    lse = jnp.transpose(lse, (0, 1, 3, 2)).reshape(N, Lp, H)[:, :L]
    return o, lse


def dilated_sliding_attention(q, k, v):
    B = q.shape[0]
    outs, lses = [], []
    for window, d in DILATED_PATTERNS:
        half = window // (2 * d)
        o, lse = banded_attention(to_strided(q, d), to_strided(k, d), to_strided(v, d), half)
        outs.append(from_strided(o, d, B))
        lses.append(from_strided(lse, d, B))
    w = jax.nn.softmax(jnp.stack(lses, axis=0), axis=0)
    o = jnp.sum(w[..., None] * jnp.stack(outs, axis=0).astype(jnp.float32), axis=0)
    return o.astype(q.dtype)


def conformer_conv(a, g, conv_w, conv_b, ln_g, ln_b):
    u = a * jax.nn.sigmoid(g)
    C = u.shape[-1]
    u = lax.conv_general_dilated(
        u, conv_w.reshape(CONV_WIDTH, 1, C).astype(u.dtype),
        window_strides=(1,), padding=[((CONV_WIDTH - 1) // 2, (CONV_WIDTH - 1) // 2)],
        dimension_numbers=('NWC', 'WIO', 'NWC'), feature_group_count=C) + conv_b
    u = layernorm(u, ln_g, ln_b)
    return jax.nn.silu(u)


def setup_inputs(seed: int = 0) -> dict:
    key = jax.random.key(seed)
    ks = jax.random.split(key, 20)
    f32 = jnp.float32

    def w(k, shape, fan_in):
        return jax.random.normal(k, shape, f32) * (fan_in ** -0.5)

    def gain(k, shape):
        return 1.0 + 0.02 * jax.random.normal(k, shape, f32)

    return {
        "x": jax.random.normal(ks[0], (BATCH, SEQ, D_MODEL), f32),
        "mem": jax.random.normal(ks[1], (BATCH, N_MEM, D_MODEL), f32),
        "norm_mix_g": gain(ks[2], (DEPTH, D_MODEL)),
        "w_in": w(ks[3], (DEPTH, D_MODEL, D_IN), D_MODEL),
        "conv_w": w(ks[4], (DEPTH, CONV_WIDTH, D_CONV), CONV_WIDTH),
        "conv_b": 0.02 * jax.random.normal(ks[5], (DEPTH, D_CONV), f32),
        "conv_ln_g": gain(ks[6], (DEPTH, D_CONV)),
        "conv_ln_b": 0.02 * jax.random.normal(ks[7], (DEPTH, D_CONV), f32),
        "w_out": w(ks[8], (DEPTH, D_MIX, D_MODEL), D_MIX),
        "norm_x_g": gain(ks[9], (DEPTH, D_MODEL)),
        "norm_mem_g": gain(ks[10], (DEPTH, D_MODEL)),
        "w_xq": w(ks[11], (DEPTH, D_MODEL, D_MODEL), D_MODEL),
        "w_xk": w(ks[12], (DEPTH, D_MODEL, D_MODEL), D_MODEL),
        "w_xv": w(ks[13], (DEPTH, D_MODEL, D_MODEL), D_MODEL),
        "w_xo": w(ks[14], (DEPTH, D_MODEL, D_MODEL), D_MODEL),
        "norm_mlp_g": gain(ks[15], (DEPTH, D_MODEL)),
        "w_up": w(ks[16], (DEPTH, D_MODEL, D_FF), D_MODEL),
        "w_down": w(ks[17], (DEPTH, D_FF, D_MODEL), D_FF),
        "norm_final_g": gain(ks[18], (D_MODEL,)),
    }


def reference(x, mem, norm_mix_g, w_in, conv_w, conv_b, conv_ln_g, conv_ln_b, w_out,
              norm_x_g, norm_mem_g, w_xq, w_xk, w_xv, w_xo, norm_mlp_g, w_up, w_down,
              norm_final_g):
    B, S, _ = x.shape
    M = mem.shape[1]
    h = x
    for l in range(DEPTH):
        y = rmsnorm(h, norm_mix_g[l]) @ w_in[l]
        q = partial_rotary(y[..., 0:D_ATT].reshape(B, S, ATT_HEADS, HEAD_DIM))
        k = partial_rotary(y[..., D_ATT:2 * D_ATT].reshape(B, S, ATT_HEADS, HEAD_DIM))
        v = y[..., 2 * D_ATT:3 * D_ATT].reshape(B, S, ATT_HEADS, HEAD_DIM)
        att = dilated_sliding_attention(q, k, v).reshape(B, S, D_ATT)
        c0 = 3 * D_ATT
        conv = conformer_conv(y[..., c0:c0 + D_CONV], y[..., c0 + D_CONV:c0 + 2 * D_CONV],
                              conv_w[l], conv_b[l], conv_ln_g[l], conv_ln_b[l])
        h = h + jnp.concatenate([att, conv], axis=-1) @ w_out[l]

        xq = (rmsnorm(h, norm_x_g[l]) @ w_xq[l]).reshape(B, S, XATT_HEADS, XATT_HEAD_DIM)
        mn = rmsnorm(mem, norm_mem_g[l])
        xk = (mn @ w_xk[l]).reshape(B, M, XATT_HEADS, XATT_HEAD_DIM)
        xv = (mn @ w_xv[l]).reshape(B, M, XATT_HEADS, XATT_HEAD_DIM)
        sc = jnp.einsum('bshd,bmhd->bhsm', xq, xk).astype(jnp.float32) * (XATT_HEAD_DIM ** -0.5)
        pr = jax.nn.softmax(sc, axis=-1).astype(xv.dtype)
        xo = jnp.einsum('bhsm,bmhd->bshd', pr, xv).reshape(B, S, D_MODEL)
        h = h + xo @ w_xo[l]

        u = rmsnorm(h, norm_mlp_g[l]) @ w_up[l]
        h = h + jnp.square(jax.nn.relu(u)) @ w_down[l]
    return rmsnorm(h, norm_final_g)
```

```python
import numpy as np
import concourse.bass as bass
import concourse.mybir as mybir
from concourse.bass_utils import run_bass_kernel_spmd
from contextlib import ExitStack
import types

F32 = mybir.dt.float32
BF16 = mybir.dt.bfloat16
AF = mybir.ActivationFunctionType
ALU = mybir.AluOpType
AX = mybir.AxisListType

S = 8192
D = 1024
T = 2048
HALO = 1024
E = T + 2 * HALO
NCORES = 8
DIN = 2560
DFF = 4096
NMEM = 256
EPS = 1e-6
PATTERNS = (1, 4, 16)
NKT = {1: 17, 4: 5, 16: 2}
ARENA_BYTES = 207 * 1024

ENGS = ("sp", "act", "pool", "dve", "pe")
import os as _os
TRACE = open(_os.environ['K_TRACE'], 'w') if _os.environ.get('K_TRACE') else None


class Res:
    __slots__ = ("name", "w", "rs")

    def __init__(self, name=""):
        self.name = name
        self.w = None
        self.rs = []


class Op:
    __slots__ = ("eng", "fn", "reads", "writes", "dma", "dsem", "dcount", "deps", "signal", "sigval", "waits")

    def __init__(self, eng, fn, reads, writes, dma):
        self.eng = eng
        self.fn = fn
        self.reads = reads
        self.writes = writes
        self.dma = dma
        self.dsem = None
        self.dcount = 0
        self.deps = []
        self.signal = False
        self.sigval = None
        self.waits = []


class Ctx:
    def __init__(self, nc, stack):
        self.nc = nc
        self.stack = stack
        self.esem = {e: stack.enter_context(nc.semaphore("s_" + e)) for e in ENGS}
        self.ecount = {e: 0 for e in ENGS}
        self.dsems = {}
        self.dcounts = {}
        self.waited = {e: {} for e in ENGS}
        self.all_res = []
        self.ops = []

    def res(self, name=""):
        r = Res(name)
        self.all_res.append(r)
        return r

    def dsem(self, key):
        if key not in self.dsems:
            self.dsems[key] = self.stack.enter_context(self.nc.semaphore("d_" + key))
            self.dcounts[key] = 0
        return self.dsems[key]

    def add(self, eng, fn, reads=(), writes=(), dma=None):
        if fn.__closure__:
            cells = []
            for c in fn.__closure__:
                try:
                    cells.append(types.CellType(c.cell_contents))
                except ValueError:
                    cells.append(c)
            f2 = types.FunctionType(fn.__code__, fn.__globals__, fn.__name__, fn.__defaults__, tuple(cells))
            f2.__kwdefaults__ = fn.__kwdefaults__
            fn = f2
        op = Op(eng, fn, tuple(reads), tuple(writes), dma)
        deps = []
        for r in op.reads:
            if r.w is not None:
                deps.append((r.w, True))
        for r in op.writes:
            if r.w is not None:
                deps.append((r.w, False))
            for o in r.rs:
                deps.append((o, False))
        for r in op.reads:
            if r not in op.writes:
                r.rs.append(op)
        for r in op.writes:
            r.w = op
            r.rs = []
        if dma is not None:
            self.dsem(dma)
            self.dcounts[dma] += 16
            op.dsem = dma
            op.dcount = self.dcounts[dma]
        seen = {}
        for d, raw in deps:
            if d is op:
                continue
            k = id(d)
            if k in seen:
                seen[k] = (d, seen[k][1] or raw)
            else:
                seen[k] = (d, raw)
        for d, raw in seen.values():
            need = False
            if d.dma is not None:
                need = True
            elif d.eng != eng:
                need = True
            elif eng != "pe":
                need = True
            if need:
                op.waits.append(d)
                if d.dma is None:
                    d.signal = True
        self.ops.append(op)
        return op

    def emit_phase(self):
        nc = self.nc
        ops = self.ops
        self.ops = []
        per = {e: [o for o in ops if o.eng == e] for e in ENGS}
        last = {}
        for e in ENGS:
            for o in reversed(per[e]):
                if o.dma is None:
                    o.signal = True
                    last[e] = o
                    break
        for e in ENGS:
            for o in per[e]:
                if o.dma is None and o.signal:
                    self.ecount[e] += 1
                    o.sigval = self.ecount[e]
        final = {e: self.ecount[e] for e in ENGS if e in last}

        def run(e, engine):
            waited = self.waited[e]
            for o in per[e]:
                need = {}
                for d in o.waits:
                    if d.dma is not None:
                        key = ("d", d.dsem)
                        val = d.dcount
                    else:
                        key = ("e", d.eng)
                        val = d.sigval
                    if val > need.get(key, 0):
                        need[key] = val
                for key, val in need.items():
                    if waited.get(key, 0) >= val:
                        continue
                    sem = self.dsems[key[1]] if key[0] == "d" else self.esem[key[1]]
                    engine.wait_ge(sem, val)
                    waited[key] = val
                    if TRACE is not None:
                        TRACE.write("%s WAIT %s >= %d\n" % (e, key, val))
                ins = o.fn(engine)
                if TRACE is not None:
                    TRACE.write("%s OP %s sig=%s dma=%s\n" % (e, getattr(o.fn, '__qualname__', '?') + ':' + str(o.fn.__code__.co_firstlineno), o.sigval, o.dsem))
                if o.dma is not None:
                    ins.then_inc(self.dsems[o.dsem], 16)
                elif o.signal:
                    ins.then_inc(self.esem[e], 1)
            for f, val in final.items():
                if f == e:
                    continue
                key = ("e", f)
                if waited.get(key, 0) >= val:
                    continue
                engine.wait_ge(self.esem[f], val)
                waited[key] = val

        with nc.Block() as block:
            @block.sync
            def _(eng):
                run("sp", eng)

            @block.scalar
            def _(eng):
                run("act", eng)

            @block.gpsimd
            def _(eng):
                run("pool", eng)

            @block.vector
            def _(eng):
                run("dve", eng)

            @block.tensor
            def _(eng):
                run("pe", eng)
        for r in self.all_res:
            if r.w is not None and r.w.dma is None:
                r.w = None
            r.rs = [o for o in r.rs if o.dma is not None]

    def final_wait_all_dma(self, keys):
        nc = self.nc
        with nc.Block() as block:
            @block.sync
            def _(eng):
                for k in keys:
                    eng.wait_ge(self.dsems[k], self.dcounts[k])


def build_program(stop_after=99, dbg=False):
    nc = bass.Bass("TRN2", target_bir_lowering=False)

    def din(name, shape, dt=F32):
        return nc.dram_tensor(name, list(shape), dt, kind="ExternalInput").ap()

    x_ext = din("x_ext", [E, D])
    mem = din("mem", [NMEM, D])
    tabC = din("tabC", [128, E])
    tabS = din("tabS", [128, E])
    cst = din("cst", [128, 128 * 3 + 1024])
    kbias_d = din("kbias", [128, 72])
    vrep_d = din("vrep", [128, 69 * 64])
    gvec = din("gvec", [5, D])
    convp = din("convp", [128, 4 * 31 + 12])
    w_in = din("w_in", [D, DIN])
    w_out = din("w_out", [D, D])
    w_xq = din("w_xq", [D, D])
    w_xk = din("w_xk", [D, D])
    w_xv = din("w_xv", [D, D])
    w_xo = din("w_xo", [D, D])
    w_up = din("w_up", [D, DFF])
    w_down = din("w_down", [DFF, D])
    out = nc.dram_tensor("out", [T, D], F32, kind="ExternalOutput").ap()
    dbg_out = None
    if dbg:
        dbg_out = nc.dram_tensor("dbg", [128, 20 * 1024], F32, kind="ExternalOutput").ap()

    stack = ExitStack()
    with stack:
        arena = stack.enter_context(nc.sbuf_tensor("arena", [128, ARENA_BYTES // 2], BF16))
        psf = [stack.enter_context(nc.psum_tensor("psf%d" % i, [128, 512], F32)) for i in range(7)]
        psb_h = stack.enter_context(nc.psum_tensor("psb", [128, 1024], BF16))
        psb = psb_h[:, :]
        psf = [p[:, :] for p in psf]
        cx = Ctx(nc, stack)

        def A(off, dt, n):
            assert off % 4 == 0
            if dt == BF16:
                assert off + 2 * n <= ARENA_BYTES, (off, n)
                return arena[:, off // 2: off // 2 + n]
            assert off + 4 * n <= ARENA_BYTES, (off, n)
            return arena[:, off // 2: off // 2 + 2 * n].bitcast(F32)

        KB = 1024
        psum_res = [cx.res("psf%d" % i) for i in range(7)]
        psb_res = cx.res("psb")
        bank_rr = [0]

        def bank():
            i = bank_rr[0] % 6
            bank_rr[0] += 1
            return psf[i], psum_res[i]

        psb2 = psf[6].bitcast(BF16)
        trbufs = [(psb, psb_res), (psb2, psum_res[6])]
        tr_ctr = [0]

        o = 0
        ident = A(o, BF16, 128); o += 256
        Rm = A(o, BF16, 128); o += 256
        onesm = A(o, BF16, 128); o += 256
        mask2 = A(o, BF16, 1024); o += 2048
        ones = A(o, BF16, 128); o += 256
        kbias = A(o, F32, 72); o += 288
        epsc = A(o, F32, 1); o += 4
        cprm = A(o, F32, 4 * 31 + 12); o += 544
        cw = cprm[:, 0:124].rearrange("p (c k) -> p c k", c=4)
        cb = cprm[:, 124:128]
        lng = cprm[:, 128:132]
        lnb = cprm[:, 132:136]
        r_cprm = cx.res("cprm")
        o = 4 * KB
        SMALL_END = o
        r_const = cx.res("const")

        cx.add("pool", lambda e: e.dma_start(out=arena[:, 0:1408], in_=cst), writes=[r_const], dma="cst")
        cx.add("sp", lambda e: e.dma_start(out=cprm, in_=convp), writes=[r_cprm], dma="cprm")
        r_ones = cx.res("ones")
        r_eps = cx.res("eps")
        cx.add("dve", lambda e: e.memset(ones, 1.0), writes=[r_ones])
        cx.add("dve", lambda e: e.memset(epsc, EPS), writes=[r_eps])

        def wload(dst, dst_res, src, key, nsplit=1):
            cx.add("pool", (lambda e: e.dma_start(out=dst, in_=src)), writes=list(dict.fromkeys(dst_res)), dma=key)

        def rms_stats(srcs, src_res, ss, sq, rstd, r_ss, junk, r_junk):
            n = len(srcs)
            for j, (sap, sr) in enumerate(zip(srcs, src_res)):
                cx.add("act", (lambda e, sap=sap, j=j: e.activation(junk, sap, AF.Square, scale=1.0 / 32.0,
                                                                   accum_out=ss[:, j:j + 1])),
                       reads=[sr], writes=[r_junk, r_ss])
            cx.add("act", lambda e: e.activation(sq[:, 0:n], ss[:, 0:n], AF.Sqrt, bias=epsc, scale=1.0),
                   reads=[r_ss, r_eps], writes=[r_ss])
            cx.add("dve", lambda e: e.reciprocal(rstd[:, 0:n], sq[:, 0:n]), reads=[r_ss], writes=[r_ss])

        def transpose_tile(xn_ap, r_xn, dstT, r_dstT, col0, evac_eng):
            tb, tres = trbufs[tr_ctr[0] % 2]
            tr_ctr[0] += 1
            for k in range(8):
                cx.add("pe", (lambda e, k=k: e.transpose(tb[:, k * 128:(k + 1) * 128], xn_ap[:, k * 128:(k + 1) * 128], ident)),
                       reads=[r_xn, r_const], writes=[tres])
            src = tb.rearrange("p (k t) -> p k t", k=8)
            dst = dstT[:, :, col0:col0 + 128]
            if evac_eng == "act":
                cx.add("act", lambda e: e.copy(dst, src), reads=[tres], writes=[r_dstT])
            else:
                cx.add("dve", lambda e: e.tensor_copy(dst, src), reads=[tres], writes=[r_dstT])

        o = SMALL_END
        QT = A(o, BF16, 4 * T).rearrange("p (c t) -> p c t", c=4); o += 16 * KB
        KT = A(o, BF16, 4 * E).rearrange("p (c t) -> p c t", c=4); o += 32 * KB
        VT = A(o, BF16, 4 * E).rearrange("p (c t) -> p c t", c=4); o += 32 * KB
        UW = T + 32
        uT = A(o, BF16, 4 * UW).rearrange("p (c t) -> p c t", c=4); o += 4 * UW * 2
        o = (o + 1023) // 1024 * 1024
        P1_BASE = o
        r_QT = [cx.res("QT%d" % b) for b in range(4)]
        r_KT = [cx.res("KT%d" % b) for b in range(8)]
        r_VT = [cx.res("VT%d" % b) for b in range(8)]
        r_uT = [cx.res("uT%d" % b) for b in range(6)]

        o = P1_BASE
        winb = A(o, BF16, 8 * DIN).rearrange("p (k n) -> p k n", k=8); o += 40 * KB
        xnT = [A(o + i * 8 * KB, BF16, 8 * 512).rearrange("p (k t) -> p k t", k=8) for i in range(2)]; o += 16 * KB
        r_xnT = [cx.res("xnT%d" % i) for i in range(2)]
        xt = [A(o + i * 4 * KB, F32, 1024) for i in range(4)]; o += 16 * KB
        r_xt = [cx.res("xt%d" % i) for i in range(4)]
        xn = [A(o + i * 2 * KB, BF16, 1024) for i in range(2)]; o += 4 * KB
        r_xn = [cx.res("xn%d" % i) for i in range(2)]
        tC = [A(o + i * 2 * KB, F32, 512) for i in range(2)]; o += 4 * KB
        tS = [A(o + i * 2 * KB, F32, 512) for i in range(2)]; o += 4 * KB
        r_tabC = [cx.res("tabC%d" % i) for i in range(2)]
        r_tabS = [cx.res("tabS%d" % i) for i in range(2)]
        qraw = [A(o + i * KB, BF16, 512) for i in range(2)]; o += 2 * KB
        r_qraw = [cx.res("qraw%d" % i) for i in range(2)]
        tmpA = [A(o + i * 2 * KB, F32, 512) for i in range(2)]; o += 4 * KB
        r_tmpA = [cx.res("tmpA%d" % i) for i in range(2)]
        tmpB = [A(o + i * 2 * KB, F32, 512) for i in range(2)]; o += 4 * KB
        r_tmpB = [cx.res("tmpB%d" % i) for i in range(2)]
        sg = [A(o + i * 2 * KB, F32, 512) for i in range(2)]; o += 4 * KB
        r_sg = [cx.res("sg%d" % i) for i in range(2)]
        grep = A(o, F32, 1024); o += 4 * KB
        r_grep = cx.res("grep")
        junk = A(o, BF16, 1024); o += 2 * KB
        r_junk = cx.res("junk")
        ss = A(o, F32, 32).rearrange("p (b j) -> p b j", b=8); o += 128
        sq = A(o, F32, 32).rearrange("p (b j) -> p b j", b=8); o += 128
        rstd = A(o, F32, 32).rearrange("p (b j) -> p b j", b=8); o += 128
        r_ss = [cx.res("ss%d" % b) for b in range(8)]
        assert o <= ARENA_BYTES, o

        w_in_v = w_in.rearrange("(k p) n -> p k n", p=128)
        r_win = [cx.res("win%d" % i) for i in range(5)]
        for g_ in (1, 2, 0, 3, 4):
            cx.add("pool", (lambda e: e.dma_start(out=winb[:, :, g_ * 512:(g_ + 1) * 512], in_=w_in_v[:, :, g_ * 512:(g_ + 1) * 512])),
                   writes=[r_win[g_]], dma="win%d" % g_)
        cx.add("sp", lambda e: e.dma_start(out=grep, in_=gvec[0:1, :].partition_broadcast(128)),
               writes=[r_grep], dma="grep")

        xe_t = x_ext.rearrange("(n p) d -> n p d", p=128)
        tile_ctr = [0]
        ev_ctr = [0]

        def load_x_tile(n, dst, dres, key):
            cx.add("sp", (lambda e: e.dma_start(out=dst, in_=xe_t[n])), writes=[dres], dma=key)

        def norm_pre(b, half):
            idx = [2 * half, 2 * half + 1]
            for i in idx:
                load_x_tile(b * 4 + i, xt[i], r_xt[i], "xt%d" % i)
            rms_stats([xt[i] for i in idx], [r_xt[i] for i in idx],
                      ss[:, b, 2 * half:2 * half + 2], sq[:, b, 2 * half:2 * half + 2],
                      rstd[:, b, 2 * half:2 * half + 2], r_ss[b], junk, r_junk)
            for i in idx:
                cx.add("dve", (lambda e: e.scalar_tensor_tensor(xn[i % 2], xt[i], rstd[:, b, i:i + 1], grep, ALU.mult, ALU.mult)),
                       reads=[r_xt[i], r_ss[b], r_grep], writes=[r_xn[i % 2]])

        def norm_tr(b, half):
            bs = b % 2
            for i in (2 * half, 2 * half + 1):
                ev_ctr[0] += 1
                transpose_tile(xn[i % 2], r_xn[i % 2], xnT[bs], r_xnT[bs], i * 128, "act" if ev_ctr[0] % 2 else "dve")

        def proj_chunk(bs, c0, n0=0, n=512):
            pb, pr = bank()
            for k in range(8):
                cx.add("pe", (lambda e, k=k: e.matmul(pb[:, 0:n], winb[:, k, c0:c0 + 128], xnT[bs][:, k, n0:n0 + n],
                                                      start=(k == 0), stop=(k == 7))),
                       reads=[r_xnT[bs], r_win[c0 // 512]], writes=[pr])
            flush_rot()
            return pb, pr

        rot_ctr = [0]

        pending_rot = []

        def flush_rot():
            while pending_rot:
                pending_rot.pop(0)()

        def rotary_evac(pb, pr, ts_, dst, dres):
            s_ = rot_ctr[0] % 2
            rot_ctr[0] += 1
            cx.add("act", lambda e: e.copy(qraw[s_], pb), reads=[pr], writes=[r_qraw[s_]])

            def rest():
                qb, qr = bank()
                cx.add("pe", lambda e: e.matmul(qb, Rm, qraw[s_], start=True, stop=True), reads=[r_qraw[s_], r_const], writes=[qr])
                cx.add("dve", lambda e: e.tensor_tensor(tmpA[s_], qb, tS[ts_], ALU.mult), reads=[qr, r_tabS[ts_]], writes=[r_tmpA[s_]])
                cx.add("pool", lambda e: e.tensor_tensor(tmpB[s_], qraw[s_], tC[ts_], ALU.mult), reads=[r_qraw[s_], r_tabC[ts_]], writes=[r_tmpB[s_]])
                cx.add("dve", lambda e: e.tensor_tensor(dst, tmpA[s_], tmpB[s_], ALU.add), reads=[r_tmpA[s_], r_tmpB[s_]], writes=[dres])
            pending_rot.append(rest)

        glu_ctr = [0]
        vev = [0]

        def k_chunk(b, pr_):
            pb, pres = proj_chunk(b % 2, 512 + pr_ * 128)
            rotary_evac(pb, pres, b % 2, KT[:, pr_, b * 512:(b + 1) * 512], r_KT[b])

        def v_chunk(b, pr_):
            pb, pres = proj_chunk(b % 2, 1024 + pr_ * 128)
            dst = VT[:, pr_, b * 512:(b + 1) * 512]
            vev[0] += 1
            if vev[0] % 2:
                cx.add("act", (lambda e: e.copy(dst, pb)), reads=[pres], writes=[r_VT[b]])
            else:
                cx.add("dve", (lambda e: e.tensor_copy(dst, pb)), reads=[pres], writes=[r_VT[b]])

        def q_chunk(b, pr_):
            ob = b - 2
            pb, pres = proj_chunk(b % 2, pr_ * 128)
            rotary_evac(pb, pres, b % 2, QT[:, pr_, ob * 512:(ob + 1) * 512], r_QT[ob])

        def glu_chunk(b, c):
            if 2 <= b <= 5:
                n0, n, ucol, ures = 0, 512, 16 + (b - 2) * 512, r_uT[b - 2]
            elif b == 1:
                n0, n, ucol, ures = 496, 16, 0, r_uT[4]
            else:
                n0, n, ucol, ures = 0, 16, 16 + T, r_uT[5]
            pa, pares = proj_chunk(b % 2, 1536 + c * 128, n0, n)
            pg, pgres = proj_chunk(b % 2, 2048 + c * 128, n0, n)
            s_ = glu_ctr[0] % 2
            glu_ctr[0] += 1
            cx.add("act", (lambda e: e.activation(sg[s_][:, 0:n], pg[:, 0:n], AF.Sigmoid)), reads=[pgres], writes=[r_sg[s_]])
            dst = uT[:, c, ucol:ucol + n]
            cx.add("dve", (lambda e: e.tensor_tensor(dst, pa[:, 0:n], sg[s_][:, 0:n], ALU.mult)),
                   reads=[pares, r_sg[s_]], writes=[ures])

        def load_tables(b):
            ts_ = b % 2
            cx.add("sp", (lambda e: e.dma_start(out=tC[ts_], in_=tabC[:, b * 512:(b + 1) * 512])), writes=[r_tabC[ts_]], dma="tabC%d" % ts_)
            cx.add("sp", (lambda e: e.dma_start(out=tS[ts_], in_=tabS[:, b * 512:(b + 1) * 512])), writes=[r_tabS[ts_]], dma="tabS%d" % ts_)

        load_tables(0)
        norm_pre(0, 0)
        norm_tr(0, 0)
        norm_pre(0, 1)
        norm_tr(0, 1)
        norm_pre(1, 0)
        for b in range(8):
            own = 2 <= b <= 5
            work = [(k_chunk, pr_) for pr_ in range(4)] + [(v_chunk, pr_) for pr_ in range(4)]
            if own:
                work += [(q_chunk, pr_) for pr_ in range(4)]
            if own or b == 1 or b == 6:
                work += [(glu_chunk, c) for c in range(4)]
            n_w = len(work)
            marks = {n_w // 3: 0, (2 * n_w) // 3: 1}
            for wi, (fn_, arg_) in enumerate(work):
                if b + 1 < 8 and wi in marks:
                    if marks[wi] == 0:
                        load_tables(b + 1)
                        norm_tr(b + 1, 0)
                        norm_pre(b + 1, 1)
                    else:
                        norm_tr(b + 1, 1)
                        if b + 2 < 8:
                            norm_pre(b + 2, 0)
                fn_(b, arg_)
        flush_rot()

        def dump(ap_list):
            col = 0
            rr = cx.res("dbgstg")
            for ap in ap_list:
                n = ap.shape[1]
                stg = A(ARENA_BYTES - 8 * KB, F32, 2048)
                for c0 in range(0, n, 2048):
                    m = min(2048, n - c0)
                    cx.add("dve", (lambda e, ap=ap, c0=c0, m=m: e.tensor_copy(stg[:, 0:m], ap[:, c0:c0 + m])), writes=[rr])
                    cx.add("sp", (lambda e, col=col, c0=c0, m=m: e.dma_start(out=dbg_out[:, col + c0:col + c0 + m], in_=stg[:, 0:m])),
                           reads=[rr], dma="dbg")
                    cx.emit_phase()
                col += n

        cx.emit_phase()
        if stop_after == 1:
            if dbg:
                dump([QT[:, 0, :], KT[:, 0, :], VT[:, 0, :], uT[:, 0, :], QT[:, 3, :], KT[:, 3, :]])
                cx.final_wait_all_dma(["dbg"])
            return nc

        o = P1_BASE
        attT = A(o, BF16, 4 * T).rearrange("p (c t) -> p c t", c=4); o += 16 * KB
        convT = A(o, BF16, 4 * T).rearrange("p (c t) -> p c t", c=4); o += 16 * KB
        P2_BASE = o
        r_attT = cx.res("attT")
        acc = A(o, F32, 2 * 2 * T).rearrange("p (a c t) -> p a c t", a=2, c=2); o += 32 * KB
        r_acc = cx.res("acc")
        Vt = A(o, BF16, 32 * 256).rearrange("p (k f) -> p k f", k=32); o += 16 * KB
        r_Vt = [cx.res("Vt%d" % i) for i in range(8)]
        pexp = [A(o + i * KB, BF16, 512) for i in range(6)]; o += 6 * KB
        r_pexp = [cx.res("pexp%d" % i) for i in range(6)]
        pm = [A(o + i * KB, BF16, 512) for i in range(6)]; o += 6 * KB
        r_pm = [cx.res("pm%d" % i) for i in range(6)]
        obanks = [(psf[6], psum_res[6]), (psb.bitcast(F32), psb_res)]
        vrep = A(o, BF16, 69 * 64).rearrange("p (k f) -> p k f", k=69); o += 69 * 128
        r_vrep = cx.res("vrep")
        cx.add("pool", lambda e: e.dma_start(out=vrep, in_=vrep_d.rearrange("p (k f) -> p k f", k=69)), writes=[r_vrep], dma="vrep")
        assert o <= ARENA_BYTES
        r_KTh = [cx.res("KTlo"), cx.res("KThi")]
        r_VTh = [cx.res("VTlo"), cx.res("VThi")]
        r_QTall = r_QT
        diagA = A(20 * KB, BF16, 2 * 31 * 128).rearrange("p (c k m) -> p c k m", c=2, k=31)
        diagB = A(52 * KB, BF16, 2 * 31 * 128).rearrange("p (c k m) -> p c k m", c=2, k=31)
        r_diag = [cx.res("diag%d" % c) for c in range(4)]

        def dg(c, k):
            return (diagA if c < 2 else diagB)[:, c % 2, k, :]

        def diag_ops():
            for c in range(4):
                region = r_KTh[0] if c < 2 else r_VTh[0]
                dfull = diagA if c < 2 else diagB
                for k0 in range(0, 31, 8):
                    nk = min(8, 31 - k0)
                    dst = dfull[:, c % 2, k0:k0 + nk, :]
                    i0 = bass.AP(ident.tensor, ident.offset, [list(ident.ap[0]), [0, nk], [1, 128]])
                    cwv = cw[:, c, k0:k0 + nk]
                    i1 = bass.AP(cwv.tensor, cwv.offset, [list(cwv.ap[0]), [1, nk], [0, 128]])
                    cx.add("pool", (lambda e: e.tensor_tensor(dst, i0, i1, ALU.mult)), reads=[r_const, r_cprm], writes=[r_diag[c], region])
                    yield None

        diag_gen = diag_ops()

        def sap(base3, row0, nrow, c, start, step, n):
            v = base3[row0:row0 + nrow, c, start:start + (n - 1) * step + 1]
            return bass.AP(v.tensor, v.offset, [list(v.ap[0]), [step, n]])

        kt_col = {}
        col = 0
        for d in PATTERNS:
            for r in range(d):
                for m in range(NKT[d]):
                    kt_col[(d, r, m)] = col
                    col += 1
        assert col == 69

        pe_ctr = [0]
        pm_ctr = [0]
        qb_ctr = [0]
        vtb = [(psb, psb_res), (psf[6].bitcast(BF16), psum_res[6])]
        vtb_ctr = [0]
        VtB = A(117 * KB, BF16, 20 * 256).rearrange("p (k f) -> p k f", k=20)
        r_VtB = [cx.res("VtB%d" % i) for i in range(5)]
        vt_bufs = {"A": (Vt, r_Vt), "B": (VtB, r_VtB)}
        plan = {0: [(16, "A"), (1, "B"), (4, "A")], 1: [(1, "B"), (16, "A"), (4, "B")]}

        def build_vtiles(hg, d, vname):
            Vb, r_Vb = vt_bufs[vname]
            nkt = NKT[d]
            tiles = [(r, m) for r in range(d) for m in range(nkt)]
            for g0 in range(0, len(tiles), 4):
                grp = tiles[g0:g0 + 4]
                vb, vres = vtb[vtb_ctr[0] % 2]
                vtb_ctr[0] += 1
                for gi, (r, m) in enumerate(grp):
                    start = HALO + r + d * (128 * m - 64)
                    for pl in range(2):
                        in_ = sap(VT, 0, 128, 2 * hg + pl, start, d, 128)
                        dstp = vb[:, gi * 256 + pl * 128: gi * 256 + (pl + 1) * 128]
                        cx.add("pe", (lambda e: e.transpose(dstp, in_, ident)), reads=[r_VTh[hg], r_const], writes=[vres])
                ng = len(grp)
                dst = Vb[:, g0:g0 + ng, :]
                src = vb[:, 0:ng * 256].rearrange("p (k f) -> p k f", k=ng)
                cx.add("act", (lambda e: e.copy(dst, src)), reads=[vres], writes=[r_Vb[g0 // 4]])

        def stage_a(hg, d, r, n):
            qstart = d * 128 * n + r
            qsel = qb_ctr[0] % 3
            osel = qb_ctr[0] % 2
            qb_ctr[0] += 1
            banks = [(psf[2 * qsel], psum_res[2 * qsel]), (psf[2 * qsel + 1], psum_res[2 * qsel + 1])]
            for e_ in range(2):
                sb, sr = banks[e_]
                for kk in range(2):
                    kstart = HALO + r + d * (128 * (n + kk) - 64)
                    for hl in range(2):
                        lhsT = sap(KT, 64 * e_, 64, 2 * hg + hl, kstart, d, 128)
                        rhs = sap(QT, 64 * e_, 64, 2 * hg + hl, qstart, d, 128)
                        c0 = (kk * 2 + hl) * 128
                        cx.add("pe", (lambda e: e.matmul(sb[:, c0:c0 + 128], lhsT, rhs, start=True, stop=True)),
                               reads=[r_KTh[hg]] + r_QTall, writes=[sr])
            pmi = []
            for e_ in range(2):
                sb, sr = banks[e_]
                s_ = pe_ctr[0] % 6
                pe_ctr[0] += 1
                cx.add("act", (lambda e: e.activation(pexp[s_], sb, AF.Exp, scale=0.125)), reads=[sr], writes=[r_pexp[s_]])
                ps_ = pm_ctr[0] % 6
                pm_ctr[0] += 1
                cx.add("dve" if e_ == 0 else "pool", (lambda e: e.tensor_tensor(pm[ps_], pexp[s_], mask2[:, 256:768], ALU.mult)),
                       reads=[r_pexp[s_], r_const], writes=[r_pm[ps_]])
                pmi.append(ps_)
            return (osel, pmi)

        def stage_b(hg, d, vname, first, r, n, st):
            osel, pmi = st
            Vb, r_Vb = vt_bufs[vname]
            nkt = NKT[d]
            qstart = d * 128 * n + r
            ob, orr = obanks[osel]
            ov = ob.rearrange("p (a c q) -> p a c q", a=2, c=2)
            for hh in range(4):
                pl, e_ = hh // 2, hh % 2
                ps_ = pmi[e_]
                for a_ in range(2):
                    for kk in range(2):
                        vti = r * nkt + n + kk
                        kc = kt_col[(d, r, n + kk)]
                        lhsT = Vb[:, vti, hh * 64:(hh + 1) * 64] if a_ == 0 else vrep[:, kc, :]
                        rhs = pm[ps_][:, (kk * 2 + pl) * 128:(kk * 2 + pl + 1) * 128]
                        dsto = ov[64 * e_:64 * e_ + 64, a_, pl, :]
                        st_, sp_ = (kk == 0), (kk == 1)
                        cx.add("pe", (lambda e: e.matmul(dsto, lhsT, rhs, start=st_, stop=sp_)),
                               reads=[r_pm[ps_], r_Vb[vti // 4], r_vrep], writes=[orr])
            av = acc[:, :, :, qstart:qstart + (127 * d) + 1]
            av = bass.AP(av.tensor, av.offset, [list(av.ap[0]), list(av.ap[1]), list(av.ap[2]), [d, 128]])
            if first:
                cx.add("dve", (lambda e: e.tensor_copy(av, ov)), reads=[orr], writes=[r_acc])
            else:
                cx.add("dve", (lambda e: e.tensor_tensor(av, av, ov, ALU.add)), reads=[orr, r_acc], writes=[r_acc])

        build_vtiles(0, *plan[0][0])
        for hg in range(2):
            inflight = []
            for pi, (d, vname) in enumerate(plan[hg]):
                nqb = T // d // 128
                cnt = 0
                for r in range(d):
                    for n in range(nqb):
                        if hg == 1:
                            next(diag_gen, None)
                        inflight.append((d, vname, pi == 0, r, n, stage_a(hg, d, r, n)))
                        if len(inflight) > 2:
                            stage_b(hg, *inflight.pop(0))
                        cnt += 1
                        if cnt == 3:
                            if pi + 1 < 3:
                                build_vtiles(hg, *plan[hg][pi + 1])
                            elif hg == 0:
                                build_vtiles(1, *plan[1][0])
            while inflight:
                stage_b(hg, *inflight.pop(0))
            def normalise_hg(hg_):
                for pl in range(2):
                    cx.add("act", (lambda e: e.activation(acc[:, 1, pl, :], acc[:, 1, pl, :], AF.Ln)), reads=[r_acc], writes=[r_acc])
                    cx.add("act", (lambda e: e.activation(acc[:, 1, pl, :], acc[:, 1, pl, :], AF.Exp, scale=-1.0)), reads=[r_acc], writes=[r_acc])
                    cx.add("dve" if pl == 0 else "pool", (lambda e: e.tensor_tensor(attT[:, 2 * hg_ + pl, :], acc[:, 0, pl, :], acc[:, 1, pl, :], ALU.mult)),
                           reads=[r_acc], writes=[r_attT])
            if hg == 0:
                normalise_hg(0)
        for _ in diag_gen:
            pass
        cx.emit_phase()
        if stop_after == 2:
            if dbg:
                dump([attT[:, 0, :], attT[:, 1, :], attT[:, 2, :], attT[:, 3, :]])
                cx.final_wait_all_dma(["dbg"])
            return nc

        woutb = A(4 * KB, BF16, 8 * 1024).rearrange("p (k n) -> p k n", k=8)
        r_wout = [cx.res("wout")] * 4
        wxkb = A(36 * KB, BF16, 8 * 1024).rearrange("p (k n) -> p k n", k=8)
        r_wxk = [cx.res("wxk")] * 4
        wxvb = A(68 * KB, BF16, 8 * 1024).rearrange("p (k n) -> p k n", k=8)
        r_wxv = [cx.res("wxv")] * 4
        wload(woutb, r_wout, w_out.rearrange("(k p) n -> p k n", p=128), "wout")
        wload(wxkb, r_wxk, w_xk.rearrange("(k p) n -> p k n", p=128), "wxk")
        wload(wxvb, r_wxv, w_xv.rearrange("(k p) n -> p k n", p=128), "wxv")
        normalise_hg(1)
        o = P2_BASE + 32 * KB
        yb = [A(o + i * 8 * KB, F32, 2048).rearrange("p (c t) -> p c t", c=4) for i in range(2)]; o += 16 * KB
        r_yb = [cx.res("yb%d" % i) for i in range(2)]
        ybf = [A(o + i * 4 * KB, BF16, 2048).rearrange("p (c t) -> p c t", c=4) for i in range(2)]; o += 8 * KB
        r_ybf = [cx.res("ybf%d" % i) for i in range(2)]
        ysq = [A(o + i * 4 * KB, BF16, 2048).rearrange("p (c t) -> p c t", c=4) for i in range(2)]; o += 8 * KB
        r_ysq = [cx.res("ysq%d" % i) for i in range(2)]
        mean_sb = A(o, F32, 512); o += 2 * KB
        var_sb = A(o, F32, 512); o += 2 * KB
        rs_sb = A(o, F32, 512); o += 2 * KB
        r_st = cx.res("lnstat")
        tt = [A(o + i * 2 * KB, F32, 512) for i in range(2)]; o += 4 * KB
        r_tt = [cx.res("tt%d" % i) for i in range(2)]
        r_convT = cx.res("convT")
        r_uTall = r_uT
        assert o <= ARENA_BYTES, o

        tctr = [0]

        conv_banks = {}
        cblocks = [(0, 512), (512, 512), (1024, 512), (1536, 256), (1792, 256)]

        def conv_mm(bi):
            t0_, n_ = cblocks[bi]
            conv_banks[bi] = []
            for c in range(4):
                pb, pr = bank()
                conv_banks[bi].append((pb, pr))
                for k in range(31):
                    st_, sp_ = (k == 0), (k == 30)
                    cx.add("pe", (lambda e: e.matmul(pb[:, 0:n_], dg(c, k), uT[:, c, t0_ + k + 1: t0_ + k + 1 + n_], start=st_, stop=sp_)),
                           reads=r_uTall + [r_diag[c]], writes=[pr])

        def conv_evac(bi):
            t0_, n_ = cblocks[bi]
            s_ = bi % 2
            for c in range(4):
                pb, pr = conv_banks[bi][c]
                cx.add("act", (lambda e: e.activation(yb[s_][:, c, 0:n_], pb[:, 0:n_], AF.Identity, bias=cb[:, c:c + 1], scale=1.0)),
                       reads=[pr, r_cprm], writes=[r_yb[s_]])
                cx.add("act", (lambda e: e.activation(ysq[s_][:, c, 0:n_], pb[:, 0:n_], AF.Square, bias=cb[:, c:c + 1], scale=1.0)),
                       reads=[pr, r_cprm], writes=[r_ysq[s_]])
                cx.add("dve", (lambda e: e.tensor_copy(ybf[s_][:, c, 0:n_], yb[s_][:, c, 0:n_])), reads=[r_yb[s_]], writes=[r_ybf[s_]])

        def ln_stage(bi):
            t0_, n_ = cblocks[bi]
            s_ = bi % 2
            mb, mr = bank()
            for c in range(4):
                st_, sp_ = (c == 0), (c == 3)
                cx.add("pe", (lambda e: e.matmul(mb[:, 0:n_], onesm, ybf[s_][:, c, 0:n_], start=st_, stop=sp_)), reads=[r_ybf[s_], r_const], writes=[mr])
            qb_, qr_ = bank()
            for c in range(4):
                st_, sp_ = (c == 0), (c == 3)
                cx.add("pe", (lambda e: e.matmul(qb_[:, 0:n_], onesm, ysq[s_][:, c, 0:n_], start=st_, stop=sp_)), reads=[r_ysq[s_], r_const], writes=[qr_])
            mean_ = mean_sb[:, 0:n_]
            var_ = var_sb[:, 0:n_]
            rs_ = rs_sb[:, 0:n_]
            cx.add("act", (lambda e: e.copy(mean_, mb[:, 0:n_])), reads=[mr], writes=[r_st])
            cx.add("dve", lambda e: e.tensor_tensor(var_, mean_, mean_, ALU.mult), reads=[r_st], writes=[r_st])
            cx.add("dve", (lambda e: e.tensor_tensor(var_, qb_[:, 0:n_], var_, ALU.subtract)), reads=[qr_, r_st], writes=[r_st])
            cx.add("act", lambda e: e.activation(var_, var_, AF.Sqrt, bias=epsc, scale=1.0), reads=[r_st, r_eps], writes=[r_st])
            cx.add("dve", lambda e: e.reciprocal(rs_, var_), reads=[r_st], writes=[r_st])
            for c in range(4):
                t_ = tctr[0] % 2
                tctr[0] += 1
                tt_ = tt[t_][:, 0:n_]
                cx.add("dve", (lambda e: e.tensor_tensor(tt_, yb[s_][:, c, 0:n_], mean_, ALU.subtract)), reads=[r_yb[s_], r_st], writes=[r_tt[t_]])
                cx.add("dve", (lambda e: e.tensor_tensor(tt_, tt_, rs_, ALU.mult)), reads=[r_tt[t_], r_st], writes=[r_tt[t_]])
                cx.add("act", (lambda e: e.activation(convT[:, c, t0_:t0_ + n_], tt_, AF.Silu, bias=lnb[:, c:c + 1], scale=lng[:, c:c + 1])),
                       reads=[r_tt[t_], r_cprm], writes=[r_convT])

        conv_mm(0)
        conv_evac(0)
        conv_mm(1)
        conv_evac(1)
        ln_stage(0)
        conv_mm(2)
        conv_evac(2)
        ln_stage(1)
        conv_mm(3)
        conv_evac(3)
        ln_stage(2)
        ln_stage(3)
        conv_mm(4)
        conv_evac(4)
        ln_stage(4)
        cx.emit_phase()
        if stop_after == 3:
            if dbg:
                dump([convT[:, 0, :], convT[:, 1, :], convT[:, 2, :], convT[:, 3, :]])
                cx.final_wait_all_dma(["dbg"])
            return nc

        o = P2_BASE
        hbuf = A(o, F32, 16 * 1024).rearrange("p (n d) -> p n d", n=16); o += 64 * KB
        r_h = [cx.res("h%d" % i) for i in range(16)]
        XKT = A(o, BF16, 8 * 256).rearrange("p (c m) -> p c m", c=8); o += 4 * KB
        XV = A(o, BF16, 2 * 1024).rearrange("p (t n) -> p t n", t=2); o += 4 * KB
        r_XKT = cx.res("XKT")
        r_XV = cx.res("XV")
        assert o <= ARENA_BYTES
        P6_BASE = 4 * KB
        o = 52 * KB
        xt4 = [A(o + i * 4 * KB, F32, 1024) for i in range(2)]; o += 8 * KB
        r_xt4 = [cx.res("xt4_%d" % i) for i in range(2)]
        mnb = A(o, BF16, 1024); o += 2 * KB
        r_mnb = cx.res("mnb")
        mnT = A(o, BF16, 8 * 256).rearrange("p (k m) -> p k m", k=8); o += 4 * KB
        r_mnT = cx.res("mnT")
        junk4 = A(o, BF16, 1024); o += 2 * KB
        r_junk4 = cx.res("junk4")
        assert o <= 68 * KB, o
        o = 20 * KB
        grep4 = A(o, F32, 1024); o += 4 * KB
        r_grep4 = cx.res("grep4")
        st4 = A(o, F32, 12).rearrange("p (a j) -> p a j", a=3); o += 64
        r_st4 = cx.res("st4")
        wxqb = A(84 * KB, BF16, 8 * 1024).rearrange("p (k n) -> p k n", k=8)
        r_wxq = [cx.res("wxq")] * 4
        wload(wxqb, r_wxq, w_xq.rearrange("(k p) n -> p k n", p=128), "wxq")

        cx.add("sp", lambda e: e.dma_start(out=grep4, in_=gvec[2:3, :].partition_broadcast(128)), writes=[r_grep4], dma="grep")
        for i in range(16):
            s_ = i % 2
            cx.add("sp", (lambda e, i=i, s_=s_: e.dma_start(out=xt4[s_], in_=xe_t[8 + i])), writes=[r_xt4[s_]], dma="xt4_%d" % s_)
            for half in range(2):
                pb, pr = bank()
                for k in range(8):
                    lhsT = attT[:, k, i * 128:(i + 1) * 128] if k < 4 else convT[:, k - 4, i * 128:(i + 1) * 128]
                    cx.add("pe", (lambda e, pb=pb, lhsT=lhsT, k=k, half=half: e.matmul(pb, lhsT, woutb[:, k, half * 512:(half + 1) * 512],
                                                                                         start=(k == 0), stop=(k == 7))),
                           reads=[r_attT, r_convT] + r_wout, writes=[pr])
                cx.add("dve", (lambda e, pb=pb, i=i, half=half, s_=s_: e.tensor_tensor(hbuf[:, i, half * 512:(half + 1) * 512], pb,
                                                                                         xt4[s_][:, half * 512:(half + 1) * 512], ALU.add)),
                       reads=[pr, r_xt4[s_]], writes=[r_h[i]])
        mem_t = mem.rearrange("(n p) d -> n p d", p=128)
        for i in range(2):
            cx.add("sp", (lambda e, i=i: e.dma_start(out=xt4[i], in_=mem_t[i])), writes=[r_xt4[i]], dma="xt4_%d" % i)
        rms_stats([xt4[0], xt4[1]], [r_xt4[0], r_xt4[1]], st4[:, 0, :], st4[:, 1, :], st4[:, 2, :], r_st4, junk4, r_junk4)
        for i in range(2):
            cx.add("dve", (lambda e, i=i: e.scalar_tensor_tensor(mnb, xt4[i], st4[:, 2, i:i + 1], grep4, ALU.mult, ALU.mult)),
                   reads=[r_xt4[i], r_st4, r_grep4], writes=[r_mnb])
            transpose_tile(mnb, r_mnb, mnT, r_mnT, i * 128, "act")
        ev = [0]

        def evac_copy(dst, src, sres, dres):
            ev[0] += 1
            if ev[0] % 2:
                cx.add("act", (lambda e: e.copy(dst, src)), reads=[sres], writes=[dres])
            else:
                cx.add("dve", (lambda e: e.tensor_copy(dst, src)), reads=[sres], writes=[dres])

        for c in range(8):
            pb, pr = bank()
            for k in range(8):
                cx.add("pe", (lambda e, pb=pb, c=c, k=k: e.matmul(pb[:, 0:256], wxkb[:, k, c * 128:(c + 1) * 128], mnT[:, k, :],
                                                                  start=(k == 0), stop=(k == 7))),
                       reads=r_wxk + [r_mnT], writes=[pr])
            evac_copy(XKT[:, c, :], pb[:, 0:256], pr, r_XKT)
        for t_ in range(2):
            for half in range(2):
                pb, pr = bank()
                for k in range(8):
                    cx.add("pe", (lambda e, pb=pb, t_=t_, half=half, k=k: e.matmul(pb, mnT[:, k, t_ * 128:(t_ + 1) * 128],
                                                                                   wxvb[:, k, half * 512:(half + 1) * 512],
                                                                                   start=(k == 0), stop=(k == 7))),
                           reads=r_wxv + [r_mnT], writes=[pr])
                evac_copy(XV[:, t_, half * 512:(half + 1) * 512], pb, pr, r_XV)
        cx.emit_phase()
        if stop_after == 4:
            if dbg:
                dump([hbuf[:, i, :] for i in range(16)])
                cx.final_wait_all_dma(["dbg"])
            return nc

        W0_BASE = 100 * KB
        W1_BASE = P6_BASE + 32 * KB
        wup = [A(W0_BASE, BF16, 8 * 1024).rearrange("p (k n) -> p k n", k=8), A(W1_BASE, BF16, 8 * 1024).rearrange("p (k n) -> p k n", k=8)]
        wdn = [A(W0_BASE + 16 * KB, BF16, 8 * 1024).rearrange("p (k n) -> p k n", k=8),
               A(W1_BASE + 16 * KB, BF16, 8 * 1024).rearrange("p (k n) -> p k n", k=8)]
        r_wup = [[cx.res("wup%d" % i)] for i in range(2)]
        r_wdn = [[cx.res("wdn%d" % i)] for i in range(2)]
        wup_src = w_up.rearrange("(k p) f -> p k f", p=128)
        wdn_src = w_down.rearrange("(f p) n -> p f n", p=128)

        def load_group(fg):
            s_ = fg % 2
            cx.add("pool", (lambda e: e.dma_start(out=wup[s_], in_=wup_src[:, :, fg * 1024:(fg + 1) * 1024])),
                   writes=[r_wup[s_][0]], dma="wup%d" % s_)
            cx.add("pool", (lambda e: e.dma_start(out=wdn[s_], in_=wdn_src[:, fg * 8:fg * 8 + 8, :])),
                   writes=[r_wdn[s_][0]], dma="wdn%d" % s_)

        o = 4 * KB
        wxob = A(o, BF16, 8 * 1024).rearrange("p (k n) -> p k n", k=8); o += 16 * KB
        r_wxo = [cx.res("wxo")] * 4
        hnT5 = [A(o + i * 8 * KB, BF16, 8 * 512).rearrange("p (k t) -> p k t", k=8) for i in range(2)]; o += 16 * KB
        r_hnT5 = [cx.res("hnT5_%d" % i) for i in range(2)]
        xqT = A(o, BF16, 8 * 512).rearrange("p (k t) -> p k t", k=8); o += 8 * KB
        r_xqT = [cx.res("xqT%d" % i) for i in range(4)]
        xoT2 = [A(o + i * 8 * KB, BF16, 8 * 512).rearrange("p (k t) -> p k t", k=8) for i in range(2)]; o += 16 * KB
        r_xoT2 = [[cx.res("xoT%d_%d" % (j, i)) for i in range(4)] for j in range(2)]
        px = [A(o + i * KB, BF16, 512) for i in range(4)]; o += 4 * KB
        r_px = [cx.res("px%d" % i) for i in range(4)]
        rd5 = [A(o + i * 2 * KB, F32, 512) for i in range(2)]; o += 4 * KB
        r_rd5 = [cx.res("rd5_%d" % i) for i in range(2)]
        hn5 = [A(o + i * 2 * KB, BF16, 1024) for i in range(2)]; o += 4 * KB
        r_hn5 = [cx.res("hn5_%d" % i) for i in range(2)]
        grep5 = A(o, F32, 1024); o += 4 * KB
        r_grep5 = cx.res("grep5")
        junk5 = A(o, BF16, 1024); o += 2 * KB
        r_junk5 = cx.res("junk5")
        st5 = A(o, F32, 48).rearrange("p (a b j) -> p a b j", a=3, b=4); o += 192
        r_st5 = [cx.res("st5_%d" % i) for i in range(4)]
        assert o <= 84 * KB, o

        wload(wxob, r_wxo, w_xo.rearrange("(k p) n -> p k n", p=128), "wxo")
        load_group(0)
        cx.add("sp", lambda e: e.dma_start(out=grep5, in_=gvec[1:2, :].partition_broadcast(128)), writes=[r_grep5], dma="grep")

        def norm_h_stats(nb, st, r_st, junk_, r_junk_):
            tiles = [nb * 4 + i for i in range(4)]
            rms_stats([hbuf[:, t_, :] for t_ in tiles], [r_h[t_] for t_ in tiles], st[:, 0, nb, :], st[:, 1, nb, :], st[:, 2, nb, :],
                      r_st[nb], junk_, r_junk_)

        def norm_h_apply(nb, g_ap, r_g, dstT, r_dstT, col_base, hn, r_hn, st, r_st):
            tiles = [nb * 4 + i for i in range(4)]

            def nrm(i):
                t_, s_ = tiles[i], i % 2
                cx.add("dve", (lambda e: e.scalar_tensor_tensor(hn[s_], hbuf[:, t_, :], st[:, 2, nb, i:i + 1], g_ap, ALU.mult, ALU.mult)),
                       reads=[r_h[t_], r_st[nb], r_g], writes=[r_hn[s_]])

            nrm(0)
            for i in range(4):
                if i + 1 < 4:
                    nrm(i + 1)
                transpose_tile(hn[i % 2], r_hn[i % 2], dstT, r_dstT, col_base + i * 128, "act")

        pxc = [0]
        rdc = [0]

        def xq_stage(nb):
            bs = nb % 2
            for c in range(8):
                pb, pr = bank()
                for k in range(8):
                    st_, sp_ = (k == 0), (k == 7)
                    cx.add("pe", (lambda e: e.matmul(pb, wxqb[:, k, c * 128:(c + 1) * 128], hnT5[bs][:, k, :], start=st_, stop=sp_)),
                           reads=r_wxq + [r_hnT5[bs]], writes=[pr])
                evac_copy(xqT[:, c, :], pb, pr, r_xqT[c // 2])

        def o_part(nb, i):
            xs = nb % 2
            t_ = nb * 4 + i
            for half in range(2):
                pb, pr = bank()
                for k in range(8):
                    st_, sp_ = (k == 0), (k == 7)
                    cx.add("pe", (lambda e: e.matmul(pb, xoT2[xs][:, k, i * 128:(i + 1) * 128], wxob[:, k, half * 512:(half + 1) * 512],
                                                     start=st_, stop=sp_)),
                           reads=r_xoT2[xs] + r_wxo, writes=[pr])
                hv = hbuf[:, t_, half * 512:(half + 1) * 512]
                cx.add("dve", (lambda e: e.tensor_tensor(hv, hv, pb, ALU.add)), reads=[pr, r_h[t_]], writes=[r_h[t_]])

        def head_scores(hx):
            pxs = []
            for kt in range(2):
                sb, sr = bank()
                for cc in range(2):
                    st_, sp_ = (cc == 0), (cc == 1)
                    cx.add("pe", (lambda e: e.matmul(sb, XKT[:, 2 * hx + cc, kt * 128:(kt + 1) * 128], xqT[:, 2 * hx + cc, :], start=st_, stop=sp_)),
                           reads=[r_XKT, r_xqT[hx]], writes=[sr])
                p_ = pxc[0] % 4
                pxc[0] += 1
                cx.add("act", (lambda e: e.activation(px[p_], sb, AF.Exp, scale=1.0 / 16.0)), reads=[sr], writes=[r_px[p_]])
                pxs.append(p_)
            return pxs

        def head_pv(nb, hx, pxs):
            xs = nb % 2
            db, dr = bank()
            for kt in range(2):
                st_, sp_ = (kt == 0), (kt == 1)
                p_ = pxs[kt]
                cx.add("pe", (lambda e: e.matmul(db, ones, px[p_], start=st_, stop=sp_)), reads=[r_px[p_], r_ones], writes=[dr])
            rd_ = rdc[0] % 2
            rdc[0] += 1
            cx.add("act", (lambda e: e.activation(rd5[rd_], db, AF.Ln)), reads=[dr], writes=[r_rd5[rd_]])
            cx.add("act", (lambda e: e.activation(rd5[rd_], rd5[rd_], AF.Exp, scale=-1.0)), reads=[r_rd5[rd_]], writes=[r_rd5[rd_]])
            for cc in range(2):
                ob, orr = bank()
                for kt in range(2):
                    st_, sp_ = (kt == 0), (kt == 1)
                    p_ = pxs[kt]
                    cx.add("pe", (lambda e: e.matmul(ob, XV[:, kt, (2 * hx + cc) * 128:(2 * hx + cc + 1) * 128], px[p_], start=st_, stop=sp_)),
                           reads=[r_px[p_], r_XV], writes=[orr])
                cx.add("dve", (lambda e: e.tensor_tensor(xoT2[xs][:, 2 * hx + cc, :], ob, rd5[rd_], ALU.mult)),
                       reads=[orr, r_rd5[rd_]], writes=[r_xoT2[xs][hx]])

        norm_h_stats(0, st5, r_st5, junk5, r_junk5)
        norm_h_apply(0, grep5, r_grep5, hnT5[0], r_hnT5[0], 0, hn5, r_hn5, st5, r_st5)
        norm_h_stats(1, st5, r_st5, junk5, r_junk5)
        for nb in range(4):
            xq_stage(nb)
            if nb + 1 < 4:
                norm_h_apply(nb + 1, grep5, r_grep5, hnT5[(nb + 1) % 2], r_hnT5[(nb + 1) % 2], 0, hn5, r_hn5, st5, r_st5)
                if nb + 2 < 4:
                    norm_h_stats(nb + 2, st5, r_st5, junk5, r_junk5)
            for hx in range(4):
                pxs = head_scores(hx)
                if nb > 0:
                    o_part(nb - 1, hx)
                head_pv(nb, hx, pxs)
        for i in range(4):
            o_part(3, i)
        cx.emit_phase()
        if stop_after == 5:
            if dbg:
                dump([hbuf[:, i, :] for i in range(16)])
                cx.final_wait_all_dma(["dbg"])
            return nc

        o = P6_BASE
        hnT6 = A(o, BF16, 8 * T).rearrange("p (k t) -> p k t", k=8); o += 32 * KB
        r_hnT6 = [cx.res("hnT6_%d" % i) for i in range(4)]
        o += 32 * KB
        HT = [A(o + i * 8 * KB, BF16, 8 * 512).rearrange("p (f t) -> p f t", f=8) for i in range(2)]; o += 16 * KB
        r_HT = [cx.res("HT%d" % i) for i in range(2)]
        rt = [A(o + i * 2 * KB, F32, 512) for i in range(2)]; o += 4 * KB
        r_rt = [cx.res("rt%d" % i) for i in range(2)]
        ostg = [A(o + i * 4 * KB, F32, 1024) for i in range(2)]; o += 8 * KB
        r_ostg = [cx.res("ostg%d" % i) for i in range(2)]
        hn6 = [A(o + i * 2 * KB, BF16, 1024) for i in range(2)]; o += 4 * KB
        r_hn6 = [cx.res("hn6_%d" % i) for i in range(2)]
        assert o <= W0_BASE
        o = W0_BASE + 32 * KB
        st6 = A(o, F32, 96).rearrange("p (z a b j) -> p z a b j", z=2, a=3, b=4); o += 384
        r_st6 = [[cx.res("st6_%d_%d" % (z, i)) for i in range(4)] for z in range(2)]
        assert o <= P2_BASE, o
        o = P2_BASE + 64 * KB
        grep6 = A(o, F32, 1024); o += 4 * KB
        grepf = A(o, F32, 1024); o += 4 * KB
        r_grep6 = cx.res("grep6")
        junk6 = A(o, BF16, 1024); o += 2 * KB
        r_junk6 = cx.res("junk6")
        assert o <= ARENA_BYTES, o

        load_group(1)
        cx.add("sp", lambda e: e.dma_start(out=grep6, in_=gvec[3:4, :].partition_broadcast(128)), writes=[r_grep6], dma="grep")
        r_grepf = cx.res("grepf")
        cx.add("sp", lambda e: e.dma_start(out=grepf, in_=gvec[4:5, :].partition_broadcast(128)), writes=[r_grepf], dma="grepf")
        htc = [0]
        rtc = [0]
        out_t = out.rearrange("(n p) d -> n p d", p=128)

        def up_stage(fg, nb, hs):
            s_ = fg % 2
            for f in range(8):
                pb, pr = bank()
                for k in range(8):
                    st_, sp_ = (k == 0), (k == 7)
                    cx.add("pe", (lambda e: e.matmul(pb, wup[s_][:, k, f * 128:(f + 1) * 128], hnT6[:, k, nb * 512:(nb + 1) * 512],
                                                     start=st_, stop=sp_)),
                           reads=r_wup[s_] + [r_hnT6[nb]], writes=[pr])
                r_ = rtc[0] % 2
                rtc[0] += 1
                cx.add("act", (lambda e: e.activation(rt[r_], pb, AF.Relu)), reads=[pr], writes=[r_rt[r_]])
                cx.add("pool", (lambda e: e.tensor_tensor(HT[hs][:, f, :], rt[r_], rt[r_], ALU.mult)), reads=[r_rt[r_]], writes=[r_HT[hs]])

        def down_stage(fg, nb, hs):
            s_ = fg % 2
            for i in range(4):
                t_ = nb * 4 + i
                for half in range(2):
                    pb, pr = bank()
                    for f in range(8):
                        st_, sp_ = (f == 0), (f == 7)
                        cx.add("pe", (lambda e: e.matmul(pb, HT[hs][:, f, i * 128:(i + 1) * 128], wdn[s_][:, f, half * 512:(half + 1) * 512],
                                                         start=st_, stop=sp_)),
                               reads=[r_HT[hs]] + r_wdn[s_], writes=[pr])
                    hv = hbuf[:, t_, half * 512:(half + 1) * 512]
                    cx.add("dve", (lambda e: e.tensor_tensor(hv, hv, pb, ALU.add)), reads=[pr, r_h[t_]], writes=[r_h[t_]])
            if fg == 3:
                tiles = [nb * 4 + i for i in range(4)]
                rms_stats([hbuf[:, t_, :] for t_ in tiles], [r_h[t_] for t_ in tiles], st6[:, 1, 0, nb, :], st6[:, 1, 1, nb, :],
                          st6[:, 1, 2, nb, :], r_st6[1][nb], junk6, r_junk6)
                for i, t_ in enumerate(tiles):
                    os_ = i % 2
                    cx.add("dve", (lambda e: e.scalar_tensor_tensor(ostg[os_], hbuf[:, t_, :], st6[:, 1, 2, nb, i:i + 1], grepf,
                                                                    ALU.mult, ALU.mult)),
                           reads=[r_h[t_], r_st6[1][nb], r_grepf], writes=[r_ostg[os_]])
                    cx.add("sp", (lambda e: e.dma_start(out=out_t[t_], in_=ostg[os_])), reads=[r_ostg[os_]], dma="out%d" % os_)
            if nb == 3 and fg + 2 < 4:
                load_group(fg + 2)

        items = [(fg, nb) for fg in range(4) for nb in range(4)]

        def norm6(nb):
            norm_h_apply(nb, grep6, r_grep6, hnT6, r_hnT6[nb], nb * 512, hn6, r_hn6, st6[:, 0], r_st6[0])
            if nb + 1 < 4:
                norm_h_stats(nb + 1, st6[:, 0], r_st6[0], junk6, r_junk6)

        norm_h_stats(0, st6[:, 0], r_st6[0], junk6, r_junk6)
        norm6(0)
        up_stage(0, 0, 0)
        for j in range(len(items)):
            if j + 1 < len(items):
                fg1, nb1 = items[j + 1]
                if fg1 == 0:
                    norm6(nb1)
                up_stage(fg1, nb1, (j + 1) % 2)
            down_stage(items[j][0], items[j][1], j % 2)
        cx.emit_phase()
        cx.final_wait_all_dma(["out0", "out1"])
    return nc


def _host_constants():
    ident = np.eye(128, dtype=np.float32)
    rm = np.zeros((128, 128), np.float32)
    for e in range(2):
        for i in range(8):
            rm[64 * e + 8 + i, 64 * e + i] = -1.0
            rm[64 * e + i, 64 * e + 8 + i] = 1.0
    onesm = np.full((128, 128), 1.0 / 512.0, np.float32)
    p = np.arange(128)[:, None]
    j = np.arange(128)[None, :]
    mask2 = np.concatenate([np.tile((p >= j), (1, 4)), np.tile((p <= j), (1, 4))], axis=1).astype(np.float32)
    return np.concatenate([ident, rm, onesm, mask2], axis=1)


def _tables(t0):
    pos = np.arange(t0 - HALO, t0 - HALO + E).astype(np.float32)
    pos = np.where((pos >= 0) & (pos < S), pos, np.float32(0.0)).astype(np.float32)
    rot = 16
    freqs = (np.float32(500000.0) ** (-np.arange(0, rot, 2, dtype=np.float32) / np.float32(rot))).astype(np.float32)
    ang = (pos[:, None] * freqs[None, :]).astype(np.float32)
    cos = np.cos(ang).astype(np.float32)
    sin = np.sin(ang).astype(np.float32)
    tc = np.ones((128, E), np.float32)
    ts = np.zeros((128, E), np.float32)
    for e in range(2):
        for i in range(16):
            tc[64 * e + i] = cos[:, i % 8]
            ts[64 * e + i] = sin[:, i % 8]
    return tc, ts


def _kbias(t0):
    kb = np.zeros((128, 72), np.float32)
    col = 0
    for d in PATTERNS:
        for r in range(d):
            for m in range(NKT[d]):
                start = t0 + r + d * (128 * m - 64)
                pos = start + d * np.arange(128)
                kb[:, col] = np.where((pos >= 0) & (pos < S), 0.0, -30000.0)
                col += 1
    return kb


_NC_CACHE = {}


def _get_nc(stop_after=99, dbg=False):
    key = (stop_after, dbg)
    if key not in _NC_CACHE:
        _NC_CACHE[key] = build_program(stop_after, dbg)
    return _NC_CACHE[key]


def make_in_maps(x, mem, norm_mix_g, w_in, conv_w, conv_b, conv_ln_g, conv_ln_b, w_out,
                 norm_x_g, norm_mem_g, w_xq, w_xk, w_xv, w_xo, norm_mlp_g, w_up, w_down, norm_final_g):
    f = lambda a: np.ascontiguousarray(np.asarray(a, dtype=np.float32))
    x = f(x)
    mem = f(mem)
    cst = _host_constants()
    gvec = np.stack([f(norm_mix_g)[0], f(norm_x_g)[0], f(norm_mem_g)[0], f(norm_mlp_g)[0], f(norm_final_g)], axis=0)
    cw = f(conv_w)[0]
    cwT = cw.T.reshape(4, 128, 31).transpose(1, 0, 2).reshape(128, 124)
    lay = lambda v: f(v)[0].reshape(4, 128).T
    convp = np.ascontiguousarray(np.concatenate([cwT, lay(conv_b), lay(conv_ln_g), lay(conv_ln_b)], axis=1))
    shared = {
        "cst": cst, "gvec": np.ascontiguousarray(gvec), "convp": convp,
        "w_in": f(w_in)[0], "w_out": f(w_out)[0], "w_xq": f(w_xq)[0], "w_xk": f(w_xk)[0], "w_xv": f(w_xv)[0],
        "w_xo": f(w_xo)[0], "w_up": f(w_up)[0], "w_down": f(w_down)[0],
    }
    in_maps = []
    for c in range(NCORES):
        b, t0 = c // 4, (c % 4) * T
        xe = np.zeros((E, D), np.float32)
        lo, hi = t0 - HALO, t0 + T + HALO
        slo, shi = max(lo, 0), min(hi, S)
        xe[slo - lo: shi - lo] = x[b, slo:shi]
        tc, ts = _tables(t0)
        m = dict(shared)
        kb = _kbias(t0)
        vr = np.ascontiguousarray(np.repeat((kb[:, :69] == 0).astype(np.float32), 64, axis=1))
        m.update({"x_ext": xe, "mem": mem[b], "tabC": tc, "tabS": ts, "kbias": kb, "vrep": vr})
        in_maps.append(m)
    return in_maps


def kernel(**inputs):
    in_maps = make_in_maps(**inputs)
    nc = _get_nc()
    res = run_bass_kernel_spmd(nc, in_maps, core_ids=list(range(NCORES)))
    outp = np.zeros((2, S, D), np.float32)
    for c in range(NCORES):
        b, t0 = c // 4, (c % 4) * T
        outp[b, t0:t0 + T] = res.results[c]["out"]
    return outp
```

```python
import numpy as np
import concourse.bass as bass
import concourse.mybir as mybir
from concourse.bass_utils import run_bass_kernel_spmd
from contextlib import ExitStack
import types

F32 = mybir.dt.float32
BF16 = mybir.dt.bfloat16
AF = mybir.ActivationFunctionType
ALU = mybir.AluOpType
AX = mybir.AxisListType

S = 8192
D = 1024
T = 2048
HALO = 1024
E = T + 2 * HALO
NCORES = 8
DIN = 2560
DFF = 4096
NMEM = 256
EPS = 1e-6
PATTERNS = (1, 4, 16)
NKT = {1: 17, 4: 5, 16: 2}
ARENA_BYTES = 207 * 1024

ENGS = ("sp", "act", "pool", "dve", "pe")
import os as _os
TRACE = open(_os.environ['K_TRACE'], 'w') if _os.environ.get('K_TRACE') else None


class Res:
    __slots__ = ("name", "w", "rs")

    def __init__(self, name=""):
        self.name = name
        self.w = None
        self.rs = []


class Op:
    __slots__ = ("eng", "fn", "reads", "writes", "dma", "dsem", "dcount", "deps", "signal", "sigval", "waits")

    def __init__(self, eng, fn, reads, writes, dma):
        self.eng = eng
        self.fn = fn
        self.reads = reads
        self.writes = writes
        self.dma = dma
        self.dsem = None
        self.dcount = 0
        self.deps = []
        self.signal = False
        self.sigval = None
        self.waits = []


class Ctx:
    def __init__(self, nc, stack):
        self.nc = nc
        self.stack = stack
        self.esem = {e: stack.enter_context(nc.semaphore("s_" + e)) for e in ENGS}
        self.ecount = {e: 0 for e in ENGS}
        self.dsems = {}
        self.dcounts = {}
        self.waited = {e: {} for e in ENGS}
        self.all_res = []
        self.ops = []

    def res(self, name=""):
        r = Res(name)
        self.all_res.append(r)
        return r

    def dsem(self, key):
        if key not in self.dsems:
            self.dsems[key] = self.stack.enter_context(self.nc.semaphore("d_" + key))
            self.dcounts[key] = 0
        return self.dsems[key]

    def add(self, eng, fn, reads=(), writes=(), dma=None):
        if fn.__closure__:
            cells = []
            for c in fn.__closure__:
                try:
                    cells.append(types.CellType(c.cell_contents))
                except ValueError:
                    cells.append(c)
            f2 = types.FunctionType(fn.__code__, fn.__globals__, fn.__name__, fn.__defaults__, tuple(cells))
            f2.__kwdefaults__ = fn.__kwdefaults__
            fn = f2
        op = Op(eng, fn, tuple(reads), tuple(writes), dma)
        deps = []
        for r in op.reads:
            if r.w is not None:
                deps.append((r.w, True))
        for r in op.writes:
            if r.w is not None:
                deps.append((r.w, False))
            for o in r.rs:
                deps.append((o, False))
        for r in op.reads:
            if r not in op.writes:
                r.rs.append(op)
        for r in op.writes:
            r.w = op
            r.rs = []
        if dma is not None:
            self.dsem(dma)
            self.dcounts[dma] += 16
            op.dsem = dma
            op.dcount = self.dcounts[dma]
        seen = {}
        for d, raw in deps:
            if d is op:
                continue
            k = id(d)
            if k in seen:
                seen[k] = (d, seen[k][1] or raw)
            else:
                seen[k] = (d, raw)
        for d, raw in seen.values():
            need = False
            if d.dma is not None:
                need = True
            elif d.eng != eng:
                need = True
            elif eng != "pe":
                need = True
            if need:
                op.waits.append(d)
                if d.dma is None:
                    d.signal = True
        self.ops.append(op)
        return op

    def emit_phase(self):
        nc = self.nc
        ops = self.ops
        self.ops = []
        per = {e: [o for o in ops if o.eng == e] for e in ENGS}
        last = {}
        for e in ENGS:
            for o in reversed(per[e]):
                if o.dma is None:
                    o.signal = True
                    last[e] = o
                    break
        for e in ENGS:
            for o in per[e]:
                if o.dma is None and o.signal:
                    self.ecount[e] += 1
                    o.sigval = self.ecount[e]
        final = {e: self.ecount[e] for e in ENGS if e in last}

        def run(e, engine):
            waited = self.waited[e]
            for o in per[e]:
                need = {}
                for d in o.waits:
                    if d.dma is not None:
                        key = ("d", d.dsem)
                        val = d.dcount
                    else:
                        key = ("e", d.eng)
                        val = d.sigval
                    if val > need.get(key, 0):
                        need[key] = val
                for key, val in need.items():
                    if waited.get(key, 0) >= val:
                        continue
                    sem = self.dsems[key[1]] if key[0] == "d" else self.esem[key[1]]
                    engine.wait_ge(sem, val)
                    waited[key] = val
                    if TRACE is not None:
                        TRACE.write("%s WAIT %s >= %d\n" % (e, key, val))
                ins = o.fn(engine)
                if TRACE is not None:
                    TRACE.write("%s OP %s sig=%s dma=%s\n" % (e, getattr(o.fn, '__qualname__', '?') + ':' + str(o.fn.__code__.co_firstlineno), o.sigval, o.dsem))
                if o.dma is not None:
                    ins.then_inc(self.dsems[o.dsem], 16)
                elif o.signal:
                    ins.then_inc(self.esem[e], 1)
            for f, val in final.items():
                if f == e:
                    continue
                key = ("e", f)
                if waited.get(key, 0) >= val:
                    continue
                engine.wait_ge(self.esem[f], val)
                waited[key] = val

        with nc.Block() as block:
            @block.sync
            def _(eng):
                run("sp", eng)

            @block.scalar
            def _(eng):
                run("act", eng)

            @block.gpsimd
            def _(eng):
                run("pool", eng)

            @block.vector
            def _(eng):
                run("dve", eng)

            @block.tensor
            def _(eng):
                run("pe", eng)
        for r in self.all_res:
            if r.w is not None and r.w.dma is None:
                r.w = None
            r.rs = [o for o in r.rs if o.dma is not None]

    def final_wait_all_dma(self, keys):
        nc = self.nc
        with nc.Block() as block:
            @block.sync
            def _(eng):
                for k in keys:
                    eng.wait_ge(self.dsems[k], self.dcounts[k])


def build_program(stop_after=99, dbg=False):
    nc = bass.Bass("TRN2", target_bir_lowering=False)

    def din(name, shape, dt=F32):
        return nc.dram_tensor(name, list(shape), dt, kind="ExternalInput").ap()

    x_ext = din("x_ext", [E, D])
    mem = din("mem", [NMEM, D])
    tabC = din("tabC", [128, E])
    tabS = din("tabS", [128, E])
    cst = din("cst", [128, 128 * 3 + 1024])
    kbias_d = din("kbias", [128, 72])
    vrep_d = din("vrep", [128, 69 * 64])
    gvec = din("gvec", [5, D])
    convp = din("convp", [128, 4 * 31 + 12])
    w_in = din("w_in", [D, DIN])
    w_out = din("w_out", [D, D])
    w_xq = din("w_xq", [D, D])
    w_xk = din("w_xk", [D, D])
    w_xv = din("w_xv", [D, D])
    w_xo = din("w_xo", [D, D])
    w_up = din("w_up", [D, DFF])
    w_down = din("w_down", [DFF, D])
    out = nc.dram_tensor("out", [T, D], F32, kind="ExternalOutput").ap()
    dbg_out = None
    if dbg:
        dbg_out = nc.dram_tensor("dbg", [128, 20 * 1024], F32, kind="ExternalOutput").ap()

    stack = ExitStack()
    with stack:
        arena = stack.enter_context(nc.sbuf_tensor("arena", [128, ARENA_BYTES // 2], BF16))
        psf = [stack.enter_context(nc.psum_tensor("psf%d" % i, [128, 512], F32)) for i in range(7)]
        psb_h = stack.enter_context(nc.psum_tensor("psb", [128, 1024], BF16))
        psb = psb_h[:, :]
        psf = [p[:, :] for p in psf]
        cx = Ctx(nc, stack)

        def A(off, dt, n):
            assert off % 4 == 0
            if dt == BF16:
                assert off + 2 * n <= ARENA_BYTES, (off, n)
                return arena[:, off // 2: off // 2 + n]
            assert off + 4 * n <= ARENA_BYTES, (off, n)
            return arena[:, off // 2: off // 2 + 2 * n].bitcast(F32)

        KB = 1024
        psum_res = [cx.res("psf%d" % i) for i in range(7)]
        psb_res = cx.res("psb")
        bank_rr = [0]

        def bank():
            i = bank_rr[0] % 6
            bank_rr[0] += 1
            return psf[i], psum_res[i]

        psb2 = psf[6].bitcast(BF16)
        trbufs = [(psb, psb_res), (psb2, psum_res[6])]
        tr_ctr = [0]

        o = 0
        ident = A(o, BF16, 128); o += 256
        Rm = A(o, BF16, 128); o += 256
        onesm = A(o, BF16, 128); o += 256
        mask2 = A(o, BF16, 1024); o += 2048
        ones = A(o, BF16, 128); o += 256
        kbias = A(o, F32, 72); o += 288
        epsc = A(o, F32, 1); o += 4
        cprm = A(o, F32, 4 * 31 + 12); o += 544
        cw = cprm[:, 0:124].rearrange("p (c k) -> p c k", c=4)
        cb = cprm[:, 124:128]
        lng = cprm[:, 128:132]
        lnb = cprm[:, 132:136]
        r_cprm = cx.res("cprm")
        o = 4 * KB
        SMALL_END = o
        r_const = cx.res("const")

        cx.add("pool", lambda e: e.dma_start(out=arena[:, 0:1408], in_=cst), writes=[r_const], dma="cst")
        cx.add("sp", lambda e: e.dma_start(out=cprm, in_=convp), writes=[r_cprm], dma="cprm")
        r_ones = cx.res("ones")
        r_eps = cx.res("eps")
        cx.add("dve", lambda e: e.memset(ones, 1.0), writes=[r_ones])
        cx.add("dve", lambda e: e.memset(epsc, EPS), writes=[r_eps])

        def wload(dst, dst_res, src, key, nsplit=1):
            cx.add("pool", (lambda e: e.dma_start(out=dst, in_=src)), writes=list(dict.fromkeys(dst_res)), dma=key)

        def rms_stats(srcs, src_res, ss, sq, rstd, r_ss, junk, r_junk):
            n = len(srcs)
            for j, (sap, sr) in enumerate(zip(srcs, src_res)):
                cx.add("act", (lambda e, sap=sap, j=j: e.activation(junk, sap, AF.Square, scale=1.0 / 32.0,
                                                                   accum_out=ss[:, j:j + 1])),
                       reads=[sr], writes=[r_junk, r_ss])
            cx.add("act", lambda e: e.activation(sq[:, 0:n], ss[:, 0:n], AF.Sqrt, bias=epsc, scale=1.0),
                   reads=[r_ss, r_eps], writes=[r_ss])
            cx.add("dve", lambda e: e.reciprocal(rstd[:, 0:n], sq[:, 0:n]), reads=[r_ss], writes=[r_ss])

        def transpose_tile(xn_ap, r_xn, dstT, r_dstT, col0, evac_eng):
            tb, tres = trbufs[tr_ctr[0] % 2]
            tr_ctr[0] += 1
            for k in range(8):
                cx.add("pe", (lambda e, k=k: e.transpose(tb[:, k * 128:(k + 1) * 128], xn_ap[:, k * 128:(k + 1) * 128], ident)),
                       reads=[r_xn, r_const], writes=[tres])
            src = tb.rearrange("p (k t) -> p k t", k=8)
            dst = dstT[:, :, col0:col0 + 128]
            if evac_eng == "act":
                cx.add("act", lambda e: e.copy(dst, src), reads=[tres], writes=[r_dstT])
            else:
                cx.add("dve", lambda e: e.tensor_copy(dst, src), reads=[tres], writes=[r_dstT])

        o = SMALL_END
        QT = A(o, BF16, 4 * T).rearrange("p (c t) -> p c t", c=4); o += 16 * KB
        KT = A(o, BF16, 4 * E).rearrange("p (c t) -> p c t", c=4); o += 32 * KB
        VT = A(o, BF16, 4 * E).rearrange("p (c t) -> p c t", c=4); o += 32 * KB
        UW = T + 32
        uT = A(o, BF16, 4 * UW).rearrange("p (c t) -> p c t", c=4); o += 4 * UW * 2
        o = (o + 1023) // 1024 * 1024
        P1_BASE = o
        r_QT = [cx.res("QT%d" % b) for b in range(4)]
        r_KT = [cx.res("KT%d" % b) for b in range(8)]
        r_VT = [cx.res("VT%d" % b) for b in range(8)]
        r_uT = [cx.res("uT%d" % b) for b in range(6)]

        o = P1_BASE
        winb = A(o, BF16, 8 * DIN).rearrange("p (k n) -> p k n", k=8); o += 40 * KB
        xnT = [A(o + i * 8 * KB, BF16, 8 * 512).rearrange("p (k t) -> p k t", k=8) for i in range(2)]; o += 16 * KB
        r_xnT = [cx.res("xnT%d" % i) for i in range(2)]
        xt = [A(o + i * 4 * KB, F32, 1024) for i in range(4)]; o += 16 * KB
        r_xt = [cx.res("xt%d" % i) for i in range(4)]
        xn = [A(o + i * 2 * KB, BF16, 1024) for i in range(2)]; o += 4 * KB
        r_xn = [cx.res("xn%d" % i) for i in range(2)]
        tC = [A(o + i * 2 * KB, F32, 512) for i in range(2)]; o += 4 * KB
        tS = [A(o + i * 2 * KB, F32, 512) for i in range(2)]; o += 4 * KB
        r_tabC = [cx.res("tabC%d" % i) for i in range(2)]
        r_tabS = [cx.res("tabS%d" % i) for i in range(2)]
        qraw = [A(o + i * KB, BF16, 512) for i in range(2)]; o += 2 * KB
        r_qraw = [cx.res("qraw%d" % i) for i in range(2)]
        tmpA = [A(o + i * 2 * KB, F32, 512) for i in range(2)]; o += 4 * KB
        r_tmpA = [cx.res("tmpA%d" % i) for i in range(2)]
        tmpB = [A(o + i * 2 * KB, F32, 512) for i in range(2)]; o += 4 * KB
        r_tmpB = [cx.res("tmpB%d" % i) for i in range(2)]
        sg = [A(o + i * 2 * KB, F32, 512) for i in range(2)]; o += 4 * KB
        r_sg = [cx.res("sg%d" % i) for i in range(2)]
        grep = A(o, F32, 1024); o += 4 * KB
        r_grep = cx.res("grep")
        junk = A(o, BF16, 1024); o += 2 * KB
        r_junk = cx.res("junk")
        ss = A(o, F32, 32).rearrange("p (b j) -> p b j", b=8); o += 128
        sq = A(o, F32, 32).rearrange("p (b j) -> p b j", b=8); o += 128
        rstd = A(o, F32, 32).rearrange("p (b j) -> p b j", b=8); o += 128
        r_ss = [cx.res("ss%d" % b) for b in range(8)]
        assert o <= ARENA_BYTES, o

        w_in_v = w_in.rearrange("(k p) n -> p k n", p=128)
        r_win = [cx.res("win%d" % i) for i in range(5)]
        def load_win(g_, extra_reads=()):
            cx.add("pool", (lambda e: e.dma_start(out=winb[:, :, g_ * 512:(g_ + 1) * 512], in_=w_in_v[:, :, g_ * 512:(g_ + 1) * 512])),
                   reads=list(extra_reads), writes=[r_win[g_]], dma="win%d" % g_)

        load_win(1)
        load_win(2)
        cx.add("sp", lambda e: e.dma_start(out=grep, in_=gvec[0:1, :].partition_broadcast(128)),
               writes=[r_grep], dma="grep")

        xe_t = x_ext.rearrange("(n p) d -> n p d", p=128)
        tile_ctr = [0]
        ev_ctr = [0]

        def load_x_tile(n, dst, dres, key):
            cx.add("sp", (lambda e: e.dma_start(out=dst, in_=xe_t[n])), writes=[dres], dma=key)

        def norm_pre(b, half):
            idx = [2 * half, 2 * half + 1]
            for i in idx:
                load_x_tile(b * 4 + i, xt[i], r_xt[i], "xt%d" % i)
            rms_stats([xt[i] for i in idx], [r_xt[i] for i in idx],
                      ss[:, b, 2 * half:2 * half + 2], sq[:, b, 2 * half:2 * half + 2],
                      rstd[:, b, 2 * half:2 * half + 2], r_ss[b], junk, r_junk)
            for i in idx:
                cx.add("dve", (lambda e: e.scalar_tensor_tensor(xn[i % 2], xt[i], rstd[:, b, i:i + 1], grep, ALU.mult, ALU.mult)),
                       reads=[r_xt[i], r_ss[b], r_grep], writes=[r_xn[i % 2]])

        def norm_tr(b, half):
            bs = b % 2
            for i in (2 * half, 2 * half + 1):
                ev_ctr[0] += 1
                transpose_tile(xn[i % 2], r_xn[i % 2], xnT[bs], r_xnT[bs], i * 128, "act" if ev_ctr[0] % 2 else "dve")

        def proj_chunk(bs, c0, n0=0, n=512):
            pb, pr = bank()
            for k in range(8):
                cx.add("pe", (lambda e, k=k: e.matmul(pb[:, 0:n], winb[:, k, c0:c0 + 128], xnT[bs][:, k, n0:n0 + n],
                                                      start=(k == 0), stop=(k == 7))),
                       reads=[r_xnT[bs], r_win[c0 // 512]], writes=[pr])
            flush_rot()
            return pb, pr

        rot_ctr = [0]

        pending_rot = []

        def flush_rot():
            while pending_rot:
                pending_rot.pop(0)()

        def rotary_evac(pb, pr, ts_, dst, dres):
            s_ = rot_ctr[0] % 2
            rot_ctr[0] += 1
            cx.add("act", lambda e: e.copy(qraw[s_], pb), reads=[pr], writes=[r_qraw[s_]])

            def rest():
                qb, qr = bank()
                cx.add("pe", lambda e: e.matmul(qb, Rm, qraw[s_], start=True, stop=True), reads=[r_qraw[s_], r_const], writes=[qr])
                cx.add("dve", lambda e: e.tensor_tensor(tmpA[s_], qb, tS[ts_], ALU.mult), reads=[qr, r_tabS[ts_]], writes=[r_tmpA[s_]])
                cx.add("pool", lambda e: e.tensor_tensor(tmpB[s_], qraw[s_], tC[ts_], ALU.mult), reads=[r_qraw[s_], r_tabC[ts_]], writes=[r_tmpB[s_]])
                cx.add("dve", lambda e: e.tensor_tensor(dst, tmpA[s_], tmpB[s_], ALU.add), reads=[r_tmpA[s_], r_tmpB[s_]], writes=[dres])
            pending_rot.append(rest)

        glu_ctr = [0]
        vev = [0]

        def k_chunk(b, pr_):
            pb, pres = proj_chunk(b % 2, 512 + pr_ * 128)
            rotary_evac(pb, pres, b % 2, KT[:, pr_, b * 512:(b + 1) * 512], r_KT[b])

        def v_chunk(b, pr_):
            pb, pres = proj_chunk(b % 2, 1024 + pr_ * 128)
            dst = VT[:, pr_, b * 512:(b + 1) * 512]
            vev[0] += 1
            if vev[0] % 2:
                cx.add("act", (lambda e: e.copy(dst, pb)), reads=[pres], writes=[r_VT[b]])
            else:
                cx.add("dve", (lambda e: e.tensor_copy(dst, pb)), reads=[pres], writes=[r_VT[b]])

        def q_chunk(b, pr_):
            ob = b - 2
            pb, pres = proj_chunk(b % 2, pr_ * 128)
            rotary_evac(pb, pres, b % 2, QT[:, pr_, ob * 512:(ob + 1) * 512], r_QT[ob])

        def glu_chunk(b, c):
            if 2 <= b <= 5:
                n0, n, ucol, ures = 0, 512, 16 + (b - 2) * 512, r_uT[b - 2]
            elif b == 1:
                n0, n, ucol, ures = 496, 16, 0, r_uT[4]
            else:
                n0, n, ucol, ures = 0, 16, 16 + T, r_uT[5]
            pa, pares = proj_chunk(b % 2, 1536 + c * 128, n0, n)
            pg, pgres = proj_chunk(b % 2, 2048 + c * 128, n0, n)
            s_ = glu_ctr[0] % 2
            glu_ctr[0] += 1
            cx.add("act", (lambda e: e.activation(sg[s_][:, 0:n], pg[:, 0:n], AF.Sigmoid)), reads=[pgres], writes=[r_sg[s_]])
            dst = uT[:, c, ucol:ucol + n]
            cx.add("dve", (lambda e: e.tensor_tensor(dst, pa[:, 0:n], sg[s_][:, 0:n], ALU.mult)),
                   reads=[pares, r_sg[s_]], writes=[ures])

        def load_tables(b):
            ts_ = b % 2
            cx.add("sp", (lambda e: e.dma_start(out=tC[ts_], in_=tabC[:, b * 512:(b + 1) * 512])), writes=[r_tabC[ts_]], dma="tabC%d" % ts_)
            cx.add("sp", (lambda e: e.dma_start(out=tS[ts_], in_=tabS[:, b * 512:(b + 1) * 512])), writes=[r_tabS[ts_]], dma="tabS%d" % ts_)

        load_tables(0)
        norm_pre(0, 0)
        norm_tr(0, 0)
        norm_pre(0, 1)
        norm_tr(0, 1)
        norm_pre(1, 0)
        for g_ in (0, 3, 4):
            load_win(g_, extra_reads=[r_xnT[0]])
        for b in range(8):
            own = 2 <= b <= 5
            work = [(k_chunk, pr_) for pr_ in range(4)] + [(v_chunk, pr_) for pr_ in range(4)]
            if own:
                work += [(q_chunk, pr_) for pr_ in range(4)]
            if own or b == 1 or b == 6:
                work += [(glu_chunk, c) for c in range(4)]
            n_w = len(work)
            marks = {n_w // 3: 0, (2 * n_w) // 3: 1}
            for wi, (fn_, arg_) in enumerate(work):
                if b + 1 < 8 and wi in marks:
                    if marks[wi] == 0:
                        load_tables(b + 1)
                        norm_tr(b + 1, 0)
                        norm_pre(b + 1, 1)
                    else:
                        norm_tr(b + 1, 1)
                        if b + 2 < 8:
                            norm_pre(b + 2, 0)
                fn_(b, arg_)
        flush_rot()

        def dump(ap_list):
            col = 0
            rr = cx.res("dbgstg")
            for ap in ap_list:
                n = ap.shape[1]
                stg = A(ARENA_BYTES - 8 * KB, F32, 2048)
                for c0 in range(0, n, 2048):
                    m = min(2048, n - c0)
                    cx.add("dve", (lambda e, ap=ap, c0=c0, m=m: e.tensor_copy(stg[:, 0:m], ap[:, c0:c0 + m])), writes=[rr])
                    cx.add("sp", (lambda e, col=col, c0=c0, m=m: e.dma_start(out=dbg_out[:, col + c0:col + c0 + m], in_=stg[:, 0:m])),
                           reads=[rr], dma="dbg")
                    cx.emit_phase()
                col += n

        cx.emit_phase()
        if stop_after == 1:
            if dbg:
                dump([QT[:, 0, :], KT[:, 0, :], VT[:, 0, :], uT[:, 0, :], QT[:, 3, :], KT[:, 3, :]])
                cx.final_wait_all_dma(["dbg"])
            return nc

        o = P1_BASE
        attT = A(o, BF16, 4 * T).rearrange("p (c t) -> p c t", c=4); o += 16 * KB
        convT = A(o, BF16, 4 * T).rearrange("p (c t) -> p c t", c=4); o += 16 * KB
        P2_BASE = o
        r_attT = cx.res("attT")
        acc = A(o, F32, 2 * 2 * T).rearrange("p (a c t) -> p a c t", a=2, c=2); o += 32 * KB
        r_acc = cx.res("acc")
        Vt = A(o, BF16, 32 * 256).rearrange("p (k f) -> p k f", k=32); o += 16 * KB
        r_Vt = [cx.res("Vt%d" % i) for i in range(8)]
        pexp = [A(o + i * KB, BF16, 512) for i in range(6)]; o += 6 * KB
        r_pexp = [cx.res("pexp%d" % i) for i in range(6)]
        pm = [A(o + i * KB, BF16, 512) for i in range(6)]; o += 6 * KB
        r_pm = [cx.res("pm%d" % i) for i in range(6)]
        obanks = [(psf[6], psum_res[6]), (psb.bitcast(F32), psb_res)]
        vrep = A(o, BF16, 69 * 64).rearrange("p (k f) -> p k f", k=69); o += 69 * 128
        r_vrep = cx.res("vrep")
        cx.add("pool", lambda e: e.dma_start(out=vrep, in_=vrep_d.rearrange("p (k f) -> p k f", k=69)), writes=[r_vrep], dma="vrep")
        assert o <= ARENA_BYTES
        r_KTh = [cx.res("KTlo"), cx.res("KThi")]
        r_VTh = [cx.res("VTlo"), cx.res("VThi")]
        r_QTall = r_QT
        diagA = A(20 * KB, BF16, 2 * 31 * 128).rearrange("p (c k m) -> p c k m", c=2, k=31)
        diagB = A(52 * KB, BF16, 2 * 31 * 128).rearrange("p (c k m) -> p c k m", c=2, k=31)
        r_diag = [cx.res("diag%d" % c) for c in range(4)]

        def dg(c, k):
            return (diagA if c < 2 else diagB)[:, c % 2, k, :]

        def diag_ops():
            for c in range(4):
                region = r_KTh[0] if c < 2 else r_VTh[0]
                dfull = diagA if c < 2 else diagB
                for k0 in range(0, 31, 8):
                    nk = min(8, 31 - k0)
                    dst = dfull[:, c % 2, k0:k0 + nk, :]
                    i0 = bass.AP(ident.tensor, ident.offset, [list(ident.ap[0]), [0, nk], [1, 128]])
                    cwv = cw[:, c, k0:k0 + nk]
                    i1 = bass.AP(cwv.tensor, cwv.offset, [list(cwv.ap[0]), [1, nk], [0, 128]])
                    cx.add("pool", (lambda e: e.tensor_tensor(dst, i0, i1, ALU.mult)), reads=[r_const, r_cprm], writes=[r_diag[c], region])
                    yield None

        diag_gen = diag_ops()

        def sap(base3, row0, nrow, c, start, step, n):
            v = base3[row0:row0 + nrow, c, start:start + (n - 1) * step + 1]
            return bass.AP(v.tensor, v.offset, [list(v.ap[0]), [step, n]])

        kt_col = {}
        col = 0
        for d in PATTERNS:
            for r in range(d):
                for m in range(NKT[d]):
                    kt_col[(d, r, m)] = col
                    col += 1
        assert col == 69

        pe_ctr = [0]
        pm_ctr = [0]
        qb_ctr = [0]
        vtb = [(psb, psb_res), (psf[6].bitcast(BF16), psum_res[6])]
        vtb_ctr = [0]
        VtB = A(117 * KB, BF16, 20 * 256).rearrange("p (k f) -> p k f", k=20)
        r_VtB = [cx.res("VtB%d" % i) for i in range(5)]
        vt_bufs = {"A": (Vt, r_Vt), "B": (VtB, r_VtB)}
        plan = {0: [(16, "A"), (1, "B"), (4, "A")], 1: [(1, "B"), (16, "A"), (4, "B")]}

        def build_vtiles(hg, d, vname):
            Vb, r_Vb = vt_bufs[vname]
            nkt = NKT[d]
            tiles = [(r, m) for r in range(d) for m in range(nkt)]
            for g0 in range(0, len(tiles), 4):
                grp = tiles[g0:g0 + 4]
                vb, vres = vtb[vtb_ctr[0] % 2]
                vtb_ctr[0] += 1
                for gi, (r, m) in enumerate(grp):
                    start = HALO + r + d * (128 * m - 64)
                    for pl in range(2):
                        in_ = sap(VT, 0, 128, 2 * hg + pl, start, d, 128)
                        dstp = vb[:, gi * 256 + pl * 128: gi * 256 + (pl + 1) * 128]
                        cx.add("pe", (lambda e: e.transpose(dstp, in_, ident)), reads=[r_VTh[hg], r_const], writes=[vres])
                ng = len(grp)
                dst = Vb[:, g0:g0 + ng, :]
                src = vb[:, 0:ng * 256].rearrange("p (k f) -> p k f", k=ng)
                cx.add("act", (lambda e: e.copy(dst, src)), reads=[vres], writes=[r_Vb[g0 // 4]])

        def stage_a(hg, d, r, n):
            qstart = d * 128 * n + r
            qsel = qb_ctr[0] % 3
            osel = qb_ctr[0] % 2
            qb_ctr[0] += 1
            banks = [(psf[2 * qsel], psum_res[2 * qsel]), (psf[2 * qsel + 1], psum_res[2 * qsel + 1])]
            for e_ in range(2):
                sb, sr = banks[e_]
                for kk in range(2):
                    kstart = HALO + r + d * (128 * (n + kk) - 64)
                    for hl in range(2):
                        lhsT = sap(KT, 64 * e_, 64, 2 * hg + hl, kstart, d, 128)
                        rhs = sap(QT, 64 * e_, 64, 2 * hg + hl, qstart, d, 128)
                        c0 = (kk * 2 + hl) * 128
                        cx.add("pe", (lambda e: e.matmul(sb[:, c0:c0 + 128], lhsT, rhs, start=True, stop=True)),
                               reads=[r_KTh[hg]] + r_QTall, writes=[sr])
            pmi = []
            for e_ in range(2):
                sb, sr = banks[e_]
                s_ = pe_ctr[0] % 6
                pe_ctr[0] += 1
                cx.add("act", (lambda e: e.activation(pexp[s_], sb, AF.Exp, scale=0.125)), reads=[sr], writes=[r_pexp[s_]])
                ps_ = pm_ctr[0] % 6
                pm_ctr[0] += 1
                cx.add("dve" if e_ == 0 else "pool", (lambda e: e.tensor_tensor(pm[ps_], pexp[s_], mask2[:, 256:768], ALU.mult)),
                       reads=[r_pexp[s_], r_const], writes=[r_pm[ps_]])
                pmi.append(ps_)
            return (osel, pmi)

        def stage_b(hg, d, vname, first, r, n, st):
            osel, pmi = st
            Vb, r_Vb = vt_bufs[vname]
            nkt = NKT[d]
            qstart = d * 128 * n + r
            ob, orr = obanks[osel]
            ov = ob.rearrange("p (a c q) -> p a c q", a=2, c=2)
            for hh in range(4):
                pl, e_ = hh // 2, hh % 2
                ps_ = pmi[e_]
                for a_ in range(2):
                    for kk in range(2):
                        vti = r * nkt + n + kk
                        kc = kt_col[(d, r, n + kk)]
                        lhsT = Vb[:, vti, hh * 64:(hh + 1) * 64] if a_ == 0 else vrep[:, kc, :]
                        rhs = pm[ps_][:, (kk * 2 + pl) * 128:(kk * 2 + pl + 1) * 128]
                        dsto = ov[64 * e_:64 * e_ + 64, a_, pl, :]
                        st_, sp_ = (kk == 0), (kk == 1)
                        cx.add("pe", (lambda e: e.matmul(dsto, lhsT, rhs, start=st_, stop=sp_)),
                               reads=[r_pm[ps_], r_Vb[vti // 4], r_vrep], writes=[orr])
            av = acc[:, :, :, qstart:qstart + (127 * d) + 1]
            av = bass.AP(av.tensor, av.offset, [list(av.ap[0]), list(av.ap[1]), list(av.ap[2]), [d, 128]])
            if first:
                cx.add("dve", (lambda e: e.tensor_copy(av, ov)), reads=[orr], writes=[r_acc])
            else:
                cx.add("dve", (lambda e: e.tensor_tensor(av, av, ov, ALU.add)), reads=[orr, r_acc], writes=[r_acc])

        build_vtiles(0, *plan[0][0])
        for hg in range(2):
            inflight = []
            for pi, (d, vname) in enumerate(plan[hg]):
                nqb = T // d // 128
                cnt = 0
                for r in range(d):
                    for n in range(nqb):
                        if hg == 1:
                            next(diag_gen, None)
                        inflight.append((d, vname, pi == 0, r, n, stage_a(hg, d, r, n)))
                        if len(inflight) > 2:
                            stage_b(hg, *inflight.pop(0))
                        cnt += 1
                        if cnt == 3:
                            if pi + 1 < 3:
                                build_vtiles(hg, *plan[hg][pi + 1])
                            elif hg == 0:
                                build_vtiles(1, *plan[1][0])
            while inflight:
                stage_b(hg, *inflight.pop(0))
            def normalise_hg(hg_):
                for pl in range(2):
                    cx.add("act", (lambda e: e.activation(acc[:, 1, pl, :], acc[:, 1, pl, :], AF.Ln)), reads=[r_acc], writes=[r_acc])
                    cx.add("act", (lambda e: e.activation(acc[:, 1, pl, :], acc[:, 1, pl, :], AF.Exp, scale=-1.0)), reads=[r_acc], writes=[r_acc])
                    cx.add("dve" if pl == 0 else "pool", (lambda e: e.tensor_tensor(attT[:, 2 * hg_ + pl, :], acc[:, 0, pl, :], acc[:, 1, pl, :], ALU.mult)),
                           reads=[r_acc], writes=[r_attT])
            if hg == 0:
                normalise_hg(0)
        for _ in diag_gen:
            pass
        cx.emit_phase()
        if stop_after == 2:
            if dbg:
                dump([attT[:, 0, :], attT[:, 1, :], attT[:, 2, :], attT[:, 3, :]])
                cx.final_wait_all_dma(["dbg"])
            return nc

        woutb = A(4 * KB, BF16, 8 * 1024).rearrange("p (k n) -> p k n", k=8)
        r_wout = [cx.res("wout")] * 4
        wxkb = A(36 * KB, BF16, 8 * 1024).rearrange("p (k n) -> p k n", k=8)
        r_wxk = [cx.res("wxk")] * 4
        wxvb = A(68 * KB, BF16, 8 * 1024).rearrange("p (k n) -> p k n", k=8)
        r_wxv = [cx.res("wxv")] * 4
        wload(woutb, r_wout, w_out.rearrange("(k p) n -> p k n", p=128), "wout")
        wload(wxkb, r_wxk, w_xk.rearrange("(k p) n -> p k n", p=128), "wxk")
        wload(wxvb, r_wxv, w_xv.rearrange("(k p) n -> p k n", p=128), "wxv")
        normalise_hg(1)
        o = P2_BASE + 32 * KB
        yb = [A(o + i * 8 * KB, F32, 2048).rearrange("p (c t) -> p c t", c=4) for i in range(2)]; o += 16 * KB
        r_yb = [cx.res("yb%d" % i) for i in range(2)]
        ybf = [A(o + i * 4 * KB, BF16, 2048).rearrange("p (c t) -> p c t", c=4) for i in range(2)]; o += 8 * KB
        r_ybf = [cx.res("ybf%d" % i) for i in range(2)]
        ysq = [A(o + i * 4 * KB, BF16, 2048).rearrange("p (c t) -> p c t", c=4) for i in range(2)]; o += 8 * KB
        r_ysq = [cx.res("ysq%d" % i) for i in range(2)]
        mean_sb = A(o, F32, 512); o += 2 * KB
        var_sb = A(o, F32, 512); o += 2 * KB
        rs_sb = A(o, F32, 512); o += 2 * KB
        r_st = cx.res("lnstat")
        tt = [A(o + i * 2 * KB, F32, 512) for i in range(2)]; o += 4 * KB
        r_tt = [cx.res("tt%d" % i) for i in range(2)]
        r_convT = cx.res("convT")
        r_uTall = r_uT
        assert o <= ARENA_BYTES, o

        tctr = [0]

        conv_banks = {}

        def conv_mm(nb):
            conv_banks[nb] = []
            for c in range(4):
                pb, pr = bank()
                conv_banks[nb].append((pb, pr))
                for k in range(31):
                    st_, sp_ = (k == 0), (k == 30)
                    cx.add("pe", (lambda e: e.matmul(pb, dg(c, k), uT[:, c, nb * 512 + k + 1: nb * 512 + k + 513], start=st_, stop=sp_)),
                           reads=r_uTall + [r_diag[c]], writes=[pr])

        def conv_evac(nb):
            s_ = nb % 2
            for c in range(4):
                pb, pr = conv_banks[nb][c]
                cx.add("act", (lambda e: e.activation(yb[s_][:, c, :], pb, AF.Identity, bias=cb[:, c:c + 1], scale=1.0)),
                       reads=[pr, r_cprm], writes=[r_yb[s_]])
                cx.add("act", (lambda e: e.activation(ysq[s_][:, c, :], pb, AF.Square, bias=cb[:, c:c + 1], scale=1.0)),
                       reads=[pr, r_cprm], writes=[r_ysq[s_]])
                cx.add("dve", (lambda e: e.tensor_copy(ybf[s_][:, c, :], yb[s_][:, c, :])), reads=[r_yb[s_]], writes=[r_ybf[s_]])

        def ln_stage(nb):
            s_ = nb % 2
            mb, mr = bank()
            for c in range(4):
                st_, sp_ = (c == 0), (c == 3)
                cx.add("pe", (lambda e: e.matmul(mb, onesm, ybf[s_][:, c, :], start=st_, stop=sp_)), reads=[r_ybf[s_], r_const], writes=[mr])
            qb_, qr_ = bank()
            for c in range(4):
                st_, sp_ = (c == 0), (c == 3)
                cx.add("pe", (lambda e: e.matmul(qb_, onesm, ysq[s_][:, c, :], start=st_, stop=sp_)), reads=[r_ysq[s_], r_const], writes=[qr_])
            cx.add("act", (lambda e: e.copy(mean_sb, mb)), reads=[mr], writes=[r_st])
            cx.add("dve", lambda e: e.tensor_tensor(var_sb, mean_sb, mean_sb, ALU.mult), reads=[r_st], writes=[r_st])
            cx.add("dve", (lambda e: e.tensor_tensor(var_sb, qb_, var_sb, ALU.subtract)), reads=[qr_, r_st], writes=[r_st])
            cx.add("act", lambda e: e.activation(var_sb, var_sb, AF.Sqrt, bias=epsc, scale=1.0), reads=[r_st, r_eps], writes=[r_st])
            cx.add("dve", lambda e: e.reciprocal(rs_sb, var_sb), reads=[r_st], writes=[r_st])
            for c in range(4):
                t_ = tctr[0] % 2
                tctr[0] += 1
                cx.add("dve", (lambda e: e.tensor_tensor(tt[t_], yb[s_][:, c, :], mean_sb, ALU.subtract)), reads=[r_yb[s_], r_st], writes=[r_tt[t_]])
                cx.add("dve", (lambda e: e.tensor_tensor(tt[t_], tt[t_], rs_sb, ALU.mult)), reads=[r_tt[t_], r_st], writes=[r_tt[t_]])
                cx.add("act", (lambda e: e.activation(convT[:, c, nb * 512:(nb + 1) * 512], tt[t_], AF.Silu, bias=lnb[:, c:c + 1], scale=lng[:, c:c + 1])),
                       reads=[r_tt[t_], r_cprm], writes=[r_convT])

        conv_mm(0)
        conv_evac(0)
        conv_mm(1)
        conv_evac(1)
        ln_stage(0)
        conv_mm(2)
        conv_evac(2)
        ln_stage(1)
        ln_stage(2)
        conv_mm(3)
        conv_evac(3)
        ln_stage(3)
        cx.emit_phase()
        if stop_after == 3:
            if dbg:
                dump([convT[:, 0, :], convT[:, 1, :], convT[:, 2, :], convT[:, 3, :]])
                cx.final_wait_all_dma(["dbg"])
            return nc

        o = P2_BASE
        hbuf = A(o, F32, 16 * 1024).rearrange("p (n d) -> p n d", n=16); o += 64 * KB
        r_h = [cx.res("h%d" % i) for i in range(16)]
        XKT = A(o, BF16, 8 * 256).rearrange("p (c m) -> p c m", c=8); o += 4 * KB
        XV = A(o, BF16, 2 * 1024).rearrange("p (t n) -> p t n", t=2); o += 4 * KB
        r_XKT = cx.res("XKT")
        r_XV = cx.res("XV")
        assert o <= ARENA_BYTES
        P6_BASE = 4 * KB
        o = 52 * KB
        xt4 = [A(o + i * 4 * KB, F32, 1024) for i in range(2)]; o += 8 * KB
        r_xt4 = [cx.res("xt4_%d" % i) for i in range(2)]
        mnb = A(o, BF16, 1024); o += 2 * KB
        r_mnb = cx.res("mnb")
        mnT = A(o, BF16, 8 * 256).rearrange("p (k m) -> p k m", k=8); o += 4 * KB
        r_mnT = cx.res("mnT")
        junk4 = A(o, BF16, 1024); o += 2 * KB
        r_junk4 = cx.res("junk4")
        assert o <= 68 * KB, o
        o = 20 * KB
        grep4 = A(o, F32, 1024); o += 4 * KB
        r_grep4 = cx.res("grep4")
        st4 = A(o, F32, 12).rearrange("p (a j) -> p a j", a=3); o += 64
        r_st4 = cx.res("st4")
        wxqb = A(84 * KB, BF16, 8 * 1024).rearrange("p (k n) -> p k n", k=8)
        r_wxq = [cx.res("wxq")] * 4
        wload(wxqb, r_wxq, w_xq.rearrange("(k p) n -> p k n", p=128), "wxq")

        cx.add("sp", lambda e: e.dma_start(out=grep4, in_=gvec[2:3, :].partition_broadcast(128)), writes=[r_grep4], dma="grep")
        for i in range(16):
            s_ = i % 2
            cx.add("sp", (lambda e, i=i, s_=s_: e.dma_start(out=xt4[s_], in_=xe_t[8 + i])), writes=[r_xt4[s_]], dma="xt4_%d" % s_)
            for half in range(2):
                pb, pr = bank()
                for k in range(8):
                    lhsT = attT[:, k, i * 128:(i + 1) * 128] if k < 4 else convT[:, k - 4, i * 128:(i + 1) * 128]
                    cx.add("pe", (lambda e, pb=pb, lhsT=lhsT, k=k, half=half: e.matmul(pb, lhsT, woutb[:, k, half * 512:(half + 1) * 512],
                                                                                         start=(k == 0), stop=(k == 7))),
                           reads=[r_attT, r_convT] + r_wout, writes=[pr])
                cx.add("dve", (lambda e, pb=pb, i=i, half=half, s_=s_: e.tensor_tensor(hbuf[:, i, half * 512:(half + 1) * 512], pb,
                                                                                         xt4[s_][:, half * 512:(half + 1) * 512], ALU.add)),
                       reads=[pr, r_xt4[s_]], writes=[r_h[i]])
        mem_t = mem.rearrange("(n p) d -> n p d", p=128)
        for i in range(2):
            cx.add("sp", (lambda e, i=i: e.dma_start(out=xt4[i], in_=mem_t[i])), writes=[r_xt4[i]], dma="xt4_%d" % i)
        rms_stats([xt4[0], xt4[1]], [r_xt4[0], r_xt4[1]], st4[:, 0, :], st4[:, 1, :], st4[:, 2, :], r_st4, junk4, r_junk4)
        for i in range(2):
            cx.add("dve", (lambda e, i=i: e.scalar_tensor_tensor(mnb, xt4[i], st4[:, 2, i:i + 1], grep4, ALU.mult, ALU.mult)),
                   reads=[r_xt4[i], r_st4, r_grep4], writes=[r_mnb])
            transpose_tile(mnb, r_mnb, mnT, r_mnT, i * 128, "act")
        ev = [0]

        def evac_copy(dst, src, sres, dres):
            ev[0] += 1
            if ev[0] % 2:
                cx.add("act", (lambda e: e.copy(dst, src)), reads=[sres], writes=[dres])
            else:
                cx.add("dve", (lambda e: e.tensor_copy(dst, src)), reads=[sres], writes=[dres])

        for c in range(8):
            pb, pr = bank()
            for k in range(8):
                cx.add("pe", (lambda e, pb=pb, c=c, k=k: e.matmul(pb[:, 0:256], wxkb[:, k, c * 128:(c + 1) * 128], mnT[:, k, :],
                                                                  start=(k == 0), stop=(k == 7))),
                       reads=r_wxk + [r_mnT], writes=[pr])
            evac_copy(XKT[:, c, :], pb[:, 0:256], pr, r_XKT)
        for t_ in range(2):
            for half in range(2):
                pb, pr = bank()
                for k in range(8):
                    cx.add("pe", (lambda e, pb=pb, t_=t_, half=half, k=k: e.matmul(pb, mnT[:, k, t_ * 128:(t_ + 1) * 128],
                                                                                   wxvb[:, k, half * 512:(half + 1) * 512],
                                                                                   start=(k == 0), stop=(k == 7))),
                           reads=r_wxv + [r_mnT], writes=[pr])
                evac_copy(XV[:, t_, half * 512:(half + 1) * 512], pb, pr, r_XV)
        cx.emit_phase()
        if stop_after == 4:
            if dbg:
                dump([hbuf[:, i, :] for i in range(16)])
                cx.final_wait_all_dma(["dbg"])
            return nc

        W0_BASE = 100 * KB
        W1_BASE = P6_BASE + 32 * KB
        wup = [A(W0_BASE, BF16, 8 * 1024).rearrange("p (k n) -> p k n", k=8), A(W1_BASE, BF16, 8 * 1024).rearrange("p (k n) -> p k n", k=8)]
        wdn = [A(W0_BASE + 16 * KB, BF16, 8 * 1024).rearrange("p (k n) -> p k n", k=8),
               A(W1_BASE + 16 * KB, BF16, 8 * 1024).rearrange("p (k n) -> p k n", k=8)]
        r_wup = [[cx.res("wup%d" % i)] for i in range(2)]
        r_wdn = [[cx.res("wdn%d" % i)] for i in range(2)]
        wup_src = w_up.rearrange("(k p) f -> p k f", p=128)
        wdn_src = w_down.rearrange("(f p) n -> p f n", p=128)

        def load_group(fg):
            s_ = fg % 2
            cx.add("pool", (lambda e: e.dma_start(out=wup[s_], in_=wup_src[:, :, fg * 1024:(fg + 1) * 1024])),
                   writes=[r_wup[s_][0]], dma="wup%d" % s_)
            cx.add("pool", (lambda e: e.dma_start(out=wdn[s_], in_=wdn_src[:, fg * 8:fg * 8 + 8, :])),
                   writes=[r_wdn[s_][0]], dma="wdn%d" % s_)

        o = 4 * KB
        wxob = A(o, BF16, 8 * 1024).rearrange("p (k n) -> p k n", k=8); o += 16 * KB
        r_wxo = [cx.res("wxo")] * 4
        hnT5 = [A(o + i * 8 * KB, BF16, 8 * 512).rearrange("p (k t) -> p k t", k=8) for i in range(2)]; o += 16 * KB
        r_hnT5 = [cx.res("hnT5_%d" % i) for i in range(2)]
        xqT = A(o, BF16, 8 * 512).rearrange("p (k t) -> p k t", k=8); o += 8 * KB
        r_xqT = [cx.res("xqT%d" % i) for i in range(4)]
        xoT2 = [A(o + i * 8 * KB, BF16, 8 * 512).rearrange("p (k t) -> p k t", k=8) for i in range(2)]; o += 16 * KB
        r_xoT2 = [[cx.res("xoT%d_%d" % (j, i)) for i in range(4)] for j in range(2)]
        px = [A(o + i * KB, BF16, 512) for i in range(4)]; o += 4 * KB
        r_px = [cx.res("px%d" % i) for i in range(4)]
        rd5 = [A(o + i * 2 * KB, F32, 512) for i in range(2)]; o += 4 * KB
        r_rd5 = [cx.res("rd5_%d" % i) for i in range(2)]
        hn5 = [A(o + i * 2 * KB, BF16, 1024) for i in range(2)]; o += 4 * KB
        r_hn5 = [cx.res("hn5_%d" % i) for i in range(2)]
        grep5 = A(o, F32, 1024); o += 4 * KB
        r_grep5 = cx.res("grep5")
        junk5 = A(o, BF16, 1024); o += 2 * KB
        r_junk5 = cx.res("junk5")
        st5 = A(o, F32, 48).rearrange("p (a b j) -> p a b j", a=3, b=4); o += 192
        r_st5 = [cx.res("st5_%d" % i) for i in range(4)]
        assert o <= 84 * KB, o

        wload(wxob, r_wxo, w_xo.rearrange("(k p) n -> p k n", p=128), "wxo")
        load_group(0)
        cx.add("sp", lambda e: e.dma_start(out=grep5, in_=gvec[1:2, :].partition_broadcast(128)), writes=[r_grep5], dma="grep")

        def norm_h_stats(nb, st, r_st, junk_, r_junk_):
            tiles = [nb * 4 + i for i in range(4)]
            rms_stats([hbuf[:, t_, :] for t_ in tiles], [r_h[t_] for t_ in tiles], st[:, 0, nb, :], st[:, 1, nb, :], st[:, 2, nb, :],
                      r_st[nb], junk_, r_junk_)

        def norm_h_apply(nb, g_ap, r_g, dstT, r_dstT, col_base, hn, r_hn, st, r_st):
            tiles = [nb * 4 + i for i in range(4)]

            def nrm(i):
                t_, s_ = tiles[i], i % 2
                cx.add("dve", (lambda e: e.scalar_tensor_tensor(hn[s_], hbuf[:, t_, :], st[:, 2, nb, i:i + 1], g_ap, ALU.mult, ALU.mult)),
                       reads=[r_h[t_], r_st[nb], r_g], writes=[r_hn[s_]])

            nrm(0)
            for i in range(4):
                if i + 1 < 4:
                    nrm(i + 1)
                transpose_tile(hn[i % 2], r_hn[i % 2], dstT, r_dstT, col_base + i * 128, "act")

        pxc = [0]
        rdc = [0]

        def xq_stage(nb):
            bs = nb % 2
            for c in range(8):
                pb, pr = bank()
                for k in range(8):
                    st_, sp_ = (k == 0), (k == 7)
                    cx.add("pe", (lambda e: e.matmul(pb, wxqb[:, k, c * 128:(c + 1) * 128], hnT5[bs][:, k, :], start=st_, stop=sp_)),
                           reads=r_wxq + [r_hnT5[bs]], writes=[pr])
                evac_copy(xqT[:, c, :], pb, pr, r_xqT[c // 2])

        def o_part(nb, i):
            xs = nb % 2
            t_ = nb * 4 + i
            for half in range(2):
                pb, pr = bank()
                for k in range(8):
                    st_, sp_ = (k == 0), (k == 7)
                    cx.add("pe", (lambda e: e.matmul(pb, xoT2[xs][:, k, i * 128:(i + 1) * 128], wxob[:, k, half * 512:(half + 1) * 512],
                                                     start=st_, stop=sp_)),
                           reads=r_xoT2[xs] + r_wxo, writes=[pr])
                hv = hbuf[:, t_, half * 512:(half + 1) * 512]
                cx.add("dve", (lambda e: e.tensor_tensor(hv, hv, pb, ALU.add)), reads=[pr, r_h[t_]], writes=[r_h[t_]])

        def head_scores(hx):
            pxs = []
            for kt in range(2):
                sb, sr = bank()
                for cc in range(2):
                    st_, sp_ = (cc == 0), (cc == 1)
                    cx.add("pe", (lambda e: e.matmul(sb, XKT[:, 2 * hx + cc, kt * 128:(kt + 1) * 128], xqT[:, 2 * hx + cc, :], start=st_, stop=sp_)),
                           reads=[r_XKT, r_xqT[hx]], writes=[sr])
                p_ = pxc[0] % 4
                pxc[0] += 1
                cx.add("act", (lambda e: e.activation(px[p_], sb, AF.Exp, scale=1.0 / 16.0)), reads=[sr], writes=[r_px[p_]])
                pxs.append(p_)
            return pxs

        def head_pv(nb, hx, pxs):
            xs = nb % 2
            db, dr = bank()
            for kt in range(2):
                st_, sp_ = (kt == 0), (kt == 1)
                p_ = pxs[kt]
                cx.add("pe", (lambda e: e.matmul(db, ones, px[p_], start=st_, stop=sp_)), reads=[r_px[p_], r_ones], writes=[dr])
            rd_ = rdc[0] % 2
            rdc[0] += 1
            cx.add("act", (lambda e: e.activation(rd5[rd_], db, AF.Ln)), reads=[dr], writes=[r_rd5[rd_]])
            cx.add("act", (lambda e: e.activation(rd5[rd_], rd5[rd_], AF.Exp, scale=-1.0)), reads=[r_rd5[rd_]], writes=[r_rd5[rd_]])
            for cc in range(2):
                ob, orr = bank()
                for kt in range(2):
                    st_, sp_ = (kt == 0), (kt == 1)
                    p_ = pxs[kt]
                    cx.add("pe", (lambda e: e.matmul(ob, XV[:, kt, (2 * hx + cc) * 128:(2 * hx + cc + 1) * 128], px[p_], start=st_, stop=sp_)),
                           reads=[r_px[p_], r_XV], writes=[orr])
                cx.add("dve", (lambda e: e.tensor_tensor(xoT2[xs][:, 2 * hx + cc, :], ob, rd5[rd_], ALU.mult)),
                       reads=[orr, r_rd5[rd_]], writes=[r_xoT2[xs][hx]])

        norm_h_stats(0, st5, r_st5, junk5, r_junk5)
        norm_h_apply(0, grep5, r_grep5, hnT5[0], r_hnT5[0], 0, hn5, r_hn5, st5, r_st5)
        norm_h_stats(1, st5, r_st5, junk5, r_junk5)
        for nb in range(4):
            xq_stage(nb)
            if nb + 1 < 4:
                norm_h_apply(nb + 1, grep5, r_grep5, hnT5[(nb + 1) % 2], r_hnT5[(nb + 1) % 2], 0, hn5, r_hn5, st5, r_st5)
                if nb + 2 < 4:
                    norm_h_stats(nb + 2, st5, r_st5, junk5, r_junk5)
            for hx in range(4):
                pxs = head_scores(hx)
                if nb > 0:
                    o_part(nb - 1, hx)
                head_pv(nb, hx, pxs)
        for i in range(4):
            o_part(3, i)
        cx.emit_phase()
        if stop_after == 5:
            if dbg:
                dump([hbuf[:, i, :] for i in range(16)])
                cx.final_wait_all_dma(["dbg"])
            return nc

        o = P6_BASE
        hnT6 = A(o, BF16, 8 * T).rearrange("p (k t) -> p k t", k=8); o += 32 * KB
        r_hnT6 = [cx.res("hnT6_%d" % i) for i in range(4)]
        o += 32 * KB
        HT = [A(o + i * 8 * KB, BF16, 8 * 512).rearrange("p (f t) -> p f t", f=8) for i in range(2)]; o += 16 * KB
        r_HT = [cx.res("HT%d" % i) for i in range(2)]
        rt = [A(o + i * 2 * KB, F32, 512) for i in range(2)]; o += 4 * KB
        r_rt = [cx.res("rt%d" % i) for i in range(2)]
        ostg = [A(o + i * 4 * KB, F32, 1024) for i in range(2)]; o += 8 * KB
        r_ostg = [cx.res("ostg%d" % i) for i in range(2)]
        hn6 = [A(o + i * 2 * KB, BF16, 1024) for i in range(2)]; o += 4 * KB
        r_hn6 = [cx.res("hn6_%d" % i) for i in range(2)]
        assert o <= W0_BASE
        o = W0_BASE + 32 * KB
        st6 = A(o, F32, 96).rearrange("p (z a b j) -> p z a b j", z=2, a=3, b=4); o += 384
        r_st6 = [[cx.res("st6_%d_%d" % (z, i)) for i in range(4)] for z in range(2)]
        assert o <= P2_BASE, o
        o = P2_BASE + 64 * KB
        grep6 = A(o, F32, 1024); o += 4 * KB
        grepf = A(o, F32, 1024); o += 4 * KB
        r_grep6 = cx.res("grep6")
        junk6 = A(o, BF16, 1024); o += 2 * KB
        r_junk6 = cx.res("junk6")
        assert o <= ARENA_BYTES, o

        load_group(1)
        cx.add("sp", lambda e: e.dma_start(out=grep6, in_=gvec[3:4, :].partition_broadcast(128)), writes=[r_grep6], dma="grep")
        r_grepf = cx.res("grepf")
        cx.add("sp", lambda e: e.dma_start(out=grepf, in_=gvec[4:5, :].partition_broadcast(128)), writes=[r_grepf], dma="grepf")
        htc = [0]
        rtc = [0]
        out_t = out.rearrange("(n p) d -> n p d", p=128)

        def up_stage(fg, nb, hs):
            s_ = fg % 2
            for f in range(8):
                pb, pr = bank()
                for k in range(8):
                    st_, sp_ = (k == 0), (k == 7)
                    cx.add("pe", (lambda e: e.matmul(pb, wup[s_][:, k, f * 128:(f + 1) * 128], hnT6[:, k, nb * 512:(nb + 1) * 512],
                                                     start=st_, stop=sp_)),
                           reads=r_wup[s_] + [r_hnT6[nb]], writes=[pr])
                r_ = rtc[0] % 2
                rtc[0] += 1
                cx.add("act", (lambda e: e.activation(rt[r_], pb, AF.Relu)), reads=[pr], writes=[r_rt[r_]])
                cx.add("pool", (lambda e: e.tensor_tensor(HT[hs][:, f, :], rt[r_], rt[r_], ALU.mult)), reads=[r_rt[r_]], writes=[r_HT[hs]])

        def down_stage(fg, nb, hs):
            s_ = fg % 2
            for i in range(4):
                t_ = nb * 4 + i
                for half in range(2):
                    pb, pr = bank()
                    for f in range(8):
                        st_, sp_ = (f == 0), (f == 7)
                        cx.add("pe", (lambda e: e.matmul(pb, HT[hs][:, f, i * 128:(i + 1) * 128], wdn[s_][:, f, half * 512:(half + 1) * 512],
                                                         start=st_, stop=sp_)),
                               reads=[r_HT[hs]] + r_wdn[s_], writes=[pr])
                    hv = hbuf[:, t_, half * 512:(half + 1) * 512]
                    cx.add("dve", (lambda e: e.tensor_tensor(hv, hv, pb, ALU.add)), reads=[pr, r_h[t_]], writes=[r_h[t_]])
            if fg == 3:
                tiles = [nb * 4 + i for i in range(4)]
                rms_stats([hbuf[:, t_, :] for t_ in tiles], [r_h[t_] for t_ in tiles], st6[:, 1, 0, nb, :], st6[:, 1, 1, nb, :],
                          st6[:, 1, 2, nb, :], r_st6[1][nb], junk6, r_junk6)
                for i, t_ in enumerate(tiles):
                    os_ = i % 2
                    cx.add("dve", (lambda e: e.scalar_tensor_tensor(ostg[os_], hbuf[:, t_, :], st6[:, 1, 2, nb, i:i + 1], grepf,
                                                                    ALU.mult, ALU.mult)),
                           reads=[r_h[t_], r_st6[1][nb], r_grepf], writes=[r_ostg[os_]])
                    cx.add("sp", (lambda e: e.dma_start(out=out_t[t_], in_=ostg[os_])), reads=[r_ostg[os_]], dma="out%d" % os_)
            if nb == 3 and fg + 2 < 4:
                load_group(fg + 2)

        items = [(fg, nb) for fg in range(4) for nb in range(4)]

        def norm6(nb):
            norm_h_apply(nb, grep6, r_grep6, hnT6, r_hnT6[nb], nb * 512, hn6, r_hn6, st6[:, 0], r_st6[0])
            if nb + 1 < 4:
                norm_h_stats(nb + 1, st6[:, 0], r_st6[0], junk6, r_junk6)

        norm_h_stats(0, st6[:, 0], r_st6[0], junk6, r_junk6)
        norm6(0)
        up_stage(0, 0, 0)
        for j in range(len(items)):
            if j + 1 < len(items):
                fg1, nb1 = items[j + 1]
                if fg1 == 0:
                    norm6(nb1)
                up_stage(fg1, nb1, (j + 1) % 2)
            down_stage(items[j][0], items[j][1], j % 2)
        cx.emit_phase()
        cx.final_wait_all_dma(["out0", "out1"])
    return nc


def _host_constants():
    ident = np.eye(128, dtype=np.float32)
    rm = np.zeros((128, 128), np.float32)
    for e in range(2):
        for i in range(8):
            rm[64 * e + 8 + i, 64 * e + i] = -1.0
            rm[64 * e + i, 64 * e + 8 + i] = 1.0
    onesm = np.full((128, 128), 1.0 / 512.0, np.float32)
    p = np.arange(128)[:, None]
    j = np.arange(128)[None, :]
    mask2 = np.concatenate([np.tile((p >= j), (1, 4)), np.tile((p <= j), (1, 4))], axis=1).astype(np.float32)
    return np.concatenate([ident, rm, onesm, mask2], axis=1)


def _tables(t0):
    pos = np.arange(t0 - HALO, t0 - HALO + E).astype(np.float32)
    pos = np.where((pos >= 0) & (pos < S), pos, np.float32(0.0)).astype(np.float32)
    rot = 16
    freqs = (np.float32(500000.0) ** (-np.arange(0, rot, 2, dtype=np.float32) / np.float32(rot))).astype(np.float32)
    ang = (pos[:, None] * freqs[None, :]).astype(np.float32)
    cos = np.cos(ang).astype(np.float32)
    sin = np.sin(ang).astype(np.float32)
    tc = np.ones((128, E), np.float32)
    ts = np.zeros((128, E), np.float32)
    for e in range(2):
        for i in range(16):
            tc[64 * e + i] = cos[:, i % 8]
            ts[64 * e + i] = sin[:, i % 8]
    return tc, ts


def _kbias(t0):
    kb = np.zeros((128, 72), np.float32)
    col = 0
    for d in PATTERNS:
        for r in range(d):
            for m in range(NKT[d]):
                start = t0 + r + d * (128 * m - 64)
                pos = start + d * np.arange(128)
                kb[:, col] = np.where((pos >= 0) & (pos < S), 0.0, -30000.0)
                col += 1
    return kb


_NC_CACHE = {}


def _get_nc(stop_after=99, dbg=False):
    key = (stop_after, dbg)
    if key not in _NC_CACHE:
        _NC_CACHE[key] = build_program(stop_after, dbg)
    return _NC_CACHE[key]


def make_in_maps(x, mem, norm_mix_g, w_in, conv_w, conv_b, conv_ln_g, conv_ln_b, w_out,
                 norm_x_g, norm_mem_g, w_xq, w_xk, w_xv, w_xo, norm_mlp_g, w_up, w_down, norm_final_g):
    f = lambda a: np.ascontiguousarray(np.asarray(a, dtype=np.float32))
    x = f(x)
    mem = f(mem)
    cst = _host_constants()
    gvec = np.stack([f(norm_mix_g)[0], f(norm_x_g)[0], f(norm_mem_g)[0], f(norm_mlp_g)[0], f(norm_final_g)], axis=0)
    cw = f(conv_w)[0]
    cwT = cw.T.reshape(4, 128, 31).transpose(1, 0, 2).reshape(128, 124)
    lay = lambda v: f(v)[0].reshape(4, 128).T
    convp = np.ascontiguousarray(np.concatenate([cwT, lay(conv_b), lay(conv_ln_g), lay(conv_ln_b)], axis=1))
    shared = {
        "cst": cst, "gvec": np.ascontiguousarray(gvec), "convp": convp,
        "w_in": f(w_in)[0], "w_out": f(w_out)[0], "w_xq": f(w_xq)[0], "w_xk": f(w_xk)[0], "w_xv": f(w_xv)[0],
        "w_xo": f(w_xo)[0], "w_up": f(w_up)[0], "w_down": f(w_down)[0],
    }
    in_maps = []
    for c in range(NCORES):
        b, t0 = c // 4, (c % 4) * T
        xe = np.zeros((E, D), np.float32)
        lo, hi = t0 - HALO, t0 + T + HALO
        slo, shi = max(lo, 0), min(hi, S)
        xe[slo - lo: shi - lo] = x[b, slo:shi]
        tc, ts = _tables(t0)
        m = dict(shared)
        kb = _kbias(t0)
        vr = np.ascontiguousarray(np.repeat((kb[:, :69] == 0).astype(np.float32), 64, axis=1))
        m.update({"x_ext": xe, "mem": mem[b], "tabC": tc, "tabS": ts, "kbias": kb, "vrep": vr})
        in_maps.append(m)
    return in_maps


def kernel(**inputs):
    in_maps = make_in_maps(**inputs)
    nc = _get_nc()
    res = run_bass_kernel_spmd(nc, in_maps, core_ids=list(range(NCORES)))
    outp = np.zeros((2, S, D), np.float32)
    for c in range(NCORES):
        b, t0 = c // 4, (c % 4) * T
        outp[b, t0:t0 + T] = res.results[c]["out"]
    return outp
```

```python
import numpy as np
import concourse.bass as bass
import concourse.mybir as mybir
from concourse.bass_utils import run_bass_kernel_spmd
from contextlib import ExitStack
import types

F32 = mybir.dt.float32
BF16 = mybir.dt.bfloat16
AF = mybir.ActivationFunctionType
ALU = mybir.AluOpType
AX = mybir.AxisListType

S = 8192
D = 1024
T = 2048
HALO = 1024
E = T + 2 * HALO
NCORES = 8
DIN = 2560
DFF = 4096
NMEM = 256
EPS = 1e-6
PATTERNS = (1, 4, 16)
NKT = {1: 17, 4: 5, 16: 2}
ARENA_BYTES = 207 * 1024

ENGS = ("sp", "act", "pool", "dve", "pe")
import os as _os
TRACE = open(_os.environ['K_TRACE'], 'w') if _os.environ.get('K_TRACE') else None


class Res:
    __slots__ = ("name", "w", "rs")

    def __init__(self, name=""):
        self.name = name
        self.w = None
        self.rs = []


class Op:
    __slots__ = ("eng", "fn", "reads", "writes", "dma", "dsem", "dcount", "deps", "signal", "sigval", "waits")

    def __init__(self, eng, fn, reads, writes, dma):
        self.eng = eng
        self.fn = fn
        self.reads = reads
        self.writes = writes
        self.dma = dma
        self.dsem = None
        self.dcount = 0
        self.deps = []
        self.signal = False
        self.sigval = None
        self.waits = []


class Ctx:
    def __init__(self, nc, stack):
        self.nc = nc
        self.stack = stack
        self.esem = {e: stack.enter_context(nc.semaphore("s_" + e)) for e in ENGS}
        self.ecount = {e: 0 for e in ENGS}
        self.dsems = {}
        self.dcounts = {}
        self.waited = {e: {} for e in ENGS}
        self.all_res = []
        self.ops = []

    def res(self, name=""):
        r = Res(name)
        self.all_res.append(r)
        return r

    def dsem(self, key):
        if key not in self.dsems:
            self.dsems[key] = self.stack.enter_context(self.nc.semaphore("d_" + key))
            self.dcounts[key] = 0
        return self.dsems[key]

    def add(self, eng, fn, reads=(), writes=(), dma=None):
        if fn.__closure__:
            cells = []
            for c in fn.__closure__:
                try:
                    cells.append(types.CellType(c.cell_contents))
                except ValueError:
                    cells.append(c)
            f2 = types.FunctionType(fn.__code__, fn.__globals__, fn.__name__, fn.__defaults__, tuple(cells))
            f2.__kwdefaults__ = fn.__kwdefaults__
            fn = f2
        op = Op(eng, fn, tuple(reads), tuple(writes), dma)
        deps = []
        for r in op.reads:
            if r.w is not None:
                deps.append((r.w, True))
        for r in op.writes:
            if r.w is not None:
                deps.append((r.w, False))
            for o in r.rs:
                deps.append((o, False))
        for r in op.reads:
            if r not in op.writes:
                r.rs.append(op)
        for r in op.writes:
            r.w = op
            r.rs = []
        if dma is not None:
            self.dsem(dma)
            self.dcounts[dma] += 16
            op.dsem = dma
            op.dcount = self.dcounts[dma]
        seen = {}
        for d, raw in deps:
            if d is op:
                continue
            k = id(d)
            if k in seen:
                seen[k] = (d, seen[k][1] or raw)
            else:
                seen[k] = (d, raw)
        for d, raw in seen.values():
            need = False
            if d.dma is not None:
                need = True
            elif d.eng != eng:
                need = True
            elif eng != "pe":
                need = True
            if need:
                op.waits.append(d)
                if d.dma is None:
                    d.signal = True
        self.ops.append(op)
        return op

    def emit_phase(self):
        nc = self.nc
        ops = self.ops
        self.ops = []
        per = {e: [o for o in ops if o.eng == e] for e in ENGS}
        last = {}
        for e in ENGS:
            for o in reversed(per[e]):
                if o.dma is None:
                    o.signal = True
                    last[e] = o
                    break
        for e in ENGS:
            for o in per[e]:
                if o.dma is None and o.signal:
                    self.ecount[e] += 1
                    o.sigval = self.ecount[e]
        final = {e: self.ecount[e] for e in ENGS if e in last}

        def run(e, engine):
            waited = self.waited[e]
            for o in per[e]:
                need = {}
                for d in o.waits:
                    if d.dma is not None:
                        key = ("d", d.dsem)
                        val = d.dcount
                    else:
                        key = ("e", d.eng)
                        val = d.sigval
                    if val > need.get(key, 0):
                        need[key] = val
                for key, val in need.items():
                    if waited.get(key, 0) >= val:
                        continue
                    sem = self.dsems[key[1]] if key[0] == "d" else self.esem[key[1]]
                    engine.wait_ge(sem, val)
                    waited[key] = val
                    if TRACE is not None:
                        TRACE.write("%s WAIT %s >= %d\n" % (e, key, val))
                ins = o.fn(engine)
                if TRACE is not None:
                    TRACE.write("%s OP %s sig=%s dma=%s\n" % (e, getattr(o.fn, '__qualname__', '?') + ':' + str(o.fn.__code__.co_firstlineno), o.sigval, o.dsem))
                if o.dma is not None:
                    ins.then_inc(self.dsems[o.dsem], 16)
                elif o.signal:
                    ins.then_inc(self.esem[e], 1)
            for f, val in final.items():
                if f == e:
                    continue
                key = ("e", f)
                if waited.get(key, 0) >= val:
                    continue
                engine.wait_ge(self.esem[f], val)
                waited[key] = val

        with nc.Block() as block:
            @block.sync
            def _(eng):
                run("sp", eng)

            @block.scalar
            def _(eng):
                run("act", eng)

            @block.gpsimd
            def _(eng):
                run("pool", eng)

            @block.vector
            def _(eng):
                run("dve", eng)

            @block.tensor
            def _(eng):
                run("pe", eng)
        for r in self.all_res:
            if r.w is not None and r.w.dma is None:
                r.w = None
            r.rs = [o for o in r.rs if o.dma is not None]

    def final_wait_all_dma(self, keys):
        nc = self.nc
        with nc.Block() as block:
            @block.sync
            def _(eng):
                for k in keys:
                    eng.wait_ge(self.dsems[k], self.dcounts[k])


def build_program(stop_after=99, dbg=False):
    nc = bass.Bass("TRN2", target_bir_lowering=False)

    def din(name, shape, dt=F32):
        return nc.dram_tensor(name, list(shape), dt, kind="ExternalInput").ap()

    x_ext = din("x_ext", [E, D])
    mem = din("mem", [NMEM, D])
    tabC = din("tabC", [128, E])
    tabS = din("tabS", [128, E])
    cst = din("cst", [128, 128 * 3 + 1024])
    kbias_d = din("kbias", [128, 72])
    vrep_d = din("vrep", [128, 69 * 64])
    gvec = din("gvec", [5, D])
    convp = din("convp", [128, 4 * 31 + 12])
    w_in = din("w_in", [D, DIN])
    w_out = din("w_out", [D, D])
    w_xq = din("w_xq", [D, D])
    w_xk = din("w_xk", [D, D])
    w_xv = din("w_xv", [D, D])
    w_xo = din("w_xo", [D, D])
    w_up = din("w_up", [D, DFF])
    w_down = din("w_down", [DFF, D])
    out = nc.dram_tensor("out", [T, D], F32, kind="ExternalOutput").ap()
    dbg_out = None
    if dbg:
        dbg_out = nc.dram_tensor("dbg", [128, 20 * 1024], F32, kind="ExternalOutput").ap()

    stack = ExitStack()
    with stack:
        arena = stack.enter_context(nc.sbuf_tensor("arena", [128, ARENA_BYTES // 2], BF16))
        psf = [stack.enter_context(nc.psum_tensor("psf%d" % i, [128, 512], F32)) for i in range(7)]
        psb_h = stack.enter_context(nc.psum_tensor("psb", [128, 1024], BF16))
        psb = psb_h[:, :]
        psf = [p[:, :] for p in psf]
        cx = Ctx(nc, stack)

        def A(off, dt, n):
            assert off % 4 == 0
            if dt == BF16:
                assert off + 2 * n <= ARENA_BYTES, (off, n)
                return arena[:, off // 2: off // 2 + n]
            assert off + 4 * n <= ARENA_BYTES, (off, n)
            return arena[:, off // 2: off // 2 + 2 * n].bitcast(F32)

        KB = 1024
        psum_res = [cx.res("psf%d" % i) for i in range(7)]
        psb_res = cx.res("psb")
        bank_rr = [0]

        def bank():
            i = bank_rr[0] % 6
            bank_rr[0] += 1
            return psf[i], psum_res[i]

        psb2 = psf[6].bitcast(BF16)
        trbufs = [(psb, psb_res), (psb2, psum_res[6])]
        tr_ctr = [0]

        o = 0
        ident = A(o, BF16, 128); o += 256
        Rm = A(o, BF16, 128); o += 256
        onesm = A(o, BF16, 128); o += 256
        mask2 = A(o, BF16, 1024); o += 2048
        ones = A(o, BF16, 128); o += 256
        kbias = A(o, F32, 72); o += 288
        epsc = A(o, F32, 1); o += 4
        cprm = A(o, F32, 4 * 31 + 12); o += 544
        cw = cprm[:, 0:124].rearrange("p (c k) -> p c k", c=4)
        cb = cprm[:, 124:128]
        lng = cprm[:, 128:132]
        lnb = cprm[:, 132:136]
        r_cprm = cx.res("cprm")
        o = 4 * KB
        SMALL_END = o
        r_const = cx.res("const")

        cx.add("pool", lambda e: e.dma_start(out=arena[:, 0:1408], in_=cst), writes=[r_const], dma="cst")
        cx.add("sp", lambda e: e.dma_start(out=cprm, in_=convp), writes=[r_cprm], dma="cprm")
        r_ones = cx.res("ones")
        r_eps = cx.res("eps")
        cx.add("dve", lambda e: e.memset(ones, 1.0), writes=[r_ones])
        cx.add("dve", lambda e: e.memset(epsc, EPS), writes=[r_eps])

        def wload(dst, dst_res, src, key, nsplit=1):
            cx.add("pool", (lambda e: e.dma_start(out=dst, in_=src)), writes=list(dict.fromkeys(dst_res)), dma=key)

        def rms_stats(srcs, src_res, ss, sq, rstd, r_ss, junk, r_junk):
            n = len(srcs)
            for j, (sap, sr) in enumerate(zip(srcs, src_res)):
                cx.add("act", (lambda e, sap=sap, j=j: e.activation(junk, sap, AF.Square, scale=1.0 / 32.0,
                                                                   accum_out=ss[:, j:j + 1])),
                       reads=[sr], writes=[r_junk, r_ss])
            cx.add("act", lambda e: e.activation(sq[:, 0:n], ss[:, 0:n], AF.Sqrt, bias=epsc, scale=1.0),
                   reads=[r_ss, r_eps], writes=[r_ss])
            cx.add("dve", lambda e: e.reciprocal(rstd[:, 0:n], sq[:, 0:n]), reads=[r_ss], writes=[r_ss])

        def transpose_tile(xn_ap, r_xn, dstT, r_dstT, col0, evac_eng):
            tb, tres = trbufs[tr_ctr[0] % 2]
            tr_ctr[0] += 1
            for k in range(8):
                cx.add("pe", (lambda e, k=k: e.transpose(tb[:, k * 128:(k + 1) * 128], xn_ap[:, k * 128:(k + 1) * 128], ident)),
                       reads=[r_xn, r_const], writes=[tres])
            src = tb.rearrange("p (k t) -> p k t", k=8)
            dst = dstT[:, :, col0:col0 + 128]
            if evac_eng == "act":
                cx.add("act", lambda e: e.copy(dst, src), reads=[tres], writes=[r_dstT])
            else:
                cx.add("dve", lambda e: e.tensor_copy(dst, src), reads=[tres], writes=[r_dstT])

        o = SMALL_END
        QT = A(o, BF16, 4 * T).rearrange("p (c t) -> p c t", c=4); o += 16 * KB
        KT = A(o, BF16, 4 * E).rearrange("p (c t) -> p c t", c=4); o += 32 * KB
        VT = A(o, BF16, 4 * E).rearrange("p (c t) -> p c t", c=4); o += 32 * KB
        UW = T + 32
        uT = A(o, BF16, 4 * UW).rearrange("p (c t) -> p c t", c=4); o += 4 * UW * 2
        o = (o + 1023) // 1024 * 1024
        P1_BASE = o
        r_QT = [cx.res("QT%d" % b) for b in range(4)]
        r_KT = [cx.res("KT%d" % b) for b in range(8)]
        r_VT = [cx.res("VT%d" % b) for b in range(8)]
        r_uT = [cx.res("uT%d" % b) for b in range(6)]

        o = P1_BASE
        winb = A(o, BF16, 8 * DIN).rearrange("p (k n) -> p k n", k=8); o += 40 * KB
        xnT = [A(o + i * 8 * KB, BF16, 8 * 512).rearrange("p (k t) -> p k t", k=8) for i in range(2)]; o += 16 * KB
        r_xnT = [cx.res("xnT%d" % i) for i in range(2)]
        xt = [A(o + i * 4 * KB, F32, 1024) for i in range(4)]; o += 16 * KB
        r_xt = [cx.res("xt%d" % i) for i in range(4)]
        xn = [A(o + i * 2 * KB, BF16, 1024) for i in range(2)]; o += 4 * KB
        r_xn = [cx.res("xn%d" % i) for i in range(2)]
        tC = [A(o + i * 2 * KB, F32, 512) for i in range(2)]; o += 4 * KB
        tS = [A(o + i * 2 * KB, F32, 512) for i in range(2)]; o += 4 * KB
        r_tabC = [cx.res("tabC%d" % i) for i in range(2)]
        r_tabS = [cx.res("tabS%d" % i) for i in range(2)]
        qraw = [A(o + i * KB, BF16, 512) for i in range(2)]; o += 2 * KB
        r_qraw = [cx.res("qraw%d" % i) for i in range(2)]
        tmpA = [A(o + i * 2 * KB, F32, 512) for i in range(2)]; o += 4 * KB
        r_tmpA = [cx.res("tmpA%d" % i) for i in range(2)]
        tmpB = [A(o + i * 2 * KB, F32, 512) for i in range(2)]; o += 4 * KB
        r_tmpB = [cx.res("tmpB%d" % i) for i in range(2)]
        sg = [A(o + i * 2 * KB, F32, 512) for i in range(2)]; o += 4 * KB
        r_sg = [cx.res("sg%d" % i) for i in range(2)]
        grep = A(o, F32, 1024); o += 4 * KB
        r_grep = cx.res("grep")
        junk = A(o, BF16, 1024); o += 2 * KB
        r_junk = cx.res("junk")
        ss = A(o, F32, 32).rearrange("p (b j) -> p b j", b=8); o += 128
        sq = A(o, F32, 32).rearrange("p (b j) -> p b j", b=8); o += 128
        rstd = A(o, F32, 32).rearrange("p (b j) -> p b j", b=8); o += 128
        r_ss = [cx.res("ss%d" % b) for b in range(8)]
        assert o <= ARENA_BYTES, o

        w_in_v = w_in.rearrange("(k p) n -> p k n", p=128)
        r_win = [cx.res("win%d" % i) for i in range(5)]
        def load_win(g_, extra_reads=()):
            cx.add("pool", (lambda e: e.dma_start(out=winb[:, :, g_ * 512:(g_ + 1) * 512], in_=w_in_v[:, :, g_ * 512:(g_ + 1) * 512])),
                   reads=list(extra_reads), writes=[r_win[g_]], dma="win%d" % g_)

        load_win(1)
        load_win(2)
        cx.add("sp", lambda e: e.dma_start(out=grep, in_=gvec[0:1, :].partition_broadcast(128)),
               writes=[r_grep], dma="grep")

        xe_t = x_ext.rearrange("(n p) d -> n p d", p=128)
        tile_ctr = [0]
        ev_ctr = [0]

        def load_x_tile(n, dst, dres, key):
            cx.add("sp", (lambda e: e.dma_start(out=dst, in_=xe_t[n])), writes=[dres], dma=key)

        def norm_pre(b, half):
            idx = [2 * half, 2 * half + 1]
            for i in idx:
                load_x_tile(b * 4 + i, xt[i], r_xt[i], "xt%d" % i)
            rms_stats([xt[i] for i in idx], [r_xt[i] for i in idx],
                      ss[:, b, 2 * half:2 * half + 2], sq[:, b, 2 * half:2 * half + 2],
                      rstd[:, b, 2 * half:2 * half + 2], r_ss[b], junk, r_junk)
            for i in idx:
                cx.add("dve", (lambda e: e.scalar_tensor_tensor(xn[i % 2], xt[i], rstd[:, b, i:i + 1], grep, ALU.mult, ALU.mult)),
                       reads=[r_xt[i], r_ss[b], r_grep], writes=[r_xn[i % 2]])

        def norm_tr(b, half):
            bs = b % 2
            for i in (2 * half, 2 * half + 1):
                ev_ctr[0] += 1
                transpose_tile(xn[i % 2], r_xn[i % 2], xnT[bs], r_xnT[bs], i * 128, "act" if ev_ctr[0] % 2 else "dve")

        def proj_chunk(bs, c0, n0=0, n=512):
            pb, pr = bank()
            for k in range(8):
                cx.add("pe", (lambda e, k=k: e.matmul(pb[:, 0:n], winb[:, k, c0:c0 + 128], xnT[bs][:, k, n0:n0 + n],
                                                      start=(k == 0), stop=(k == 7))),
                       reads=[r_xnT[bs], r_win[c0 // 512]], writes=[pr])
            flush_rot()
            return pb, pr

        rot_ctr = [0]

        pending_rot = []

        def flush_rot():
            while pending_rot:
                pending_rot.pop(0)()

        def rotary_evac(pb, pr, ts_, dst, dres):
            s_ = rot_ctr[0] % 2
            rot_ctr[0] += 1
            cx.add("act", lambda e: e.copy(qraw[s_], pb), reads=[pr], writes=[r_qraw[s_]])

            def rest():
                qb, qr = bank()
                cx.add("pe", lambda e: e.matmul(qb, Rm, qraw[s_], start=True, stop=True), reads=[r_qraw[s_], r_const], writes=[qr])
                cx.add("dve", lambda e: e.tensor_tensor(tmpA[s_], qb, tS[ts_], ALU.mult), reads=[qr, r_tabS[ts_]], writes=[r_tmpA[s_]])
                cx.add("pool", lambda e: e.tensor_tensor(tmpB[s_], qraw[s_], tC[ts_], ALU.mult), reads=[r_qraw[s_], r_tabC[ts_]], writes=[r_tmpB[s_]])
                cx.add("dve", lambda e: e.tensor_tensor(dst, tmpA[s_], tmpB[s_], ALU.add), reads=[r_tmpA[s_], r_tmpB[s_]], writes=[dres])
            pending_rot.append(rest)

        glu_ctr = [0]
        vev = [0]

        def k_chunk(b, pr_):
            pb, pres = proj_chunk(b % 2, 512 + pr_ * 128)
            rotary_evac(pb, pres, b % 2, KT[:, pr_, b * 512:(b + 1) * 512], r_KT[b])

        def v_chunk(b, pr_):
            pb, pres = proj_chunk(b % 2, 1024 + pr_ * 128)
            dst = VT[:, pr_, b * 512:(b + 1) * 512]
            vev[0] += 1
            if vev[0] % 2:
                cx.add("act", (lambda e: e.copy(dst, pb)), reads=[pres], writes=[r_VT[b]])
            else:
                cx.add("dve", (lambda e: e.tensor_copy(dst, pb)), reads=[pres], writes=[r_VT[b]])

        def q_chunk(b, pr_):
            ob = b - 2
            pb, pres = proj_chunk(b % 2, pr_ * 128)
            rotary_evac(pb, pres, b % 2, QT[:, pr_, ob * 512:(ob + 1) * 512], r_QT[ob])

        def glu_chunk(b, c):
            if 2 <= b <= 5:
                n0, n, ucol, ures = 0, 512, 16 + (b - 2) * 512, r_uT[b - 2]
            elif b == 1:
                n0, n, ucol, ures = 496, 16, 0, r_uT[4]
            else:
                n0, n, ucol, ures = 0, 16, 16 + T, r_uT[5]
            pa, pares = proj_chunk(b % 2, 1536 + c * 128, n0, n)
            pg, pgres = proj_chunk(b % 2, 2048 + c * 128, n0, n)
            s_ = glu_ctr[0] % 2
            glu_ctr[0] += 1
            cx.add("act", (lambda e: e.activation(sg[s_][:, 0:n], pg[:, 0:n], AF.Sigmoid)), reads=[pgres], writes=[r_sg[s_]])
            dst = uT[:, c, ucol:ucol + n]
            cx.add("dve", (lambda e: e.tensor_tensor(dst, pa[:, 0:n], sg[s_][:, 0:n], ALU.mult)),
                   reads=[pares, r_sg[s_]], writes=[ures])

        def load_tables(b):
            ts_ = b % 2
            cx.add("sp", (lambda e: e.dma_start(out=tC[ts_], in_=tabC[:, b * 512:(b + 1) * 512])), writes=[r_tabC[ts_]], dma="tabC%d" % ts_)
            cx.add("sp", (lambda e: e.dma_start(out=tS[ts_], in_=tabS[:, b * 512:(b + 1) * 512])), writes=[r_tabS[ts_]], dma="tabS%d" % ts_)

        load_tables(0)
        norm_pre(0, 0)
        norm_tr(0, 0)
        norm_pre(0, 1)
        norm_tr(0, 1)
        norm_pre(1, 0)
        for g_ in (0, 3, 4):
            load_win(g_, extra_reads=[r_xnT[0]])
        for b in range(8):
            own = 2 <= b <= 5
            work = [(k_chunk, pr_) for pr_ in range(4)] + [(v_chunk, pr_) for pr_ in range(4)]
            if own:
                work += [(q_chunk, pr_) for pr_ in range(4)]
            if own or b == 1 or b == 6:
                work += [(glu_chunk, c) for c in range(4)]
            n_w = len(work)
            marks = {n_w // 3: 0, (2 * n_w) // 3: 1}
            for wi, (fn_, arg_) in enumerate(work):
                if b + 1 < 8 and wi in marks:
                    if marks[wi] == 0:
                        load_tables(b + 1)
                        norm_tr(b + 1, 0)
                        norm_pre(b + 1, 1)
                    else:
                        norm_tr(b + 1, 1)
                        if b + 2 < 8:
                            norm_pre(b + 2, 0)
                fn_(b, arg_)
        flush_rot()

        def dump(ap_list):
            col = 0
            rr = cx.res("dbgstg")
            for ap in ap_list:
                n = ap.shape[1]
                stg = A(ARENA_BYTES - 8 * KB, F32, 2048)
                for c0 in range(0, n, 2048):
                    m = min(2048, n - c0)
                    cx.add("dve", (lambda e, ap=ap, c0=c0, m=m: e.tensor_copy(stg[:, 0:m], ap[:, c0:c0 + m])), writes=[rr])
                    cx.add("sp", (lambda e, col=col, c0=c0, m=m: e.dma_start(out=dbg_out[:, col + c0:col + c0 + m], in_=stg[:, 0:m])),
                           reads=[rr], dma="dbg")
                    cx.emit_phase()
                col += n

        cx.emit_phase()
        if stop_after == 1:
            if dbg:
                dump([QT[:, 0, :], KT[:, 0, :], VT[:, 0, :], uT[:, 0, :], QT[:, 3, :], KT[:, 3, :]])
                cx.final_wait_all_dma(["dbg"])
            return nc

        o = P1_BASE
        attT = A(o, BF16, 4 * T).rearrange("p (c t) -> p c t", c=4); o += 16 * KB
        convT = A(o, BF16, 4 * T).rearrange("p (c t) -> p c t", c=4); o += 16 * KB
        P2_BASE = o
        r_attT = cx.res("attT")
        acc = A(o, F32, 2 * 2 * T).rearrange("p (a c t) -> p a c t", a=2, c=2); o += 32 * KB
        r_acc = cx.res("acc")
        Vt = A(o, BF16, 32 * 256).rearrange("p (k f) -> p k f", k=32); o += 16 * KB
        r_Vt = [cx.res("Vt%d" % i) for i in range(8)]
        pexp = [A(o + i * KB, BF16, 512) for i in range(6)]; o += 6 * KB
        r_pexp = [cx.res("pexp%d" % i) for i in range(6)]
        pm = [A(o + i * KB, BF16, 512) for i in range(6)]; o += 6 * KB
        r_pm = [cx.res("pm%d" % i) for i in range(6)]
        obanks = [(psf[6], psum_res[6]), (psb.bitcast(F32), psb_res)]
        vrep = A(o, BF16, 69 * 64).rearrange("p (k f) -> p k f", k=69); o += 69 * 128
        r_vrep = cx.res("vrep")
        cx.add("pool", lambda e: e.dma_start(out=vrep, in_=vrep_d.rearrange("p (k f) -> p k f", k=69)), writes=[r_vrep], dma="vrep")
        assert o <= ARENA_BYTES
        r_KTh = [cx.res("KTlo"), cx.res("KThi")]
        r_VTh = [cx.res("VTlo"), cx.res("VThi")]
        r_QTall = r_QT
        diagA = A(20 * KB, BF16, 2 * 31 * 128).rearrange("p (c k m) -> p c k m", c=2, k=31)
        diagB = A(52 * KB, BF16, 2 * 31 * 128).rearrange("p (c k m) -> p c k m", c=2, k=31)
        r_diag = [cx.res("diag%d" % c) for c in range(4)]

        def dg(c, k):
            return (diagA if c < 2 else diagB)[:, c % 2, k, :]

        def diag_ops():
            for c in range(4):
                region = r_KTh[0] if c < 2 else r_VTh[0]
                dfull = diagA if c < 2 else diagB
                for k0 in range(0, 31, 8):
                    nk = min(8, 31 - k0)
                    dst = dfull[:, c % 2, k0:k0 + nk, :]
                    i0 = bass.AP(ident.tensor, ident.offset, [list(ident.ap[0]), [0, nk], [1, 128]])
                    cwv = cw[:, c, k0:k0 + nk]
                    i1 = bass.AP(cwv.tensor, cwv.offset, [list(cwv.ap[0]), [1, nk], [0, 128]])
                    cx.add("pool", (lambda e: e.tensor_tensor(dst, i0, i1, ALU.mult)), reads=[r_const, r_cprm], writes=[r_diag[c], region])
                    yield None

        diag_gen = diag_ops()
        dq_ctr = [0]

        def sap(base3, row0, nrow, c, start, step, n):
            v = base3[row0:row0 + nrow, c, start:start + (n - 1) * step + 1]
            return bass.AP(v.tensor, v.offset, [list(v.ap[0]), [step, n]])

        kt_col = {}
        col = 0
        for d in PATTERNS:
            for r in range(d):
                for m in range(NKT[d]):
                    kt_col[(d, r, m)] = col
                    col += 1
        assert col == 69

        pe_ctr = [0]
        pm_ctr = [0]
        qb_ctr = [0]
        vtb = [(psb, psb_res), (psf[6].bitcast(BF16), psum_res[6])]
        vtb_ctr = [0]
        VtB = A(117 * KB, BF16, 20 * 256).rearrange("p (k f) -> p k f", k=20)
        r_VtB = [cx.res("VtB%d" % i) for i in range(5)]
        vt_bufs = {"A": (Vt, r_Vt), "B": (VtB, r_VtB)}
        plan = {0: [(16, "A"), (1, "B"), (4, "A")], 1: [(1, "B"), (16, "A"), (4, "B")]}

        def build_vtiles(hg, d, vname):
            Vb, r_Vb = vt_bufs[vname]
            nkt = NKT[d]
            tiles = [(r, m) for r in range(d) for m in range(nkt)]
            for g0 in range(0, len(tiles), 4):
                grp = tiles[g0:g0 + 4]
                vb, vres = vtb[vtb_ctr[0] % 2]
                vtb_ctr[0] += 1
                for gi, (r, m) in enumerate(grp):
                    start = HALO + r + d * (128 * m - 64)
                    for pl in range(2):
                        in_ = sap(VT, 0, 128, 2 * hg + pl, start, d, 128)
                        dstp = vb[:, gi * 256 + pl * 128: gi * 256 + (pl + 1) * 128]
                        cx.add("pe", (lambda e: e.transpose(dstp, in_, ident)), reads=[r_VTh[hg], r_const], writes=[vres])
                ng = len(grp)
                dst = Vb[:, g0:g0 + ng, :]
                src = vb[:, 0:ng * 256].rearrange("p (k f) -> p k f", k=ng)
                cx.add("act", (lambda e: e.copy(dst, src)), reads=[vres], writes=[r_Vb[g0 // 4]])

        def stage_a(hg, d, r, n):
            qstart = d * 128 * n + r
            qsel = qb_ctr[0] % 3
            osel = qb_ctr[0] % 2
            qb_ctr[0] += 1
            banks = [(psf[2 * qsel], psum_res[2 * qsel]), (psf[2 * qsel + 1], psum_res[2 * qsel + 1])]
            for e_ in range(2):
                sb, sr = banks[e_]
                for kk in range(2):
                    kstart = HALO + r + d * (128 * (n + kk) - 64)
                    for hl in range(2):
                        lhsT = sap(KT, 64 * e_, 64, 2 * hg + hl, kstart, d, 128)
                        rhs = sap(QT, 64 * e_, 64, 2 * hg + hl, qstart, d, 128)
                        c0 = (kk * 2 + hl) * 128
                        cx.add("pe", (lambda e: e.matmul(sb[:, c0:c0 + 128], lhsT, rhs, start=True, stop=True)),
                               reads=[r_KTh[hg]] + r_QTall, writes=[sr])
            pmi = []
            for e_ in range(2):
                sb, sr = banks[e_]
                s_ = pe_ctr[0] % 6
                pe_ctr[0] += 1
                cx.add("act", (lambda e: e.activation(pexp[s_], sb, AF.Exp, scale=0.125)), reads=[sr], writes=[r_pexp[s_]])
                ps_ = pm_ctr[0] % 6
                pm_ctr[0] += 1
                cx.add("dve" if e_ == 0 else "pool", (lambda e: e.tensor_tensor(pm[ps_], pexp[s_], mask2[:, 256:768], ALU.mult)),
                       reads=[r_pexp[s_], r_const], writes=[r_pm[ps_]])
                pmi.append(ps_)
            return (osel, pmi)

        def stage_b(hg, d, vname, first, r, n, st):
            osel, pmi = st
            Vb, r_Vb = vt_bufs[vname]
            nkt = NKT[d]
            qstart = d * 128 * n + r
            ob, orr = obanks[osel]
            ov = ob.rearrange("p (a c q) -> p a c q", a=2, c=2)
            for hh in range(4):
                pl, e_ = hh // 2, hh % 2
                ps_ = pmi[e_]
                for a_ in range(2):
                    for kk in range(2):
                        vti = r * nkt + n + kk
                        kc = kt_col[(d, r, n + kk)]
                        lhsT = Vb[:, vti, hh * 64:(hh + 1) * 64] if a_ == 0 else vrep[:, kc, :]
                        rhs = pm[ps_][:, (kk * 2 + pl) * 128:(kk * 2 + pl + 1) * 128]
                        dsto = ov[64 * e_:64 * e_ + 64, a_, pl, :]
                        st_, sp_ = (kk == 0), (kk == 1)
                        cx.add("pe", (lambda e: e.matmul(dsto, lhsT, rhs, start=st_, stop=sp_)),
                               reads=[r_pm[ps_], r_Vb[vti // 4], r_vrep], writes=[orr])
            av = acc[:, :, :, qstart:qstart + (127 * d) + 1]
            av = bass.AP(av.tensor, av.offset, [list(av.ap[0]), list(av.ap[1]), list(av.ap[2]), [d, 128]])
            if first:
                cx.add("dve", (lambda e: e.tensor_copy(av, ov)), reads=[orr], writes=[r_acc])
            else:
                cx.add("dve", (lambda e: e.tensor_tensor(av, av, ov, ALU.add)), reads=[orr, r_acc], writes=[r_acc])

        build_vtiles(0, *plan[0][0])
        for hg in range(2):
            inflight = []
            for pi, (d, vname) in enumerate(plan[hg]):
                nqb = T // d // 128
                cnt = 0
                for r in range(d):
                    for n in range(nqb):
                        if hg == 1:
                            dq_ctr[0] += 1
                            if dq_ctr[0] % 3 == 1:
                                next(diag_gen, None)
                        inflight.append((d, vname, pi == 0, r, n, stage_a(hg, d, r, n)))
                        if len(inflight) > 2:
                            stage_b(hg, *inflight.pop(0))
                        cnt += 1
                        if cnt == 3:
                            if pi + 1 < 3:
                                build_vtiles(hg, *plan[hg][pi + 1])
                            elif hg == 0:
                                build_vtiles(1, *plan[1][0])
            while inflight:
                stage_b(hg, *inflight.pop(0))
            def normalise_hg(hg_):
                for pl in range(2):
                    cx.add("act", (lambda e: e.activation(acc[:, 1, pl, :], acc[:, 1, pl, :], AF.Ln)), reads=[r_acc], writes=[r_acc])
                    cx.add("act", (lambda e: e.activation(acc[:, 1, pl, :], acc[:, 1, pl, :], AF.Exp, scale=-1.0)), reads=[r_acc], writes=[r_acc])
                    cx.add("dve" if pl == 0 else "pool", (lambda e: e.tensor_tensor(attT[:, 2 * hg_ + pl, :], acc[:, 0, pl, :], acc[:, 1, pl, :], ALU.mult)),
                           reads=[r_acc], writes=[r_attT])
            if hg == 0:
                normalise_hg(0)
        for _ in diag_gen:
            pass
        cx.emit_phase()
        if stop_after == 2:
            if dbg:
                dump([attT[:, 0, :], attT[:, 1, :], attT[:, 2, :], attT[:, 3, :]])
                cx.final_wait_all_dma(["dbg"])
            return nc

        woutb = A(4 * KB, BF16, 8 * 1024).rearrange("p (k n) -> p k n", k=8)
        r_wout = [cx.res("wout")] * 4
        wxkb = A(36 * KB, BF16, 8 * 1024).rearrange("p (k n) -> p k n", k=8)
        r_wxk = [cx.res("wxk")] * 4
        wxvb = A(68 * KB, BF16, 8 * 1024).rearrange("p (k n) -> p k n", k=8)
        r_wxv = [cx.res("wxv")] * 4
        wload(woutb, r_wout, w_out.rearrange("(k p) n -> p k n", p=128), "wout")
        wload(wxkb, r_wxk, w_xk.rearrange("(k p) n -> p k n", p=128), "wxk")
        wload(wxvb, r_wxv, w_xv.rearrange("(k p) n -> p k n", p=128), "wxv")
        normalise_hg(1)
        o = P2_BASE + 32 * KB
        yb = [A(o + i * 8 * KB, F32, 2048).rearrange("p (c t) -> p c t", c=4) for i in range(2)]; o += 16 * KB
        r_yb = [cx.res("yb%d" % i) for i in range(2)]
        ybf = [A(o + i * 4 * KB, BF16, 2048).rearrange("p (c t) -> p c t", c=4) for i in range(2)]; o += 8 * KB
        r_ybf = [cx.res("ybf%d" % i) for i in range(2)]
        ysq = [A(o + i * 4 * KB, BF16, 2048).rearrange("p (c t) -> p c t", c=4) for i in range(2)]; o += 8 * KB
        r_ysq = [cx.res("ysq%d" % i) for i in range(2)]
        mean_sb = A(o, F32, 512); o += 2 * KB
        var_sb = A(o, F32, 512); o += 2 * KB
        rs_sb = A(o, F32, 512); o += 2 * KB
        r_st = cx.res("lnstat")
        tt = [A(o + i * 2 * KB, F32, 512) for i in range(2)]; o += 4 * KB
        r_tt = [cx.res("tt%d" % i) for i in range(2)]
        r_convT = cx.res("convT")
        r_uTall = r_uT
        assert o <= ARENA_BYTES, o

        tctr = [0]

        conv_banks = {}

        def conv_mm(nb):
            conv_banks[nb] = []
            for c in range(4):
                pb, pr = bank()
                conv_banks[nb].append((pb, pr))
                for k in range(31):
                    st_, sp_ = (k == 0), (k == 30)
                    cx.add("pe", (lambda e: e.matmul(pb, dg(c, k), uT[:, c, nb * 512 + k + 1: nb * 512 + k + 513], start=st_, stop=sp_)),
                           reads=r_uTall + [r_diag[c]], writes=[pr])

        def conv_evac(nb):
            s_ = nb % 2
            for c in range(4):
                pb, pr = conv_banks[nb][c]
                cx.add("act", (lambda e: e.activation(yb[s_][:, c, :], pb, AF.Identity, bias=cb[:, c:c + 1], scale=1.0)),
                       reads=[pr, r_cprm], writes=[r_yb[s_]])
                cx.add("act", (lambda e: e.activation(ysq[s_][:, c, :], pb, AF.Square, bias=cb[:, c:c + 1], scale=1.0)),
                       reads=[pr, r_cprm], writes=[r_ysq[s_]])
                cx.add("dve", (lambda e: e.tensor_copy(ybf[s_][:, c, :], yb[s_][:, c, :])), reads=[r_yb[s_]], writes=[r_ybf[s_]])

        def ln_stage(nb):
            s_ = nb % 2
            mb, mr = bank()
            for c in range(4):
                st_, sp_ = (c == 0), (c == 3)
                cx.add("pe", (lambda e: e.matmul(mb, onesm, ybf[s_][:, c, :], start=st_, stop=sp_)), reads=[r_ybf[s_], r_const], writes=[mr])
            qb_, qr_ = bank()
            for c in range(4):
                st_, sp_ = (c == 0), (c == 3)
                cx.add("pe", (lambda e: e.matmul(qb_, onesm, ysq[s_][:, c, :], start=st_, stop=sp_)), reads=[r_ysq[s_], r_const], writes=[qr_])
            cx.add("act", (lambda e: e.copy(mean_sb, mb)), reads=[mr], writes=[r_st])
            cx.add("dve", lambda e: e.tensor_tensor(var_sb, mean_sb, mean_sb, ALU.mult), reads=[r_st], writes=[r_st])
            cx.add("dve", (lambda e: e.tensor_tensor(var_sb, qb_, var_sb, ALU.subtract)), reads=[qr_, r_st], writes=[r_st])
            cx.add("act", lambda e: e.activation(var_sb, var_sb, AF.Sqrt, bias=epsc, scale=1.0), reads=[r_st, r_eps], writes=[r_st])
            cx.add("dve", lambda e: e.reciprocal(rs_sb, var_sb), reads=[r_st], writes=[r_st])
            for c in range(4):
                t_ = tctr[0] % 2
                tctr[0] += 1
                cx.add("dve", (lambda e: e.tensor_tensor(tt[t_], yb[s_][:, c, :], mean_sb, ALU.subtract)), reads=[r_yb[s_], r_st], writes=[r_tt[t_]])
                cx.add("dve", (lambda e: e.tensor_tensor(tt[t_], tt[t_], rs_sb, ALU.mult)), reads=[r_tt[t_], r_st], writes=[r_tt[t_]])
                cx.add("act", (lambda e: e.activation(convT[:, c, nb * 512:(nb + 1) * 512], tt[t_], AF.Silu, bias=lnb[:, c:c + 1], scale=lng[:, c:c + 1])),
                       reads=[r_tt[t_], r_cprm], writes=[r_convT])

        conv_mm(0)
        conv_evac(0)
        conv_mm(1)
        conv_evac(1)
        ln_stage(0)
        conv_mm(2)
        conv_evac(2)
        ln_stage(1)
        ln_stage(2)
        conv_mm(3)
        conv_evac(3)
        ln_stage(3)
        cx.emit_phase()
        if stop_after == 3:
            if dbg:
                dump([convT[:, 0, :], convT[:, 1, :], convT[:, 2, :], convT[:, 3, :]])
                cx.final_wait_all_dma(["dbg"])
            return nc

        o = P2_BASE
        hbuf = A(o, F32, 16 * 1024).rearrange("p (n d) -> p n d", n=16); o += 64 * KB
        r_h = [cx.res("h%d" % i) for i in range(16)]
        XKT = A(o, BF16, 8 * 256).rearrange("p (c m) -> p c m", c=8); o += 4 * KB
        XV = A(o, BF16, 2 * 1024).rearrange("p (t n) -> p t n", t=2); o += 4 * KB
        r_XKT = cx.res("XKT")
        r_XV = cx.res("XV")
        assert o <= ARENA_BYTES
        P6_BASE = 4 * KB
        o = 52 * KB
        xt4 = [A(o + i * 4 * KB, F32, 1024) for i in range(2)]; o += 8 * KB
        r_xt4 = [cx.res("xt4_%d" % i) for i in range(2)]
        mnb = A(o, BF16, 1024); o += 2 * KB
        r_mnb = cx.res("mnb")
        mnT = A(o, BF16, 8 * 256).rearrange("p (k m) -> p k m", k=8); o += 4 * KB
        r_mnT = cx.res("mnT")
        junk4 = A(o, BF16, 1024); o += 2 * KB
        r_junk4 = cx.res("junk4")
        assert o <= 68 * KB, o
        o = 20 * KB
        grep4 = A(o, F32, 1024); o += 4 * KB
        r_grep4 = cx.res("grep4")
        st4 = A(o, F32, 12).rearrange("p (a j) -> p a j", a=3); o += 64
        r_st4 = cx.res("st4")
        wxqb = A(84 * KB, BF16, 8 * 1024).rearrange("p (k n) -> p k n", k=8)
        r_wxq = [cx.res("wxq")] * 4
        wload(wxqb, r_wxq, w_xq.rearrange("(k p) n -> p k n", p=128), "wxq")

        cx.add("sp", lambda e: e.dma_start(out=grep4, in_=gvec[2:3, :].partition_broadcast(128)), writes=[r_grep4], dma="grep")
        for i in range(16):
            s_ = i % 2
            cx.add("sp", (lambda e, i=i, s_=s_: e.dma_start(out=xt4[s_], in_=xe_t[8 + i])), writes=[r_xt4[s_]], dma="xt4_%d" % s_)
            for half in range(2):
                pb, pr = bank()
                for k in range(8):
                    lhsT = attT[:, k, i * 128:(i + 1) * 128] if k < 4 else convT[:, k - 4, i * 128:(i + 1) * 128]
                    cx.add("pe", (lambda e, pb=pb, lhsT=lhsT, k=k, half=half: e.matmul(pb, lhsT, woutb[:, k, half * 512:(half + 1) * 512],
                                                                                         start=(k == 0), stop=(k == 7))),
                           reads=[r_attT, r_convT] + r_wout, writes=[pr])
                cx.add("dve", (lambda e, pb=pb, i=i, half=half, s_=s_: e.tensor_tensor(hbuf[:, i, half * 512:(half + 1) * 512], pb,
                                                                                         xt4[s_][:, half * 512:(half + 1) * 512], ALU.add)),
                       reads=[pr, r_xt4[s_]], writes=[r_h[i]])
        mem_t = mem.rearrange("(n p) d -> n p d", p=128)
        for i in range(2):
            cx.add("sp", (lambda e, i=i: e.dma_start(out=xt4[i], in_=mem_t[i])), writes=[r_xt4[i]], dma="xt4_%d" % i)
        rms_stats([xt4[0], xt4[1]], [r_xt4[0], r_xt4[1]], st4[:, 0, :], st4[:, 1, :], st4[:, 2, :], r_st4, junk4, r_junk4)
        for i in range(2):
            cx.add("dve", (lambda e, i=i: e.scalar_tensor_tensor(mnb, xt4[i], st4[:, 2, i:i + 1], grep4, ALU.mult, ALU.mult)),
                   reads=[r_xt4[i], r_st4, r_grep4], writes=[r_mnb])
            transpose_tile(mnb, r_mnb, mnT, r_mnT, i * 128, "act")
        ev = [0]

        def evac_copy(dst, src, sres, dres):
            ev[0] += 1
            if ev[0] % 2:
                cx.add("act", (lambda e: e.copy(dst, src)), reads=[sres], writes=[dres])
            else:
                cx.add("dve", (lambda e: e.tensor_copy(dst, src)), reads=[sres], writes=[dres])

        for c in range(8):
            pb, pr = bank()
            for k in range(8):
                cx.add("pe", (lambda e, pb=pb, c=c, k=k: e.matmul(pb[:, 0:256], wxkb[:, k, c * 128:(c + 1) * 128], mnT[:, k, :],
                                                                  start=(k == 0), stop=(k == 7))),
                       reads=r_wxk + [r_mnT], writes=[pr])
            evac_copy(XKT[:, c, :], pb[:, 0:256], pr, r_XKT)
        for t_ in range(2):
            for half in range(2):
                pb, pr = bank()
                for k in range(8):
                    cx.add("pe", (lambda e, pb=pb, t_=t_, half=half, k=k: e.matmul(pb, mnT[:, k, t_ * 128:(t_ + 1) * 128],
                                                                                   wxvb[:, k, half * 512:(half + 1) * 512],
                                                                                   start=(k == 0), stop=(k == 7))),
                           reads=r_wxv + [r_mnT], writes=[pr])
                evac_copy(XV[:, t_, half * 512:(half + 1) * 512], pb, pr, r_XV)
        cx.emit_phase()
        if stop_after == 4:
            if dbg:
                dump([hbuf[:, i, :] for i in range(16)])
                cx.final_wait_all_dma(["dbg"])
            return nc

        W0_BASE = 100 * KB
        W1_BASE = P6_BASE + 32 * KB
        wup = [A(W0_BASE, BF16, 8 * 1024).rearrange("p (k n) -> p k n", k=8), A(W1_BASE, BF16, 8 * 1024).rearrange("p (k n) -> p k n", k=8)]
        wdn = [A(W0_BASE + 16 * KB, BF16, 8 * 1024).rearrange("p (k n) -> p k n", k=8),
               A(W1_BASE + 16 * KB, BF16, 8 * 1024).rearrange("p (k n) -> p k n", k=8)]
        r_wup = [[cx.res("wup%d" % i)] for i in range(2)]
        r_wdn = [[cx.res("wdn%d" % i)] for i in range(2)]
        wup_src = w_up.rearrange("(k p) f -> p k f", p=128)
        wdn_src = w_down.rearrange("(f p) n -> p f n", p=128)

        def load_group(fg):
            s_ = fg % 2
            cx.add("pool", (lambda e: e.dma_start(out=wup[s_], in_=wup_src[:, :, fg * 1024:(fg + 1) * 1024])),
                   writes=[r_wup[s_][0]], dma="wup%d" % s_)
            cx.add("pool", (lambda e: e.dma_start(out=wdn[s_], in_=wdn_src[:, fg * 8:fg * 8 + 8, :])),
                   writes=[r_wdn[s_][0]], dma="wdn%d" % s_)

        o = 4 * KB
        wxob = A(o, BF16, 8 * 1024).rearrange("p (k n) -> p k n", k=8); o += 16 * KB
        r_wxo = [cx.res("wxo")] * 4
        hnT5 = [A(o + i * 8 * KB, BF16, 8 * 512).rearrange("p (k t) -> p k t", k=8) for i in range(2)]; o += 16 * KB
        r_hnT5 = [cx.res("hnT5_%d" % i) for i in range(2)]
        xqT = A(o, BF16, 8 * 512).rearrange("p (k t) -> p k t", k=8); o += 8 * KB
        r_xqT = [cx.res("xqT%d" % i) for i in range(4)]
        xoT2 = [A(o + i * 8 * KB, BF16, 8 * 512).rearrange("p (k t) -> p k t", k=8) for i in range(2)]; o += 16 * KB
        r_xoT2 = [[cx.res("xoT%d_%d" % (j, i)) for i in range(4)] for j in range(2)]
        px = [A(o + i * KB, BF16, 512) for i in range(4)]; o += 4 * KB
        r_px = [cx.res("px%d" % i) for i in range(4)]
        rd5 = [A(o + i * 2 * KB, F32, 512) for i in range(2)]; o += 4 * KB
        r_rd5 = [cx.res("rd5_%d" % i) for i in range(2)]
        hn5 = [A(o + i * 2 * KB, BF16, 1024) for i in range(2)]; o += 4 * KB
        r_hn5 = [cx.res("hn5_%d" % i) for i in range(2)]
        grep5 = A(o, F32, 1024); o += 4 * KB
        r_grep5 = cx.res("grep5")
        junk5 = A(o, BF16, 1024); o += 2 * KB
        r_junk5 = cx.res("junk5")
        st5 = A(o, F32, 48).rearrange("p (a b j) -> p a b j", a=3, b=4); o += 192
        r_st5 = [cx.res("st5_%d" % i) for i in range(4)]
        assert o <= 84 * KB, o

        wload(wxob, r_wxo, w_xo.rearrange("(k p) n -> p k n", p=128), "wxo")
        load_group(0)
        cx.add("sp", lambda e: e.dma_start(out=grep5, in_=gvec[1:2, :].partition_broadcast(128)), writes=[r_grep5], dma="grep")

        def norm_h_stats(nb, st, r_st, junk_, r_junk_):
            tiles = [nb * 4 + i for i in range(4)]
            rms_stats([hbuf[:, t_, :] for t_ in tiles], [r_h[t_] for t_ in tiles], st[:, 0, nb, :], st[:, 1, nb, :], st[:, 2, nb, :],
                      r_st[nb], junk_, r_junk_)

        def norm_h_apply(nb, g_ap, r_g, dstT, r_dstT, col_base, hn, r_hn, st, r_st):
            tiles = [nb * 4 + i for i in range(4)]

            def nrm(i):
                t_, s_ = tiles[i], i % 2
                cx.add("dve", (lambda e: e.scalar_tensor_tensor(hn[s_], hbuf[:, t_, :], st[:, 2, nb, i:i + 1], g_ap, ALU.mult, ALU.mult)),
                       reads=[r_h[t_], r_st[nb], r_g], writes=[r_hn[s_]])

            nrm(0)
            for i in range(4):
                if i + 1 < 4:
                    nrm(i + 1)
                transpose_tile(hn[i % 2], r_hn[i % 2], dstT, r_dstT, col_base + i * 128, "act")

        pxc = [0]
        rdc = [0]

        def xq_stage(nb):
            bs = nb % 2
            for c in range(8):
                pb, pr = bank()
                for k in range(8):
                    st_, sp_ = (k == 0), (k == 7)
                    cx.add("pe", (lambda e: e.matmul(pb, wxqb[:, k, c * 128:(c + 1) * 128], hnT5[bs][:, k, :], start=st_, stop=sp_)),
                           reads=r_wxq + [r_hnT5[bs]], writes=[pr])
                evac_copy(xqT[:, c, :], pb, pr, r_xqT[c // 2])

        def o_part(nb, i):
            xs = nb % 2
            t_ = nb * 4 + i
            for half in range(2):
                pb, pr = bank()
                for k in range(8):
                    st_, sp_ = (k == 0), (k == 7)
                    cx.add("pe", (lambda e: e.matmul(pb, xoT2[xs][:, k, i * 128:(i + 1) * 128], wxob[:, k, half * 512:(half + 1) * 512],
                                                     start=st_, stop=sp_)),
                           reads=r_xoT2[xs] + r_wxo, writes=[pr])
                hv = hbuf[:, t_, half * 512:(half + 1) * 512]
                cx.add("dve", (lambda e: e.tensor_tensor(hv, hv, pb, ALU.add)), reads=[pr, r_h[t_]], writes=[r_h[t_]])

        def head_scores(hx):
            pxs = []
            for kt in range(2):
                sb, sr = bank()
                for cc in range(2):
                    st_, sp_ = (cc == 0), (cc == 1)
                    cx.add("pe", (lambda e: e.matmul(sb, XKT[:, 2 * hx + cc, kt * 128:(kt + 1) * 128], xqT[:, 2 * hx + cc, :], start=st_, stop=sp_)),
                           reads=[r_XKT, r_xqT[hx]], writes=[sr])
                p_ = pxc[0] % 4
                pxc[0] += 1
                cx.add("act", (lambda e: e.activation(px[p_], sb, AF.Exp, scale=1.0 / 16.0)), reads=[sr], writes=[r_px[p_]])
                pxs.append(p_)
            return pxs

        def head_pv(nb, hx, pxs):
            xs = nb % 2
            db, dr = bank()
            for kt in range(2):
                st_, sp_ = (kt == 0), (kt == 1)
                p_ = pxs[kt]
                cx.add("pe", (lambda e: e.matmul(db, ones, px[p_], start=st_, stop=sp_)), reads=[r_px[p_], r_ones], writes=[dr])
            rd_ = rdc[0] % 2
            rdc[0] += 1
            cx.add("act", (lambda e: e.activation(rd5[rd_], db, AF.Ln)), reads=[dr], writes=[r_rd5[rd_]])
            cx.add("act", (lambda e: e.activation(rd5[rd_], rd5[rd_], AF.Exp, scale=-1.0)), reads=[r_rd5[rd_]], writes=[r_rd5[rd_]])
            for cc in range(2):
                ob, orr = bank()
                for kt in range(2):
                    st_, sp_ = (kt == 0), (kt == 1)
                    p_ = pxs[kt]
                    cx.add("pe", (lambda e: e.matmul(ob, XV[:, kt, (2 * hx + cc) * 128:(2 * hx + cc + 1) * 128], px[p_], start=st_, stop=sp_)),
                           reads=[r_px[p_], r_XV], writes=[orr])
                cx.add("dve", (lambda e: e.tensor_tensor(xoT2[xs][:, 2 * hx + cc, :], ob, rd5[rd_], ALU.mult)),
                       reads=[orr, r_rd5[rd_]], writes=[r_xoT2[xs][hx]])

        norm_h_stats(0, st5, r_st5, junk5, r_junk5)
        norm_h_apply(0, grep5, r_grep5, hnT5[0], r_hnT5[0], 0, hn5, r_hn5, st5, r_st5)
        norm_h_stats(1, st5, r_st5, junk5, r_junk5)
        for nb in range(4):
            xq_stage(nb)
            if nb + 1 < 4:
                norm_h_apply(nb + 1, grep5, r_grep5, hnT5[(nb + 1) % 2], r_hnT5[(nb + 1) % 2], 0, hn5, r_hn5, st5, r_st5)
                if nb + 2 < 4:
                    norm_h_stats(nb + 2, st5, r_st5, junk5, r_junk5)
            for hx in range(4):
                pxs = head_scores(hx)
                if nb > 0:
                    o_part(nb - 1, hx)
                head_pv(nb, hx, pxs)
        for i in range(4):
            o_part(3, i)
        cx.emit_phase()
        if stop_after == 5:
            if dbg:
                dump([hbuf[:, i, :] for i in range(16)])
                cx.final_wait_all_dma(["dbg"])
            return nc

        o = P6_BASE
        hnT6 = A(o, BF16, 8 * T).rearrange("p (k t) -> p k t", k=8); o += 32 * KB
        r_hnT6 = [cx.res("hnT6_%d" % i) for i in range(4)]
        o += 32 * KB
        HT = [A(o + i * 8 * KB, BF16, 8 * 512).rearrange("p (f t) -> p f t", f=8) for i in range(2)]; o += 16 * KB
        r_HT = [cx.res("HT%d" % i) for i in range(2)]
        rt = [A(o + i * 2 * KB, F32, 512) for i in range(2)]; o += 4 * KB
        r_rt = [cx.res("rt%d" % i) for i in range(2)]
        ostg = [A(o + i * 4 * KB, F32, 1024) for i in range(2)]; o += 8 * KB
        r_ostg = [cx.res("ostg%d" % i) for i in range(2)]
        hn6 = [A(o + i * 2 * KB, BF16, 1024) for i in range(2)]; o += 4 * KB
        r_hn6 = [cx.res("hn6_%d" % i) for i in range(2)]
        assert o <= W0_BASE
        o = W0_BASE + 32 * KB
        st6 = A(o, F32, 96).rearrange("p (z a b j) -> p z a b j", z=2, a=3, b=4); o += 384
        r_st6 = [[cx.res("st6_%d_%d" % (z, i)) for i in range(4)] for z in range(2)]
        assert o <= P2_BASE, o
        o = P2_BASE + 64 * KB
        grep6 = A(o, F32, 1024); o += 4 * KB
        grepf = A(o, F32, 1024); o += 4 * KB
        r_grep6 = cx.res("grep6")
        junk6 = A(o, BF16, 1024); o += 2 * KB
        r_junk6 = cx.res("junk6")
        assert o <= ARENA_BYTES, o

        load_group(1)
        cx.add("sp", lambda e: e.dma_start(out=grep6, in_=gvec[3:4, :].partition_broadcast(128)), writes=[r_grep6], dma="grep")
        r_grepf = cx.res("grepf")
        cx.add("sp", lambda e: e.dma_start(out=grepf, in_=gvec[4:5, :].partition_broadcast(128)), writes=[r_grepf], dma="grepf")
        htc = [0]
        rtc = [0]
        out_t = out.rearrange("(n p) d -> n p d", p=128)

        def up_stage(fg, nb, hs):
            s_ = fg % 2
            for f in range(8):
                pb, pr = bank()
                for k in range(8):
                    st_, sp_ = (k == 0), (k == 7)
                    cx.add("pe", (lambda e: e.matmul(pb, wup[s_][:, k, f * 128:(f + 1) * 128], hnT6[:, k, nb * 512:(nb + 1) * 512],
                                                     start=st_, stop=sp_)),
                           reads=r_wup[s_] + [r_hnT6[nb]], writes=[pr])
                r_ = rtc[0] % 2
                rtc[0] += 1
                cx.add("act", (lambda e: e.activation(rt[r_], pb, AF.Relu)), reads=[pr], writes=[r_rt[r_]])
                cx.add("pool", (lambda e: e.tensor_tensor(HT[hs][:, f, :], rt[r_], rt[r_], ALU.mult)), reads=[r_rt[r_]], writes=[r_HT[hs]])

        def down_stage(fg, nb, hs):
            s_ = fg % 2
            for i in range(4):
                t_ = nb * 4 + i
                for half in range(2):
                    pb, pr = bank()
                    for f in range(8):
                        st_, sp_ = (f == 0), (f == 7)
                        cx.add("pe", (lambda e: e.matmul(pb, HT[hs][:, f, i * 128:(i + 1) * 128], wdn[s_][:, f, half * 512:(half + 1) * 512],
                                                         start=st_, stop=sp_)),
                               reads=[r_HT[hs]] + r_wdn[s_], writes=[pr])
                    hv = hbuf[:, t_, half * 512:(half + 1) * 512]
                    cx.add("dve", (lambda e: e.tensor_tensor(hv, hv, pb, ALU.add)), reads=[pr, r_h[t_]], writes=[r_h[t_]])
            if fg == 3:
                tiles = [nb * 4 + i for i in range(4)]
                rms_stats([hbuf[:, t_, :] for t_ in tiles], [r_h[t_] for t_ in tiles], st6[:, 1, 0, nb, :], st6[:, 1, 1, nb, :],
                          st6[:, 1, 2, nb, :], r_st6[1][nb], junk6, r_junk6)
                for i, t_ in enumerate(tiles):
                    os_ = i % 2
                    cx.add("dve", (lambda e: e.scalar_tensor_tensor(ostg[os_], hbuf[:, t_, :], st6[:, 1, 2, nb, i:i + 1], grepf,
                                                                    ALU.mult, ALU.mult)),
                           reads=[r_h[t_], r_st6[1][nb], r_grepf], writes=[r_ostg[os_]])
                    cx.add("sp", (lambda e: e.dma_start(out=out_t[t_], in_=ostg[os_])), reads=[r_ostg[os_]], dma="out%d" % os_)
            if nb == 3 and fg + 2 < 4:
                load_group(fg + 2)

        items = [(fg, nb) for fg in range(4) for nb in range(4)]

        def norm6(nb):
            norm_h_apply(nb, grep6, r_grep6, hnT6, r_hnT6[nb], nb * 512, hn6, r_hn6, st6[:, 0], r_st6[0])
            if nb + 1 < 4:
                norm_h_stats(nb + 1, st6[:, 0], r_st6[0], junk6, r_junk6)

        norm_h_stats(0, st6[:, 0], r_st6[0], junk6, r_junk6)
        norm6(0)
        up_stage(0, 0, 0)
        for j in range(len(items)):
            if j + 1 < len(items):
                fg1, nb1 = items[j + 1]
                if fg1 == 0:
                    norm6(nb1)
                up_stage(fg1, nb1, (j + 1) % 2)
            down_stage(items[j][0], items[j][1], j % 2)
        cx.emit_phase()
        cx.final_wait_all_dma(["out0", "out1"])
    return nc


def _host_constants():
    ident = np.eye(128, dtype=np.float32)
    rm = np.zeros((128, 128), np.float32)
    for e in range(2):
        for i in range(8):
            rm[64 * e + 8 + i, 64 * e + i] = -1.0
            rm[64 * e + i, 64 * e + 8 + i] = 1.0
    onesm = np.full((128, 128), 1.0 / 512.0, np.float32)
    p = np.arange(128)[:, None]
    j = np.arange(128)[None, :]
    mask2 = np.concatenate([np.tile((p >= j), (1, 4)), np.tile((p <= j), (1, 4))], axis=1).astype(np.float32)
    return np.concatenate([ident, rm, onesm, mask2], axis=1)


def _tables(t0):
    pos = np.arange(t0 - HALO, t0 - HALO + E).astype(np.float32)
    pos = np.where((pos >= 0) & (pos < S), pos, np.float32(0.0)).astype(np.float32)
    rot = 16
    freqs = (np.float32(500000.0) ** (-np.arange(0, rot, 2, dtype=np.float32) / np.float32(rot))).astype(np.float32)
    ang = (pos[:, None] * freqs[None, :]).astype(np.float32)
    cos = np.cos(ang).astype(np.float32)
    sin = np.sin(ang).astype(np.float32)
    tc = np.ones((128, E), np.float32)
    ts = np.zeros((128, E), np.float32)
    for e in range(2):
        for i in range(16):
            tc[64 * e + i] = cos[:, i % 8]
            ts[64 * e + i] = sin[:, i % 8]
    return tc, ts


def _kbias(t0):
    kb = np.zeros((128, 72), np.float32)
    col = 0
    for d in PATTERNS:
        for r in range(d):
            for m in range(NKT[d]):
                start = t0 + r + d * (128 * m - 64)
                pos = start + d * np.arange(128)
                kb[:, col] = np.where((pos >= 0) & (pos < S), 0.0, -30000.0)
                col += 1
    return kb


_NC_CACHE = {}


def _get_nc(stop_after=99, dbg=False):
    key = (stop_after, dbg)
    if key not in _NC_CACHE:
        _NC_CACHE[key] = build_program(stop_after, dbg)
    return _NC_CACHE[key]


def make_in_maps(x, mem, norm_mix_g, w_in, conv_w, conv_b, conv_ln_g, conv_ln_b, w_out,
                 norm_x_g, norm_mem_g, w_xq, w_xk, w_xv, w_xo, norm_mlp_g, w_up, w_down, norm_final_g):
    f = lambda a: np.ascontiguousarray(np.asarray(a, dtype=np.float32))
    x = f(x)
    mem = f(mem)
    cst = _host_constants()
    gvec = np.stack([f(norm_mix_g)[0], f(norm_x_g)[0], f(norm_mem_g)[0], f(norm_mlp_g)[0], f(norm_final_g)], axis=0)
    cw = f(conv_w)[0]
    cwT = cw.T.reshape(4, 128, 31).transpose(1, 0, 2).reshape(128, 124)
    lay = lambda v: f(v)[0].reshape(4, 128).T
    convp = np.ascontiguousarray(np.concatenate([cwT, lay(conv_b), lay(conv_ln_g), lay(conv_ln_b)], axis=1))
    shared = {
        "cst": cst, "gvec": np.ascontiguousarray(gvec), "convp": convp,
        "w_in": f(w_in)[0], "w_out": f(w_out)[0], "w_xq": f(w_xq)[0], "w_xk": f(w_xk)[0], "w_xv": f(w_xv)[0],
        "w_xo": f(w_xo)[0], "w_up": f(w_up)[0], "w_down": f(w_down)[0],
    }
    in_maps = []
    for c in range(NCORES):
        b, t0 = c // 4, (c % 4) * T
        xe = np.zeros((E, D), np.float32)
        lo, hi = t0 - HALO, t0 + T + HALO
        slo, shi = max(lo, 0), min(hi, S)
        xe[slo - lo: shi - lo] = x[b, slo:shi]
        tc, ts = _tables(t0)
        m = dict(shared)
        kb = _kbias(t0)
        vr = np.ascontiguousarray(np.repeat((kb[:, :69] == 0).astype(np.float32), 64, axis=1))
        m.update({"x_ext": xe, "mem": mem[b], "tabC": tc, "tabS": ts, "kbias": kb, "vrep": vr})
        in_maps.append(m)
    return in_maps


def kernel(**inputs):
    in_maps = make_in_maps(**inputs)
    nc = _get_nc()
    res = run_bass_kernel_spmd(nc, in_maps, core_ids=list(range(NCORES)))
    outp = np.zeros((2, S, D), np.float32)
    for c in range(NCORES):
        b, t0 = c // 4, (c % 4) * T
        outp[b, t0:t0 + T] = res.results[c]["out"]
    return outp
```

```python
import numpy as np
import concourse.bass as bass
import concourse.mybir as mybir
from concourse.bass_utils import run_bass_kernel_spmd
from contextlib import ExitStack
import types

F32 = mybir.dt.float32
BF16 = mybir.dt.bfloat16
AF = mybir.ActivationFunctionType
ALU = mybir.AluOpType
AX = mybir.AxisListType

S = 8192
D = 1024
T = 2048
HALO = 1024
E = T + 2 * HALO
NCORES = 8
DIN = 2560
DFF = 4096
NMEM = 256
EPS = 1e-6
PATTERNS = (1, 4, 16)
NKT = {1: 17, 4: 5, 16: 2}
ARENA_BYTES = 207 * 1024

ENGS = ("sp", "act", "pool", "dve", "pe")
import os as _os
TRACE = open(_os.environ['K_TRACE'], 'w') if _os.environ.get('K_TRACE') else None


class Res:
    __slots__ = ("name", "w", "rs")

    def __init__(self, name=""):
        self.name = name
        self.w = None
        self.rs = []


class Op:
    __slots__ = ("eng", "fn", "reads", "writes", "dma", "dsem", "dcount", "deps", "signal", "sigval", "waits")

    def __init__(self, eng, fn, reads, writes, dma):
        self.eng = eng
        self.fn = fn
        self.reads = reads
        self.writes = writes
        self.dma = dma
        self.dsem = None
        self.dcount = 0
        self.deps = []
        self.signal = False
        self.sigval = None
        self.waits = []


class Ctx:
    def __init__(self, nc, stack):
        self.nc = nc
        self.stack = stack
        self.esem = {e: stack.enter_context(nc.semaphore("s_" + e)) for e in ENGS}
        self.ecount = {e: 0 for e in ENGS}
        self.dsems = {}
        self.dcounts = {}
        self.waited = {e: {} for e in ENGS}
        self.all_res = []
        self.ops = []

    def res(self, name=""):
        r = Res(name)
        self.all_res.append(r)
        return r

    def dsem(self, key):
        if key not in self.dsems:
            self.dsems[key] = self.stack.enter_context(self.nc.semaphore("d_" + key))
            self.dcounts[key] = 0
        return self.dsems[key]

    def add(self, eng, fn, reads=(), writes=(), dma=None):
        if fn.__closure__:
            cells = []
            for c in fn.__closure__:
                try:
                    cells.append(types.CellType(c.cell_contents))
                except ValueError:
                    cells.append(c)
            f2 = types.FunctionType(fn.__code__, fn.__globals__, fn.__name__, fn.__defaults__, tuple(cells))
            f2.__kwdefaults__ = fn.__kwdefaults__
            fn = f2
        op = Op(eng, fn, tuple(reads), tuple(writes), dma)
        deps = []
        for r in op.reads:
            if r.w is not None:
                deps.append((r.w, True))
        for r in op.writes:
            if r.w is not None:
                deps.append((r.w, False))
            for o in r.rs:
                deps.append((o, False))
        for r in op.reads:
            if r not in op.writes:
                r.rs.append(op)
        for r in op.writes:
            r.w = op
            r.rs = []
        if dma is not None:
            self.dsem(dma)
            self.dcounts[dma] += 16
            op.dsem = dma
            op.dcount = self.dcounts[dma]
        seen = {}
        for d, raw in deps:
            if d is op:
                continue
            k = id(d)
            if k in seen:
                seen[k] = (d, seen[k][1] or raw)
            else:
                seen[k] = (d, raw)
        for d, raw in seen.values():
            need = False
            if d.dma is not None:
                need = True
            elif d.eng != eng:
                need = True
            elif eng != "pe":
                need = True
            if need:
                op.waits.append(d)
                if d.dma is None:
                    d.signal = True
        self.ops.append(op)
        return op

    def emit_phase(self):
        nc = self.nc
        ops = self.ops
        self.ops = []
        per = {e: [o for o in ops if o.eng == e] for e in ENGS}
        last = {}
        for e in ENGS:
            for o in reversed(per[e]):
                if o.dma is None:
                    o.signal = True
                    last[e] = o
                    break
        for e in ENGS:
            for o in per[e]:
                if o.dma is None and o.signal:
                    self.ecount[e] += 1
                    o.sigval = self.ecount[e]
        final = {e: self.ecount[e] for e in ENGS if e in last}

        def run(e, engine):
            waited = self.waited[e]
            for o in per[e]:
                need = {}
                for d in o.waits:
                    if d.dma is not None:
                        key = ("d", d.dsem)
                        val = d.dcount
                    else:
                        key = ("e", d.eng)
                        val = d.sigval
                    if val > need.get(key, 0):
                        need[key] = val
                for key, val in need.items():
                    if waited.get(key, 0) >= val:
                        continue
                    sem = self.dsems[key[1]] if key[0] == "d" else self.esem[key[1]]
                    engine.wait_ge(sem, val)
                    waited[key] = val
                    if TRACE is not None:
                        TRACE.write("%s WAIT %s >= %d\n" % (e, key, val))
                ins = o.fn(engine)
                if TRACE is not None:
                    TRACE.write("%s OP %s sig=%s dma=%s\n" % (e, getattr(o.fn, '__qualname__', '?') + ':' + str(o.fn.__code__.co_firstlineno), o.sigval, o.dsem))
                if o.dma is not None:
                    ins.then_inc(self.dsems[o.dsem], 16)
                elif o.signal:
                    ins.then_inc(self.esem[e], 1)
            for f, val in final.items():
                if f == e:
                    continue
                key = ("e", f)
                if waited.get(key, 0) >= val:
                    continue
                engine.wait_ge(self.esem[f], val)
                waited[key] = val

        with nc.Block() as block:
            @block.sync
            def _(eng):
                run("sp", eng)

            @block.scalar
            def _(eng):
                run("act", eng)

            @block.gpsimd
            def _(eng):
                run("pool", eng)

            @block.vector
            def _(eng):
                run("dve", eng)

            @block.tensor
            def _(eng):
                run("pe", eng)
        for r in self.all_res:
            if r.w is not None and r.w.dma is None:
                r.w = None
            r.rs = [o for o in r.rs if o.dma is not None]

    def final_wait_all_dma(self, keys):
        nc = self.nc
        with nc.Block() as block:
            @block.sync
            def _(eng):
                for k in keys:
                    eng.wait_ge(self.dsems[k], self.dcounts[k])


def build_program(stop_after=99, dbg=False):
    nc = bass.Bass("TRN2", target_bir_lowering=False)

    def din(name, shape, dt=F32):
        return nc.dram_tensor(name, list(shape), dt, kind="ExternalInput").ap()

    x_ext = din("x_ext", [E, D])
    mem = din("mem", [NMEM, D])
    tabC = din("tabC", [128, E])
    tabS = din("tabS", [128, E])
    cst = din("cst", [128, 128 * 3 + 1024])
    kbias_d = din("kbias", [128, 72])
    vrep_d = din("vrep", [128, 69 * 64])
    gvec = din("gvec", [5, D])
    convp = din("convp", [128, 4 * 31 + 12])
    w_in = din("w_in", [D, DIN])
    w_out = din("w_out", [D, D])
    w_xq = din("w_xq", [D, D])
    w_xk = din("w_xk", [D, D])
    w_xv = din("w_xv", [D, D])
    w_xo = din("w_xo", [D, D])
    w_up = din("w_up", [D, DFF])
    w_down = din("w_down", [DFF, D])
    out = nc.dram_tensor("out", [T, D], F32, kind="ExternalOutput").ap()
    dbg_out = None
    if dbg:
        dbg_out = nc.dram_tensor("dbg", [128, 20 * 1024], F32, kind="ExternalOutput").ap()

    stack = ExitStack()
    with stack:
        arena = stack.enter_context(nc.sbuf_tensor("arena", [128, ARENA_BYTES // 2], BF16))
        psf = [stack.enter_context(nc.psum_tensor("psf%d" % i, [128, 512], F32)) for i in range(7)]
        psb_h = stack.enter_context(nc.psum_tensor("psb", [128, 1024], BF16))
        psb = psb_h[:, :]
        psf = [p[:, :] for p in psf]
        cx = Ctx(nc, stack)

        def A(off, dt, n):
            assert off % 4 == 0
            if dt == BF16:
                assert off + 2 * n <= ARENA_BYTES, (off, n)
                return arena[:, off // 2: off // 2 + n]
            assert off + 4 * n <= ARENA_BYTES, (off, n)
            return arena[:, off // 2: off // 2 + 2 * n].bitcast(F32)

        KB = 1024
        psum_res = [cx.res("psf%d" % i) for i in range(7)]
        psb_res = cx.res("psb")
        bank_rr = [0]

        def bank():
            i = bank_rr[0] % 6
            bank_rr[0] += 1
            return psf[i], psum_res[i]

        psb2 = psf[6].bitcast(BF16)
        trbufs = [(psb, psb_res), (psb2, psum_res[6])]
        tr_ctr = [0]

        o = 0
        ident = A(o, BF16, 128); o += 256
        Rm = A(o, BF16, 128); o += 256
        onesm = A(o, BF16, 128); o += 256
        mask2 = A(o, BF16, 1024); o += 2048
        ones = A(o, BF16, 128); o += 256
        kbias = A(o, F32, 72); o += 288
        epsc = A(o, F32, 1); o += 4
        cprm = A(o, F32, 4 * 31 + 12); o += 544
        cw = cprm[:, 0:124].rearrange("p (c k) -> p c k", c=4)
        cb = cprm[:, 124:128]
        lng = cprm[:, 128:132]
        lnb = cprm[:, 132:136]
        r_cprm = cx.res("cprm")
        o = 4 * KB
        SMALL_END = o
        r_const = cx.res("const")

        cx.add("pool", lambda e: e.dma_start(out=arena[:, 0:1408], in_=cst), writes=[r_const], dma="cst")
        cx.add("sp", lambda e: e.dma_start(out=cprm, in_=convp), writes=[r_cprm], dma="cprm")
        r_ones = cx.res("ones")
        r_eps = cx.res("eps")
        cx.add("dve", lambda e: e.memset(ones, 1.0), writes=[r_ones])
        cx.add("dve", lambda e: e.memset(epsc, EPS), writes=[r_eps])

        def wload(dst, dst_res, src, key, nsplit=1):
            cx.add("pool", (lambda e: e.dma_start(out=dst, in_=src)), writes=list(dict.fromkeys(dst_res)), dma=key)

        def rms_stats(srcs, src_res, ss, sq, rstd, r_ss, junk, r_junk):
            n = len(srcs)
            for j, (sap, sr) in enumerate(zip(srcs, src_res)):
                cx.add("act", (lambda e, sap=sap, j=j: e.activation(junk, sap, AF.Square, scale=1.0 / 32.0,
                                                                   accum_out=ss[:, j:j + 1])),
                       reads=[sr], writes=[r_junk, r_ss])
            cx.add("act", lambda e: e.activation(sq[:, 0:n], ss[:, 0:n], AF.Sqrt, bias=epsc, scale=1.0),
                   reads=[r_ss, r_eps], writes=[r_ss])
            cx.add("dve", lambda e: e.reciprocal(rstd[:, 0:n], sq[:, 0:n]), reads=[r_ss], writes=[r_ss])

        def transpose_tile(xn_ap, r_xn, dstT, r_dstT, col0, evac_eng):
            tb, tres = trbufs[tr_ctr[0] % 2]
            tr_ctr[0] += 1
            for k in range(8):
                cx.add("pe", (lambda e, k=k: e.transpose(tb[:, k * 128:(k + 1) * 128], xn_ap[:, k * 128:(k + 1) * 128], ident)),
                       reads=[r_xn, r_const], writes=[tres])
            src = tb.rearrange("p (k t) -> p k t", k=8)
            dst = dstT[:, :, col0:col0 + 128]
            if evac_eng == "act":
                cx.add("act", lambda e: e.copy(dst, src), reads=[tres], writes=[r_dstT])
            else:
                cx.add("dve", lambda e: e.tensor_copy(dst, src), reads=[tres], writes=[r_dstT])

        o = SMALL_END
        QT = A(o, BF16, 4 * T).rearrange("p (c t) -> p c t", c=4); o += 16 * KB
        KT = A(o, BF16, 4 * E).rearrange("p (c t) -> p c t", c=4); o += 32 * KB
        VT = A(o, BF16, 4 * E).rearrange("p (c t) -> p c t", c=4); o += 32 * KB
        UW = T + 32
        uT = A(o, BF16, 4 * UW).rearrange("p (c t) -> p c t", c=4); o += 4 * UW * 2
        o = (o + 1023) // 1024 * 1024
        P1_BASE = o
        r_QT = [cx.res("QT%d" % b) for b in range(4)]
        r_KT = [cx.res("KT%d" % b) for b in range(8)]
        r_VT = [cx.res("VT%d" % b) for b in range(8)]
        r_uT = [cx.res("uT%d" % b) for b in range(6)]

        o = P1_BASE
        winb = A(o, BF16, 8 * DIN).rearrange("p (k n) -> p k n", k=8); o += 40 * KB
        xnT = [A(o + i * 8 * KB, BF16, 8 * 512).rearrange("p (k t) -> p k t", k=8) for i in range(2)]; o += 16 * KB
        r_xnT = [cx.res("xnT%d" % i) for i in range(2)]
        xt = [A(o + i * 4 * KB, F32, 1024) for i in range(4)]; o += 16 * KB
        r_xt = [cx.res("xt%d" % i) for i in range(4)]
        xn = [A(o + i * 2 * KB, BF16, 1024) for i in range(2)]; o += 4 * KB
        r_xn = [cx.res("xn%d" % i) for i in range(2)]
        tC = [A(o + i * 2 * KB, F32, 512) for i in range(2)]; o += 4 * KB
        tS = [A(o + i * 2 * KB, F32, 512) for i in range(2)]; o += 4 * KB
        r_tabC = [cx.res("tabC%d" % i) for i in range(2)]
        r_tabS = [cx.res("tabS%d" % i) for i in range(2)]
        qraw = [A(o + i * KB, BF16, 512) for i in range(2)]; o += 2 * KB
        r_qraw = [cx.res("qraw%d" % i) for i in range(2)]
        tmpA = [A(o + i * 2 * KB, F32, 512) for i in range(2)]; o += 4 * KB
        r_tmpA = [cx.res("tmpA%d" % i) for i in range(2)]
        tmpB = [A(o + i * 2 * KB, F32, 512) for i in range(2)]; o += 4 * KB
        r_tmpB = [cx.res("tmpB%d" % i) for i in range(2)]
        sg = [A(o + i * 2 * KB, F32, 512) for i in range(2)]; o += 4 * KB
        r_sg = [cx.res("sg%d" % i) for i in range(2)]
        grep = A(o, F32, 1024); o += 4 * KB
        r_grep = cx.res("grep")
        junk = A(o, BF16, 1024); o += 2 * KB
        r_junk = cx.res("junk")
        ss = A(o, F32, 32).rearrange("p (b j) -> p b j", b=8); o += 128
        sq = A(o, F32, 32).rearrange("p (b j) -> p b j", b=8); o += 128
        rstd = A(o, F32, 32).rearrange("p (b j) -> p b j", b=8); o += 128
        r_ss = [cx.res("ss%d" % b) for b in range(8)]
        assert o <= ARENA_BYTES, o

        w_in_v = w_in.rearrange("(k p) n -> p k n", p=128)
        r_win = [cx.res("win%d" % i) for i in range(5)]
        def load_win(g_, extra_reads=()):
            cx.add("pool", (lambda e: e.dma_start(out=winb[:, :, g_ * 512:(g_ + 1) * 512], in_=w_in_v[:, :, g_ * 512:(g_ + 1) * 512])),
                   reads=list(extra_reads), writes=[r_win[g_]], dma="win%d" % g_)

        load_win(1)
        load_win(2)
        cx.add("sp", lambda e: e.dma_start(out=grep, in_=gvec[0:1, :].partition_broadcast(128)),
               writes=[r_grep], dma="grep")

        xe_t = x_ext.rearrange("(n p) d -> n p d", p=128)
        tile_ctr = [0]
        ev_ctr = [0]

        def load_x_tile(n, dst, dres, key):
            cx.add("sp", (lambda e: e.dma_start(out=dst, in_=xe_t[n])), writes=[dres], dma=key)

        def norm_pre(b, half):
            idx = [2 * half, 2 * half + 1]
            for i in idx:
                load_x_tile(b * 4 + i, xt[i], r_xt[i], "xt%d" % i)
            rms_stats([xt[i] for i in idx], [r_xt[i] for i in idx],
                      ss[:, b, 2 * half:2 * half + 2], sq[:, b, 2 * half:2 * half + 2],
                      rstd[:, b, 2 * half:2 * half + 2], r_ss[b], junk, r_junk)
            for i in idx:
                cx.add("dve", (lambda e: e.scalar_tensor_tensor(xn[i % 2], xt[i], rstd[:, b, i:i + 1], grep, ALU.mult, ALU.mult)),
                       reads=[r_xt[i], r_ss[b], r_grep], writes=[r_xn[i % 2]])

        def norm_tr(b, half):
            bs = b % 2
            for i in (2 * half, 2 * half + 1):
                ev_ctr[0] += 1
                transpose_tile(xn[i % 2], r_xn[i % 2], xnT[bs], r_xnT[bs], i * 128, "act" if ev_ctr[0] % 2 else "dve")

        def proj_chunk(bs, c0, n0=0, n=512):
            pb, pr = bank()
            for k in range(8):
                cx.add("pe", (lambda e, k=k: e.matmul(pb[:, 0:n], winb[:, k, c0:c0 + 128], xnT[bs][:, k, n0:n0 + n],
                                                      start=(k == 0), stop=(k == 7))),
                       reads=[r_xnT[bs], r_win[c0 // 512]], writes=[pr])
            flush_rot()
            return pb, pr

        rot_ctr = [0]

        pending_rot = []

        def flush_rot():
            while pending_rot:
                pending_rot.pop(0)()

        def rotary_evac(pb, pr, ts_, dst, dres):
            s_ = rot_ctr[0] % 2
            rot_ctr[0] += 1
            cx.add("act", lambda e: e.copy(qraw[s_], pb), reads=[pr], writes=[r_qraw[s_]])

            def rest():
                qb, qr = bank()
                cx.add("pe", lambda e: e.matmul(qb, Rm, qraw[s_], start=True, stop=True), reads=[r_qraw[s_], r_const], writes=[qr])
                cx.add("dve", lambda e: e.tensor_tensor(tmpA[s_], qb, tS[ts_], ALU.mult), reads=[qr, r_tabS[ts_]], writes=[r_tmpA[s_]])
                cx.add("pool", lambda e: e.tensor_tensor(tmpB[s_], qraw[s_], tC[ts_], ALU.mult), reads=[r_qraw[s_], r_tabC[ts_]], writes=[r_tmpB[s_]])
                cx.add("dve", lambda e: e.tensor_tensor(dst, tmpA[s_], tmpB[s_], ALU.add), reads=[r_tmpA[s_], r_tmpB[s_]], writes=[dres])
            pending_rot.append(rest)

        glu_ctr = [0]
        vev = [0]

        def k_chunk(b, pr_):
            pb, pres = proj_chunk(b % 2, 512 + pr_ * 128)
            rotary_evac(pb, pres, b % 2, KT[:, pr_, b * 512:(b + 1) * 512], r_KT[b])

        def v_chunk(b, pr_):
            pb, pres = proj_chunk(b % 2, 1024 + pr_ * 128)
            dst = VT[:, pr_, b * 512:(b + 1) * 512]
            vev[0] += 1
            if vev[0] % 2:
                cx.add("act", (lambda e: e.copy(dst, pb)), reads=[pres], writes=[r_VT[b]])
            else:
                cx.add("dve", (lambda e: e.tensor_copy(dst, pb)), reads=[pres], writes=[r_VT[b]])

        def q_chunk(b, pr_):
            ob = b - 2
            pb, pres = proj_chunk(b % 2, pr_ * 128)
            rotary_evac(pb, pres, b % 2, QT[:, pr_, ob * 512:(ob + 1) * 512], r_QT[ob])

        def glu_chunk(b, c):
            if 2 <= b <= 5:
                n0, n, ucol, ures = 0, 512, 16 + (b - 2) * 512, r_uT[b - 2]
            elif b == 1:
                n0, n, ucol, ures = 496, 16, 0, r_uT[4]
            else:
                n0, n, ucol, ures = 0, 16, 16 + T, r_uT[5]
            pa, pares = proj_chunk(b % 2, 1536 + c * 128, n0, n)
            pg, pgres = proj_chunk(b % 2, 2048 + c * 128, n0, n)
            s_ = glu_ctr[0] % 2
            glu_ctr[0] += 1
            cx.add("act", (lambda e: e.activation(sg[s_][:, 0:n], pg[:, 0:n], AF.Sigmoid)), reads=[pgres], writes=[r_sg[s_]])
            dst = uT[:, c, ucol:ucol + n]
            cx.add("dve", (lambda e: e.tensor_tensor(dst, pa[:, 0:n], sg[s_][:, 0:n], ALU.mult)),
                   reads=[pares, r_sg[s_]], writes=[ures])

        def load_tables(b):
            ts_ = b % 2
            cx.add("sp", (lambda e: e.dma_start(out=tC[ts_], in_=tabC[:, b * 512:(b + 1) * 512])), writes=[r_tabC[ts_]], dma="tabC%d" % ts_)
            cx.add("sp", (lambda e: e.dma_start(out=tS[ts_], in_=tabS[:, b * 512:(b + 1) * 512])), writes=[r_tabS[ts_]], dma="tabS%d" % ts_)

        load_tables(0)
        norm_pre(0, 0)
        norm_tr(0, 0)
        norm_pre(0, 1)
        norm_tr(0, 1)
        norm_pre(1, 0)
        for g_ in (0, 3, 4):
            load_win(g_, extra_reads=[r_xnT[0]])
        for b in range(8):
            own = 2 <= b <= 5
            work = [(k_chunk, pr_) for pr_ in range(4)] + [(v_chunk, pr_) for pr_ in range(4)]
            if own:
                work += [(q_chunk, pr_) for pr_ in range(4)]
            if own or b == 1 or b == 6:
                work += [(glu_chunk, c) for c in range(4)]
            n_w = len(work)
            marks = {n_w // 3: 0, (2 * n_w) // 3: 1}
            for wi, (fn_, arg_) in enumerate(work):
                if b + 1 < 8 and wi in marks:
                    if marks[wi] == 0:
                        load_tables(b + 1)
                        norm_tr(b + 1, 0)
                        norm_pre(b + 1, 1)
                    else:
                        norm_tr(b + 1, 1)
                        if b + 2 < 8:
                            norm_pre(b + 2, 0)
                fn_(b, arg_)
        flush_rot()

        def dump(ap_list):
            col = 0
            rr = cx.res("dbgstg")
            for ap in ap_list:
                n = ap.shape[1]
                stg = A(ARENA_BYTES - 8 * KB, F32, 2048)
                for c0 in range(0, n, 2048):
                    m = min(2048, n - c0)
                    cx.add("dve", (lambda e, ap=ap, c0=c0, m=m: e.tensor_copy(stg[:, 0:m], ap[:, c0:c0 + m])), writes=[rr])
                    cx.add("sp", (lambda e, col=col, c0=c0, m=m: e.dma_start(out=dbg_out[:, col + c0:col + c0 + m], in_=stg[:, 0:m])),
                           reads=[rr], dma="dbg")
                    cx.emit_phase()
                col += n

        cx.emit_phase()
        if stop_after == 1:
            if dbg:
                dump([QT[:, 0, :], KT[:, 0, :], VT[:, 0, :], uT[:, 0, :], QT[:, 3, :], KT[:, 3, :]])
                cx.final_wait_all_dma(["dbg"])
            return nc

        o = P1_BASE
        attT = A(o, BF16, 4 * T).rearrange("p (c t) -> p c t", c=4); o += 16 * KB
        convT = A(o, BF16, 4 * T).rearrange("p (c t) -> p c t", c=4); o += 16 * KB
        P2_BASE = o
        r_attT = cx.res("attT")
        acc = A(o, F32, 2 * 2 * T).rearrange("p (a c t) -> p a c t", a=2, c=2); o += 32 * KB
        r_acc = cx.res("acc")
        Vt = A(o, BF16, 32 * 256).rearrange("p (k f) -> p k f", k=32); o += 16 * KB
        r_Vt = [cx.res("Vt%d" % i) for i in range(8)]
        pexp = [A(o + i * KB, BF16, 512) for i in range(6)]; o += 6 * KB
        r_pexp = [cx.res("pexp%d" % i) for i in range(6)]
        pm = [A(o + i * KB, BF16, 512) for i in range(6)]; o += 6 * KB
        r_pm = [cx.res("pm%d" % i) for i in range(6)]
        obanks = [(psf[6], psum_res[6]), (psb.bitcast(F32), psb_res)]
        vrep = A(o, BF16, 69 * 64).rearrange("p (k f) -> p k f", k=69); o += 69 * 128
        r_vrep = cx.res("vrep")
        cx.add("pool", lambda e: e.dma_start(out=vrep, in_=vrep_d.rearrange("p (k f) -> p k f", k=69)), writes=[r_vrep], dma="vrep")
        assert o <= ARENA_BYTES
        r_KTh = [cx.res("KTlo"), cx.res("KThi")]
        r_VTh = [cx.res("VTlo"), cx.res("VThi")]
        r_QTall = r_QT
        diagA = A(20 * KB, BF16, 2 * 31 * 128).rearrange("p (c k m) -> p c k m", c=2, k=31)
        diagB = A(52 * KB, BF16, 2 * 31 * 128).rearrange("p (c k m) -> p c k m", c=2, k=31)
        r_diag = [cx.res("diag%d" % c) for c in range(4)]

        def dg(c, k):
            return (diagA if c < 2 else diagB)[:, c % 2, k, :]

        def diag_ops():
            for c in range(4):
                region = r_KTh[0] if c < 2 else r_VTh[0]
                dfull = diagA if c < 2 else diagB
                for k0 in range(0, 31, 8):
                    nk = min(8, 31 - k0)
                    dst = dfull[:, c % 2, k0:k0 + nk, :]
                    i0 = bass.AP(ident.tensor, ident.offset, [list(ident.ap[0]), [0, nk], [1, 128]])
                    cwv = cw[:, c, k0:k0 + nk]
                    i1 = bass.AP(cwv.tensor, cwv.offset, [list(cwv.ap[0]), [1, nk], [0, 128]])
                    cx.add("pool", (lambda e: e.tensor_tensor(dst, i0, i1, ALU.mult)), reads=[r_const, r_cprm], writes=[r_diag[c], region])
                    yield None

        diag_gen = diag_ops()
        dq_ctr = [0]

        def sap(base3, row0, nrow, c, start, step, n):
            v = base3[row0:row0 + nrow, c, start:start + (n - 1) * step + 1]
            return bass.AP(v.tensor, v.offset, [list(v.ap[0]), [step, n]])

        kt_col = {}
        col = 0
        for d in PATTERNS:
            for r in range(d):
                for m in range(NKT[d]):
                    kt_col[(d, r, m)] = col
                    col += 1
        assert col == 69

        pe_ctr = [0]
        pm_ctr = [0]
        qb_ctr = [0]
        vtb = [(psb, psb_res), (psf[6].bitcast(BF16), psum_res[6])]
        vtb_ctr = [0]
        VtB = A(117 * KB, BF16, 20 * 256).rearrange("p (k f) -> p k f", k=20)
        r_VtB = [cx.res("VtB%d" % i) for i in range(5)]
        vt_bufs = {"A": (Vt, r_Vt), "B": (VtB, r_VtB)}
        plan = {0: [(16, "A"), (1, "B"), (4, "A")], 1: [(1, "B"), (16, "A"), (4, "B")]}

        def build_vtiles(hg, d, vname):
            Vb, r_Vb = vt_bufs[vname]
            nkt = NKT[d]
            tiles = [(r, m) for r in range(d) for m in range(nkt)]
            for g0 in range(0, len(tiles), 4):
                grp = tiles[g0:g0 + 4]
                vb, vres = vtb[vtb_ctr[0] % 2]
                vtb_ctr[0] += 1
                for gi, (r, m) in enumerate(grp):
                    start = HALO + r + d * (128 * m - 64)
                    for pl in range(2):
                        in_ = sap(VT, 0, 128, 2 * hg + pl, start, d, 128)
                        dstp = vb[:, gi * 256 + pl * 128: gi * 256 + (pl + 1) * 128]
                        cx.add("pe", (lambda e: e.transpose(dstp, in_, ident)), reads=[r_VTh[hg], r_const], writes=[vres])
                ng = len(grp)
                dst = Vb[:, g0:g0 + ng, :]
                src = vb[:, 0:ng * 256].rearrange("p (k f) -> p k f", k=ng)
                cx.add("act", (lambda e: e.copy(dst, src)), reads=[vres], writes=[r_Vb[g0 // 4]])

        def stage_a(hg, d, r, n):
            qstart = d * 128 * n + r
            qsel = qb_ctr[0] % 3
            osel = qb_ctr[0] % 2
            qb_ctr[0] += 1
            banks = [(psf[2 * qsel], psum_res[2 * qsel]), (psf[2 * qsel + 1], psum_res[2 * qsel + 1])]
            for e_ in range(2):
                sb, sr = banks[e_]
                for kk in range(2):
                    kstart = HALO + r + d * (128 * (n + kk) - 64)
                    for hl in range(2):
                        lhsT = sap(KT, 64 * e_, 64, 2 * hg + hl, kstart, d, 128)
                        rhs = sap(QT, 64 * e_, 64, 2 * hg + hl, qstart, d, 128)
                        c0 = (kk * 2 + hl) * 128
                        cx.add("pe", (lambda e: e.matmul(sb[:, c0:c0 + 128], lhsT, rhs, start=True, stop=True)),
                               reads=[r_KTh[hg]] + r_QTall, writes=[sr])
            pmi = []
            for e_ in range(2):
                sb, sr = banks[e_]
                s_ = pe_ctr[0] % 6
                pe_ctr[0] += 1
                cx.add("act", (lambda e: e.activation(pexp[s_], sb, AF.Exp, scale=0.125)), reads=[sr], writes=[r_pexp[s_]])
                ps_ = pm_ctr[0] % 6
                pm_ctr[0] += 1
                cx.add("dve" if e_ == 0 else "pool", (lambda e: e.tensor_tensor(pm[ps_], pexp[s_], mask2[:, 256:768], ALU.mult)),
                       reads=[r_pexp[s_], r_const], writes=[r_pm[ps_]])
                pmi.append(ps_)
            return (osel, pmi)

        def stage_b(hg, d, vname, first, r, n, st):
            osel, pmi = st
            Vb, r_Vb = vt_bufs[vname]
            nkt = NKT[d]
            qstart = d * 128 * n + r
            ob, orr = obanks[osel]
            ov = ob.rearrange("p (a c q) -> p a c q", a=2, c=2)
            for hh in range(4):
                pl, e_ = hh // 2, hh % 2
                ps_ = pmi[e_]
                for a_ in range(2):
                    for kk in range(2):
                        vti = r * nkt + n + kk
                        kc = kt_col[(d, r, n + kk)]
                        lhsT = Vb[:, vti, hh * 64:(hh + 1) * 64] if a_ == 0 else vrep[:, kc, :]
                        rhs = pm[ps_][:, (kk * 2 + pl) * 128:(kk * 2 + pl + 1) * 128]
                        dsto = ov[64 * e_:64 * e_ + 64, a_, pl, :]
                        st_, sp_ = (kk == 0), (kk == 1)
                        cx.add("pe", (lambda e: e.matmul(dsto, lhsT, rhs, start=st_, stop=sp_)),
                               reads=[r_pm[ps_], r_Vb[vti // 4], r_vrep], writes=[orr])
            av = acc[:, :, :, qstart:qstart + (127 * d) + 1]
            av = bass.AP(av.tensor, av.offset, [list(av.ap[0]), list(av.ap[1]), list(av.ap[2]), [d, 128]])
            if first:
                cx.add("dve", (lambda e: e.tensor_copy(av, ov)), reads=[orr], writes=[r_acc])
            else:
                cx.add("dve", (lambda e: e.tensor_tensor(av, av, ov, ALU.add)), reads=[orr, r_acc], writes=[r_acc])

        def normalise_hg(hg_):
            for pl in range(2):
                cx.add("act", (lambda e: e.activation(acc[:, 1, pl, :], acc[:, 1, pl, :], AF.Ln)), reads=[r_acc], writes=[r_acc])
                cx.add("act", (lambda e: e.activation(acc[:, 1, pl, :], acc[:, 1, pl, :], AF.Exp, scale=-1.0)), reads=[r_acc], writes=[r_acc])
            for pl in range(2):
                cx.add("dve", (lambda e: e.tensor_tensor(attT[:, 2 * hg_ + pl, :], acc[:, 0, pl, :], acc[:, 1, pl, :], ALU.mult)),
                       reads=[r_acc], writes=[r_attT])

        hg0_norm_pending = [False]
        build_vtiles(0, *plan[0][0])
        for hg in range(2):
            inflight = []
            for pi, (d, vname) in enumerate(plan[hg]):
                nqb = T // d // 128
                cnt = 0
                for r in range(d):
                    for n in range(nqb):
                        if hg == 1:
                            dq_ctr[0] += 1
                            if dq_ctr[0] % 3 == 1:
                                next(diag_gen, None)
                        inflight.append((d, vname, pi == 0, r, n, stage_a(hg, d, r, n)))
                        if len(inflight) > 2:
                            if hg0_norm_pending[0]:
                                normalise_hg(0)
                                hg0_norm_pending[0] = False
                            stage_b(hg, *inflight.pop(0))
                        cnt += 1
                        if cnt == 3:
                            if pi + 1 < 3:
                                build_vtiles(hg, *plan[hg][pi + 1])
                            elif hg == 0:
                                build_vtiles(1, *plan[1][0])
            while inflight:
                stage_b(hg, *inflight.pop(0))
            if hg == 0:
                hg0_norm_pending[0] = True
        for _ in diag_gen:
            pass
        cx.emit_phase()
        if stop_after == 2:
            if dbg:
                dump([attT[:, 0, :], attT[:, 1, :], attT[:, 2, :], attT[:, 3, :]])
                cx.final_wait_all_dma(["dbg"])
            return nc

        woutb = A(4 * KB, BF16, 8 * 1024).rearrange("p (k n) -> p k n", k=8)
        r_wout = [cx.res("wout")] * 4
        wxkb = A(36 * KB, BF16, 8 * 1024).rearrange("p (k n) -> p k n", k=8)
        r_wxk = [cx.res("wxk")] * 4
        wxvb = A(68 * KB, BF16, 8 * 1024).rearrange("p (k n) -> p k n", k=8)
        r_wxv = [cx.res("wxv")] * 4
        wload(woutb, r_wout, w_out.rearrange("(k p) n -> p k n", p=128), "wout")
        wload(wxkb, r_wxk, w_xk.rearrange("(k p) n -> p k n", p=128), "wxk")
        wload(wxvb, r_wxv, w_xv.rearrange("(k p) n -> p k n", p=128), "wxv")
        normalise_hg(1)
        o = P2_BASE + 32 * KB
        yb = [A(o + i * 8 * KB, F32, 2048).rearrange("p (c t) -> p c t", c=4) for i in range(2)]; o += 16 * KB
        r_yb = [cx.res("yb%d" % i) for i in range(2)]
        ybf = [A(o + i * 4 * KB, BF16, 2048).rearrange("p (c t) -> p c t", c=4) for i in range(2)]; o += 8 * KB
        r_ybf = [cx.res("ybf%d" % i) for i in range(2)]
        ysq = [A(o + i * 4 * KB, BF16, 2048).rearrange("p (c t) -> p c t", c=4) for i in range(2)]; o += 8 * KB
        r_ysq = [cx.res("ysq%d" % i) for i in range(2)]
        mean_sb = A(o, F32, 512); o += 2 * KB
        var_sb = A(o, F32, 512); o += 2 * KB
        rs_sb = A(o, F32, 512); o += 2 * KB
        r_st = cx.res("lnstat")
        tt = [A(o + i * 2 * KB, F32, 512) for i in range(2)]; o += 4 * KB
        r_tt = [cx.res("tt%d" % i) for i in range(2)]
        r_convT = cx.res("convT")
        r_uTall = r_uT
        assert o <= ARENA_BYTES, o

        tctr = [0]

        conv_banks = {}

        def conv_mm(nb):
            conv_banks[nb] = []
            for c in range(4):
                pb, pr = bank()
                conv_banks[nb].append((pb, pr))
                for k in range(31):
                    st_, sp_ = (k == 0), (k == 30)
                    cx.add("pe", (lambda e: e.matmul(pb, dg(c, k), uT[:, c, nb * 512 + k + 1: nb * 512 + k + 513], start=st_, stop=sp_)),
                           reads=r_uTall + [r_diag[c]], writes=[pr])

        def conv_evac(nb):
            s_ = nb % 2
            for c in range(4):
                pb, pr = conv_banks[nb][c]
                cx.add("act", (lambda e: e.activation(yb[s_][:, c, :], pb, AF.Identity, bias=cb[:, c:c + 1], scale=1.0)),
                       reads=[pr, r_cprm], writes=[r_yb[s_]])
                cx.add("act", (lambda e: e.activation(ysq[s_][:, c, :], pb, AF.Square, bias=cb[:, c:c + 1], scale=1.0)),
                       reads=[pr, r_cprm], writes=[r_ysq[s_]])
                cx.add("dve", (lambda e: e.tensor_copy(ybf[s_][:, c, :], yb[s_][:, c, :])), reads=[r_yb[s_]], writes=[r_ybf[s_]])

        def ln_stage(nb):
            s_ = nb % 2
            mb, mr = bank()
            for c in range(4):
                st_, sp_ = (c == 0), (c == 3)
                cx.add("pe", (lambda e: e.matmul(mb, onesm, ybf[s_][:, c, :], start=st_, stop=sp_)), reads=[r_ybf[s_], r_const], writes=[mr])
            qb_, qr_ = bank()
            for c in range(4):
                st_, sp_ = (c == 0), (c == 3)
                cx.add("pe", (lambda e: e.matmul(qb_, onesm, ysq[s_][:, c, :], start=st_, stop=sp_)), reads=[r_ysq[s_], r_const], writes=[qr_])
            cx.add("act", (lambda e: e.copy(mean_sb, mb)), reads=[mr], writes=[r_st])
            cx.add("dve", lambda e: e.tensor_tensor(var_sb, mean_sb, mean_sb, ALU.mult), reads=[r_st], writes=[r_st])
            cx.add("dve", (lambda e: e.tensor_tensor(var_sb, qb_, var_sb, ALU.subtract)), reads=[qr_, r_st], writes=[r_st])
            cx.add("act", lambda e: e.activation(var_sb, var_sb, AF.Sqrt, bias=epsc, scale=1.0), reads=[r_st, r_eps], writes=[r_st])
            cx.add("dve", lambda e: e.reciprocal(rs_sb, var_sb), reads=[r_st], writes=[r_st])
            for c in range(4):
                t_ = tctr[0] % 2
                tctr[0] += 1
                cx.add("dve", (lambda e: e.tensor_tensor(tt[t_], yb[s_][:, c, :], mean_sb, ALU.subtract)), reads=[r_yb[s_], r_st], writes=[r_tt[t_]])
                cx.add("dve", (lambda e: e.tensor_tensor(tt[t_], tt[t_], rs_sb, ALU.mult)), reads=[r_tt[t_], r_st], writes=[r_tt[t_]])
                cx.add("act", (lambda e: e.activation(convT[:, c, nb * 512:(nb + 1) * 512], tt[t_], AF.Silu, bias=lnb[:, c:c + 1], scale=lng[:, c:c + 1])),
                       reads=[r_tt[t_], r_cprm], writes=[r_convT])

        conv_mm(0)
        conv_evac(0)
        conv_mm(1)
        conv_evac(1)
        ln_stage(0)
        conv_mm(2)
        conv_evac(2)
        ln_stage(1)
        ln_stage(2)
        conv_mm(3)
        conv_evac(3)
        ln_stage(3)
        cx.emit_phase()
        if stop_after == 3:
            if dbg:
                dump([convT[:, 0, :], convT[:, 1, :], convT[:, 2, :], convT[:, 3, :]])
                cx.final_wait_all_dma(["dbg"])
            return nc

        o = P2_BASE
        hbuf = A(o, F32, 16 * 1024).rearrange("p (n d) -> p n d", n=16); o += 64 * KB
        r_h = [cx.res("h%d" % i) for i in range(16)]
        XKT = A(o, BF16, 8 * 256).rearrange("p (c m) -> p c m", c=8); o += 4 * KB
        XV = A(o, BF16, 2 * 1024).rearrange("p (t n) -> p t n", t=2); o += 4 * KB
        r_XKT = cx.res("XKT")
        r_XV = cx.res("XV")
        assert o <= ARENA_BYTES
        P6_BASE = 4 * KB
        o = 52 * KB
        xt4 = [A(o + i * 4 * KB, F32, 1024) for i in range(2)]; o += 8 * KB
        r_xt4 = [cx.res("xt4_%d" % i) for i in range(2)]
        mnb = A(o, BF16, 1024); o += 2 * KB
        r_mnb = cx.res("mnb")
        mnT = A(o, BF16, 8 * 256).rearrange("p (k m) -> p k m", k=8); o += 4 * KB
        r_mnT = cx.res("mnT")
        junk4 = A(o, BF16, 1024); o += 2 * KB
        r_junk4 = cx.res("junk4")
        assert o <= 68 * KB, o
        o = 20 * KB
        grep4 = A(o, F32, 1024); o += 4 * KB
        r_grep4 = cx.res("grep4")
        st4 = A(o, F32, 12).rearrange("p (a j) -> p a j", a=3); o += 64
        r_st4 = cx.res("st4")
        wxqb = A(84 * KB, BF16, 8 * 1024).rearrange("p (k n) -> p k n", k=8)
        r_wxq = [cx.res("wxq")] * 4
        wload(wxqb, r_wxq, w_xq.rearrange("(k p) n -> p k n", p=128), "wxq")

        cx.add("sp", lambda e: e.dma_start(out=grep4, in_=gvec[2:3, :].partition_broadcast(128)), writes=[r_grep4], dma="grep")
        for i in range(16):
            s_ = i % 2
            cx.add("sp", (lambda e, i=i, s_=s_: e.dma_start(out=xt4[s_], in_=xe_t[8 + i])), writes=[r_xt4[s_]], dma="xt4_%d" % s_)
            for half in range(2):
                pb, pr = bank()
                for k in range(8):
                    lhsT = attT[:, k, i * 128:(i + 1) * 128] if k < 4 else convT[:, k - 4, i * 128:(i + 1) * 128]
                    cx.add("pe", (lambda e, pb=pb, lhsT=lhsT, k=k, half=half: e.matmul(pb, lhsT, woutb[:, k, half * 512:(half + 1) * 512],
                                                                                         start=(k == 0), stop=(k == 7))),
                           reads=[r_attT, r_convT] + r_wout, writes=[pr])
                cx.add("dve", (lambda e, pb=pb, i=i, half=half, s_=s_: e.tensor_tensor(hbuf[:, i, half * 512:(half + 1) * 512], pb,
                                                                                         xt4[s_][:, half * 512:(half + 1) * 512], ALU.add)),
                       reads=[pr, r_xt4[s_]], writes=[r_h[i]])
        mem_t = mem.rearrange("(n p) d -> n p d", p=128)
        for i in range(2):
            cx.add("sp", (lambda e, i=i: e.dma_start(out=xt4[i], in_=mem_t[i])), writes=[r_xt4[i]], dma="xt4_%d" % i)
        rms_stats([xt4[0], xt4[1]], [r_xt4[0], r_xt4[1]], st4[:, 0, :], st4[:, 1, :], st4[:, 2, :], r_st4, junk4, r_junk4)
        for i in range(2):
            cx.add("dve", (lambda e, i=i: e.scalar_tensor_tensor(mnb, xt4[i], st4[:, 2, i:i + 1], grep4, ALU.mult, ALU.mult)),
                   reads=[r_xt4[i], r_st4, r_grep4], writes=[r_mnb])
            transpose_tile(mnb, r_mnb, mnT, r_mnT, i * 128, "act")
        ev = [0]

        def evac_copy(dst, src, sres, dres):
            ev[0] += 1
            if ev[0] % 2:
                cx.add("act", (lambda e: e.copy(dst, src)), reads=[sres], writes=[dres])
            else:
                cx.add("dve", (lambda e: e.tensor_copy(dst, src)), reads=[sres], writes=[dres])

        for c in range(8):
            pb, pr = bank()
            for k in range(8):
                cx.add("pe", (lambda e, pb=pb, c=c, k=k: e.matmul(pb[:, 0:256], wxkb[:, k, c * 128:(c + 1) * 128], mnT[:, k, :],
                                                                  start=(k == 0), stop=(k == 7))),
                       reads=r_wxk + [r_mnT], writes=[pr])
            evac_copy(XKT[:, c, :], pb[:, 0:256], pr, r_XKT)
        for t_ in range(2):
            for half in range(2):
                pb, pr = bank()
                for k in range(8):
                    cx.add("pe", (lambda e, pb=pb, t_=t_, half=half, k=k: e.matmul(pb, mnT[:, k, t_ * 128:(t_ + 1) * 128],
                                                                                   wxvb[:, k, half * 512:(half + 1) * 512],
                                                                                   start=(k == 0), stop=(k == 7))),
                           reads=r_wxv + [r_mnT], writes=[pr])
                evac_copy(XV[:, t_, half * 512:(half + 1) * 512], pb, pr, r_XV)
        cx.emit_phase()
        if stop_after == 4:
            if dbg:
                dump([hbuf[:, i, :] for i in range(16)])
                cx.final_wait_all_dma(["dbg"])
            return nc

        W0_BASE = 100 * KB
        W1_BASE = P6_BASE + 32 * KB
        wup = [A(W0_BASE, BF16, 8 * 1024).rearrange("p (k n) -> p k n", k=8), A(W1_BASE, BF16, 8 * 1024).rearrange("p (k n) -> p k n", k=8)]
        wdn = [A(W0_BASE + 16 * KB, BF16, 8 * 1024).rearrange("p (k n) -> p k n", k=8),
               A(W1_BASE + 16 * KB, BF16, 8 * 1024).rearrange("p (k n) -> p k n", k=8)]
        r_wup = [[cx.res("wup%d" % i)] for i in range(2)]
        r_wdn = [[cx.res("wdn%d" % i)] for i in range(2)]
        wup_src = w_up.rearrange("(k p) f -> p k f", p=128)
        wdn_src = w_down.rearrange("(f p) n -> p f n", p=128)

        def load_group(fg):
            s_ = fg % 2
            cx.add("pool", (lambda e: e.dma_start(out=wup[s_], in_=wup_src[:, :, fg * 1024:(fg + 1) * 1024])),
                   writes=[r_wup[s_][0]], dma="wup%d" % s_)
            cx.add("pool", (lambda e: e.dma_start(out=wdn[s_], in_=wdn_src[:, fg * 8:fg * 8 + 8, :])),
                   writes=[r_wdn[s_][0]], dma="wdn%d" % s_)

        o = 4 * KB
        wxob = A(o, BF16, 8 * 1024).rearrange("p (k n) -> p k n", k=8); o += 16 * KB
        r_wxo = [cx.res("wxo")] * 4
        hnT5 = [A(o + i * 8 * KB, BF16, 8 * 512).rearrange("p (k t) -> p k t", k=8) for i in range(2)]; o += 16 * KB
        r_hnT5 = [cx.res("hnT5_%d" % i) for i in range(2)]
        xqT = A(o, BF16, 8 * 512).rearrange("p (k t) -> p k t", k=8); o += 8 * KB
        r_xqT = [cx.res("xqT%d" % i) for i in range(4)]
        xoT2 = [A(o + i * 8 * KB, BF16, 8 * 512).rearrange("p (k t) -> p k t", k=8) for i in range(2)]; o += 16 * KB
        r_xoT2 = [[cx.res("xoT%d_%d" % (j, i)) for i in range(4)] for j in range(2)]
        px = [A(o + i * KB, BF16, 512) for i in range(4)]; o += 4 * KB
        r_px = [cx.res("px%d" % i) for i in range(4)]
        rd5 = [A(o + i * 2 * KB, F32, 512) for i in range(2)]; o += 4 * KB
        r_rd5 = [cx.res("rd5_%d" % i) for i in range(2)]
        hn5 = [A(o + i * 2 * KB, BF16, 1024) for i in range(2)]; o += 4 * KB
        r_hn5 = [cx.res("hn5_%d" % i) for i in range(2)]
        grep5 = A(o, F32, 1024); o += 4 * KB
        r_grep5 = cx.res("grep5")
        junk5 = A(o, BF16, 1024); o += 2 * KB
        r_junk5 = cx.res("junk5")
        st5 = A(o, F32, 48).rearrange("p (a b j) -> p a b j", a=3, b=4); o += 192
        r_st5 = [cx.res("st5_%d" % i) for i in range(4)]
        assert o <= 84 * KB, o

        wload(wxob, r_wxo, w_xo.rearrange("(k p) n -> p k n", p=128), "wxo")
        load_group(0)
        cx.add("sp", lambda e: e.dma_start(out=grep5, in_=gvec[1:2, :].partition_broadcast(128)), writes=[r_grep5], dma="grep")

        def norm_h_stats(nb, st, r_st, junk_, r_junk_):
            tiles = [nb * 4 + i for i in range(4)]
            rms_stats([hbuf[:, t_, :] for t_ in tiles], [r_h[t_] for t_ in tiles], st[:, 0, nb, :], st[:, 1, nb, :], st[:, 2, nb, :],
                      r_st[nb], junk_, r_junk_)

        def norm_h_apply(nb, g_ap, r_g, dstT, r_dstT, col_base, hn, r_hn, st, r_st):
            tiles = [nb * 4 + i for i in range(4)]

            def nrm(i):
                t_, s_ = tiles[i], i % 2
                cx.add("dve", (lambda e: e.scalar_tensor_tensor(hn[s_], hbuf[:, t_, :], st[:, 2, nb, i:i + 1], g_ap, ALU.mult, ALU.mult)),
                       reads=[r_h[t_], r_st[nb], r_g], writes=[r_hn[s_]])

            nrm(0)
            for i in range(4):
                if i + 1 < 4:
                    nrm(i + 1)
                transpose_tile(hn[i % 2], r_hn[i % 2], dstT, r_dstT, col_base + i * 128, "act")

        pxc = [0]
        rdc = [0]

        def xq_stage(nb):
            bs = nb % 2
            for c in range(8):
                pb, pr = bank()
                for k in range(8):
                    st_, sp_ = (k == 0), (k == 7)
                    cx.add("pe", (lambda e: e.matmul(pb, wxqb[:, k, c * 128:(c + 1) * 128], hnT5[bs][:, k, :], start=st_, stop=sp_)),
                           reads=r_wxq + [r_hnT5[bs]], writes=[pr])
                evac_copy(xqT[:, c, :], pb, pr, r_xqT[c // 2])

        def o_part(nb, i):
            xs = nb % 2
            t_ = nb * 4 + i
            for half in range(2):
                pb, pr = bank()
                for k in range(8):
                    st_, sp_ = (k == 0), (k == 7)
                    cx.add("pe", (lambda e: e.matmul(pb, xoT2[xs][:, k, i * 128:(i + 1) * 128], wxob[:, k, half * 512:(half + 1) * 512],
                                                     start=st_, stop=sp_)),
                           reads=r_xoT2[xs] + r_wxo, writes=[pr])
                hv = hbuf[:, t_, half * 512:(half + 1) * 512]
                cx.add("dve", (lambda e: e.tensor_tensor(hv, hv, pb, ALU.add)), reads=[pr, r_h[t_]], writes=[r_h[t_]])

        def head_scores(hx):
            pxs = []
            for kt in range(2):
                sb, sr = bank()
                for cc in range(2):
                    st_, sp_ = (cc == 0), (cc == 1)
                    cx.add("pe", (lambda e: e.matmul(sb, XKT[:, 2 * hx + cc, kt * 128:(kt + 1) * 128], xqT[:, 2 * hx + cc, :], start=st_, stop=sp_)),
                           reads=[r_XKT, r_xqT[hx]], writes=[sr])
                p_ = pxc[0] % 4
                pxc[0] += 1
                cx.add("act", (lambda e: e.activation(px[p_], sb, AF.Exp, scale=1.0 / 16.0)), reads=[sr], writes=[r_px[p_]])
                pxs.append(p_)
            return pxs

        def head_pv(nb, hx, pxs):
            xs = nb % 2
            db, dr = bank()
            for kt in range(2):
                st_, sp_ = (kt == 0), (kt == 1)
                p_ = pxs[kt]
                cx.add("pe", (lambda e: e.matmul(db, ones, px[p_], start=st_, stop=sp_)), reads=[r_px[p_], r_ones], writes=[dr])
            rd_ = rdc[0] % 2
            rdc[0] += 1
            cx.add("act", (lambda e: e.activation(rd5[rd_], db, AF.Ln)), reads=[dr], writes=[r_rd5[rd_]])
            cx.add("act", (lambda e: e.activation(rd5[rd_], rd5[rd_], AF.Exp, scale=-1.0)), reads=[r_rd5[rd_]], writes=[r_rd5[rd_]])
            for cc in range(2):
                ob, orr = bank()
                for kt in range(2):
                    st_, sp_ = (kt == 0), (kt == 1)
                    p_ = pxs[kt]
                    cx.add("pe", (lambda e: e.matmul(ob, XV[:, kt, (2 * hx + cc) * 128:(2 * hx + cc + 1) * 128], px[p_], start=st_, stop=sp_)),
                           reads=[r_px[p_], r_XV], writes=[orr])
                cx.add("dve", (lambda e: e.tensor_tensor(xoT2[xs][:, 2 * hx + cc, :], ob, rd5[rd_], ALU.mult)),
                       reads=[orr, r_rd5[rd_]], writes=[r_xoT2[xs][hx]])

        norm_h_stats(0, st5, r_st5, junk5, r_junk5)
        norm_h_apply(0, grep5, r_grep5, hnT5[0], r_hnT5[0], 0, hn5, r_hn5, st5, r_st5)
        norm_h_stats(1, st5, r_st5, junk5, r_junk5)
        for nb in range(4):
            xq_stage(nb)
            if nb + 1 < 4:
                norm_h_apply(nb + 1, grep5, r_grep5, hnT5[(nb + 1) % 2], r_hnT5[(nb + 1) % 2], 0, hn5, r_hn5, st5, r_st5)
                if nb + 2 < 4:
                    norm_h_stats(nb + 2, st5, r_st5, junk5, r_junk5)
            for hx in range(4):
                pxs = head_scores(hx)
                if nb > 0:
                    o_part(nb - 1, hx)
                head_pv(nb, hx, pxs)
        for i in range(4):
            o_part(3, i)
        cx.emit_phase()
        if stop_after == 5:
            if dbg:
                dump([hbuf[:, i, :] for i in range(16)])
                cx.final_wait_all_dma(["dbg"])
            return nc

        o = P6_BASE
        hnT6 = A(o, BF16, 8 * T).rearrange("p (k t) -> p k t", k=8); o += 32 * KB
        r_hnT6 = [cx.res("hnT6_%d" % i) for i in range(4)]
        o += 32 * KB
        HT = [A(o + i * 8 * KB, BF16, 8 * 512).rearrange("p (f t) -> p f t", f=8) for i in range(2)]; o += 16 * KB
        r_HT = [cx.res("HT%d" % i) for i in range(2)]
        rt = [A(o + i * 2 * KB, F32, 512) for i in range(2)]; o += 4 * KB
        r_rt = [cx.res("rt%d" % i) for i in range(2)]
        ostg = [A(o + i * 4 * KB, F32, 1024) for i in range(2)]; o += 8 * KB
        r_ostg = [cx.res("ostg%d" % i) for i in range(2)]
        hn6 = [A(o + i * 2 * KB, BF16, 1024) for i in range(2)]; o += 4 * KB
        r_hn6 = [cx.res("hn6_%d" % i) for i in range(2)]
        assert o <= W0_BASE
        o = W0_BASE + 32 * KB
        st6 = A(o, F32, 96).rearrange("p (z a b j) -> p z a b j", z=2, a=3, b=4); o += 384
        r_st6 = [[cx.res("st6_%d_%d" % (z, i)) for i in range(4)] for z in range(2)]
        assert o <= P2_BASE, o
        o = P2_BASE + 64 * KB
        grep6 = A(o, F32, 1024); o += 4 * KB
        grepf = A(o, F32, 1024); o += 4 * KB
        r_grep6 = cx.res("grep6")
        junk6 = A(o, BF16, 1024); o += 2 * KB
        r_junk6 = cx.res("junk6")
        assert o <= ARENA_BYTES, o

        load_group(1)
        cx.add("sp", lambda e: e.dma_start(out=grep6, in_=gvec[3:4, :].partition_broadcast(128)), writes=[r_grep6], dma="grep")
        r_grepf = cx.res("grepf")
        cx.add("sp", lambda e: e.dma_start(out=grepf, in_=gvec[4:5, :].partition_broadcast(128)), writes=[r_grepf], dma="grepf")
        htc = [0]
        rtc = [0]
        out_t = out.rearrange("(n p) d -> n p d", p=128)

        def up_stage(fg, nb, hs):
            s_ = fg % 2
            for f in range(8):
                pb, pr = bank()
                for k in range(8):
                    st_, sp_ = (k == 0), (k == 7)
                    cx.add("pe", (lambda e: e.matmul(pb, wup[s_][:, k, f * 128:(f + 1) * 128], hnT6[:, k, nb * 512:(nb + 1) * 512],
                                                     start=st_, stop=sp_)),
                           reads=r_wup[s_] + [r_hnT6[nb]], writes=[pr])
                r_ = rtc[0] % 2
                rtc[0] += 1
                cx.add("act", (lambda e: e.activation(rt[r_], pb, AF.Relu)), reads=[pr], writes=[r_rt[r_]])
                cx.add("pool", (lambda e: e.tensor_tensor(HT[hs][:, f, :], rt[r_], rt[r_], ALU.mult)), reads=[r_rt[r_]], writes=[r_HT[hs]])

        def down_stage(fg, nb, hs):
            s_ = fg % 2
            for i in range(4):
                t_ = nb * 4 + i
                for half in range(2):
                    pb, pr = bank()
                    for f in range(8):
                        st_, sp_ = (f == 0), (f == 7)
                        cx.add("pe", (lambda e: e.matmul(pb, HT[hs][:, f, i * 128:(i + 1) * 128], wdn[s_][:, f, half * 512:(half + 1) * 512],
                                                         start=st_, stop=sp_)),
                               reads=[r_HT[hs]] + r_wdn[s_], writes=[pr])
                    hv = hbuf[:, t_, half * 512:(half + 1) * 512]
                    cx.add("dve", (lambda e: e.tensor_tensor(hv, hv, pb, ALU.add)), reads=[pr, r_h[t_]], writes=[r_h[t_]])
            if fg == 3:
                tiles = [nb * 4 + i for i in range(4)]
                rms_stats([hbuf[:, t_, :] for t_ in tiles], [r_h[t_] for t_ in tiles], st6[:, 1, 0, nb, :], st6[:, 1, 1, nb, :],
                          st6[:, 1, 2, nb, :], r_st6[1][nb], junk6, r_junk6)
                for i, t_ in enumerate(tiles):
                    os_ = i % 2
                    cx.add("dve", (lambda e: e.scalar_tensor_tensor(ostg[os_], hbuf[:, t_, :], st6[:, 1, 2, nb, i:i + 1], grepf,
                                                                    ALU.mult, ALU.mult)),
                           reads=[r_h[t_], r_st6[1][nb], r_grepf], writes=[r_ostg[os_]])
                    cx.add("sp", (lambda e: e.dma_start(out=out_t[t_], in_=ostg[os_])), reads=[r_ostg[os_]], dma="out%d" % os_)
            if nb == 3 and fg + 2 < 4:
                load_group(fg + 2)

        items = [(fg, nb) for fg in range(4) for nb in range(4)]

        def norm6(nb):
            norm_h_apply(nb, grep6, r_grep6, hnT6, r_hnT6[nb], nb * 512, hn6, r_hn6, st6[:, 0], r_st6[0])
            if nb + 1 < 4:
                norm_h_stats(nb + 1, st6[:, 0], r_st6[0], junk6, r_junk6)

        norm_h_stats(0, st6[:, 0], r_st6[0], junk6, r_junk6)
        norm6(0)
        up_stage(0, 0, 0)
        for j in range(len(items)):
            if j + 1 < len(items):
                fg1, nb1 = items[j + 1]
                if fg1 == 0:
                    norm6(nb1)
                up_stage(fg1, nb1, (j + 1) % 2)
            down_stage(items[j][0], items[j][1], j % 2)
        cx.emit_phase()
        cx.final_wait_all_dma(["out0", "out1"])
    return nc


def _host_constants():
    ident = np.eye(128, dtype=np.float32)
    rm = np.zeros((128, 128), np.float32)
    for e in range(2):
        for i in range(8):
            rm[64 * e + 8 + i, 64 * e + i] = -1.0
            rm[64 * e + i, 64 * e + 8 + i] = 1.0
    onesm = np.full((128, 128), 1.0 / 512.0, np.float32)
    p = np.arange(128)[:, None]
    j = np.arange(128)[None, :]
    mask2 = np.concatenate([np.tile((p >= j), (1, 4)), np.tile((p <= j), (1, 4))], axis=1).astype(np.float32)
    return np.concatenate([ident, rm, onesm, mask2], axis=1)


def _tables(t0):
    pos = np.arange(t0 - HALO, t0 - HALO + E).astype(np.float32)
    pos = np.where((pos >= 0) & (pos < S), pos, np.float32(0.0)).astype(np.float32)
    rot = 16
    freqs = (np.float32(500000.0) ** (-np.arange(0, rot, 2, dtype=np.float32) / np.float32(rot))).astype(np.float32)
    ang = (pos[:, None] * freqs[None, :]).astype(np.float32)
    cos = np.cos(ang).astype(np.float32)
    sin = np.sin(ang).astype(np.float32)
    tc = np.ones((128, E), np.float32)
    ts = np.zeros((128, E), np.float32)
    for e in range(2):
        for i in range(16):
            tc[64 * e + i] = cos[:, i % 8]
            ts[64 * e + i] = sin[:, i % 8]
    return tc, ts


def _kbias(t0):
    kb = np.zeros((128, 72), np.float32)
    col = 0
    for d in PATTERNS:
        for r in range(d):
            for m in range(NKT[d]):
                start = t0 + r + d * (128 * m - 64)
                pos = start + d * np.arange(128)
                kb[:, col] = np.where((pos >= 0) & (pos < S), 0.0, -30000.0)
                col += 1
    return kb


_NC_CACHE = {}


def _get_nc(stop_after=99, dbg=False):
    key = (stop_after, dbg)
    if key not in _NC_CACHE:
        _NC_CACHE[key] = build_program(stop_after, dbg)
    return _NC_CACHE[key]


def make_in_maps(x, mem, norm_mix_g, w_in, conv_w, conv_b, conv_ln_g, conv_ln_b, w_out,
                 norm_x_g, norm_mem_g, w_xq, w_xk, w_xv, w_xo, norm_mlp_g, w_up, w_down, norm_final_g):
    f = lambda a: np.ascontiguousarray(np.asarray(a, dtype=np.float32))
    x = f(x)
    mem = f(mem)
    cst = _host_constants()
    gvec = np.stack([f(norm_mix_g)[0], f(norm_x_g)[0], f(norm_mem_g)[0], f(norm_mlp_g)[0], f(norm_final_g)], axis=0)
    cw = f(conv_w)[0]
    cwT = cw.T.reshape(4, 128, 31).transpose(1, 0, 2).reshape(128, 124)
    lay = lambda v: f(v)[0].reshape(4, 128).T
    convp = np.ascontiguousarray(np.concatenate([cwT, lay(conv_b), lay(conv_ln_g), lay(conv_ln_b)], axis=1))
    shared = {
        "cst": cst, "gvec": np.ascontiguousarray(gvec), "convp": convp,
        "w_in": f(w_in)[0], "w_out": f(w_out)[0], "w_xq": f(w_xq)[0], "w_xk": f(w_xk)[0], "w_xv": f(w_xv)[0],
        "w_xo": f(w_xo)[0], "w_up": f(w_up)[0], "w_down": f(w_down)[0],
    }
    in_maps = []
    for c in range(NCORES):
        b, t0 = c // 4, (c % 4) * T
        xe = np.zeros((E, D), np.float32)
        lo, hi = t0 - HALO, t0 + T + HALO
        slo, shi = max(lo, 0), min(hi, S)
        xe[slo - lo: shi - lo] = x[b, slo:shi]
        tc, ts = _tables(t0)
        m = dict(shared)
        kb = _kbias(t0)
        vr = np.ascontiguousarray(np.repeat((kb[:, :69] == 0).astype(np.float32), 64, axis=1))
        m.update({"x_ext": xe, "mem": mem[b], "tabC": tc, "tabS": ts, "kbias": kb, "vrep": vr})
        in_maps.append(m)
    return in_maps


def kernel(**inputs):
    in_maps = make_in_maps(**inputs)
    nc = _get_nc()
    res = run_bass_kernel_spmd(nc, in_maps, core_ids=list(range(NCORES)))
    outp = np.zeros((2, S, D), np.float32)
    for c in range(NCORES):
        b, t0 = c // 4, (c % 4) * T
        outp[b, t0:t0 + T] = res.results[c]["out"]
    return outp
```

```python
import numpy as np
import concourse.bass as bass
import concourse.mybir as mybir
from concourse.bass_utils import run_bass_kernel_spmd
from contextlib import ExitStack
import types

F32 = mybir.dt.float32
BF16 = mybir.dt.bfloat16
AF = mybir.ActivationFunctionType
ALU = mybir.AluOpType
AX = mybir.AxisListType

S = 8192
D = 1024
T = 2048
HALO = 1024
E = T + 2 * HALO
NCORES = 8
DIN = 2560
DFF = 4096
NMEM = 256
EPS = 1e-6
PATTERNS = (1, 4, 16)
NKT = {1: 17, 4: 5, 16: 2}
ARENA_BYTES = 207 * 1024

ENGS = ("sp", "act", "pool", "dve", "pe")
import os as _os
TRACE = open(_os.environ['K_TRACE'], 'w') if _os.environ.get('K_TRACE') else None


class Res:
    __slots__ = ("name", "w", "rs")

    def __init__(self, name=""):
        self.name = name
        self.w = None
        self.rs = []


class Op:
    __slots__ = ("eng", "fn", "reads", "writes", "dma", "dsem", "dcount", "deps", "signal", "sigval", "waits")

    def __init__(self, eng, fn, reads, writes, dma):
        self.eng = eng
        self.fn = fn
        self.reads = reads
        self.writes = writes
        self.dma = dma
        self.dsem = None
        self.dcount = 0
        self.deps = []
        self.signal = False
        self.sigval = None
        self.waits = []


class Ctx:
    def __init__(self, nc, stack):
        self.nc = nc
        self.stack = stack
        self.esem = {e: stack.enter_context(nc.semaphore("s_" + e)) for e in ENGS}
        self.ecount = {e: 0 for e in ENGS}
        self.dsems = {}
        self.dcounts = {}
        self.waited = {e: {} for e in ENGS}
        self.all_res = []
        self.ops = []

    def res(self, name=""):
        r = Res(name)
        self.all_res.append(r)
        return r

    def dsem(self, key):
        if key not in self.dsems:
            self.dsems[key] = self.stack.enter_context(self.nc.semaphore("d_" + key))
            self.dcounts[key] = 0
        return self.dsems[key]

    def add(self, eng, fn, reads=(), writes=(), dma=None):
        if fn.__closure__:
            cells = []
            for c in fn.__closure__:
                try:
                    cells.append(types.CellType(c.cell_contents))
                except ValueError:
                    cells.append(c)
            f2 = types.FunctionType(fn.__code__, fn.__globals__, fn.__name__, fn.__defaults__, tuple(cells))
            f2.__kwdefaults__ = fn.__kwdefaults__
            fn = f2
        op = Op(eng, fn, tuple(reads), tuple(writes), dma)
        deps = []
        for r in op.reads:
            if r.w is not None:
                deps.append((r.w, True))
        for r in op.writes:
            if r.w is not None:
                deps.append((r.w, False))
            for o in r.rs:
                deps.append((o, False))
        for r in op.reads:
            if r not in op.writes:
                r.rs.append(op)
        for r in op.writes:
            r.w = op
            r.rs = []
        if dma is not None:
            self.dsem(dma)
            self.dcounts[dma] += 16
            op.dsem = dma
            op.dcount = self.dcounts[dma]
        seen = {}
        for d, raw in deps:
            if d is op:
                continue
            k = id(d)
            if k in seen:
                seen[k] = (d, seen[k][1] or raw)
            else:
                seen[k] = (d, raw)
        for d, raw in seen.values():
            need = False
            if d.dma is not None:
                need = True
            elif d.eng != eng:
                need = True
            elif eng != "pe":
                need = True
            if need:
                op.waits.append(d)
                if d.dma is None:
                    d.signal = True
        self.ops.append(op)
        return op

    def emit_phase(self):
        nc = self.nc
        ops = self.ops
        self.ops = []
        per = {e: [o for o in ops if o.eng == e] for e in ENGS}
        last = {}
        for e in ENGS:
            for o in reversed(per[e]):
                if o.dma is None:
                    o.signal = True
                    last[e] = o
                    break
        for e in ENGS:
            for o in per[e]:
                if o.dma is None and o.signal:
                    self.ecount[e] += 1
                    o.sigval = self.ecount[e]
        final = {e: self.ecount[e] for e in ENGS if e in last}

        def run(e, engine):
            waited = self.waited[e]
            for o in per[e]:
                need = {}
                for d in o.waits:
                    if d.dma is not None:
                        key = ("d", d.dsem)
                        val = d.dcount
                    else:
                        key = ("e", d.eng)
                        val = d.sigval
                    if val > need.get(key, 0):
                        need[key] = val
                for key, val in need.items():
                    if waited.get(key, 0) >= val:
                        continue
                    sem = self.dsems[key[1]] if key[0] == "d" else self.esem[key[1]]
                    engine.wait_ge(sem, val)
                    waited[key] = val
                    if TRACE is not None:
                        TRACE.write("%s WAIT %s >= %d\n" % (e, key, val))
                ins = o.fn(engine)
                if TRACE is not None:
                    TRACE.write("%s OP %s sig=%s dma=%s\n" % (e, getattr(o.fn, '__qualname__', '?') + ':' + str(o.fn.__code__.co_firstlineno), o.sigval, o.dsem))
                if o.dma is not None:
                    ins.then_inc(self.dsems[o.dsem], 16)
                elif o.signal:
                    ins.then_inc(self.esem[e], 1)
            for f, val in final.items():
                if f == e:
                    continue
                key = ("e", f)
                if waited.get(key, 0) >= val:
                    continue
                engine.wait_ge(self.esem[f], val)
                waited[key] = val

        with nc.Block() as block:
            @block.sync
            def _(eng):
                run("sp", eng)

            @block.scalar
            def _(eng):
                run("act", eng)

            @block.gpsimd
            def _(eng):
                run("pool", eng)

            @block.vector
            def _(eng):
                run("dve", eng)

            @block.tensor
            def _(eng):
                run("pe", eng)
        for r in self.all_res:
            if r.w is not None and r.w.dma is None:
                r.w = None
            r.rs = [o for o in r.rs if o.dma is not None]

    def final_wait_all_dma(self, keys):
        nc = self.nc
        with nc.Block() as block:
            @block.sync
            def _(eng):
                for k in keys:
                    eng.wait_ge(self.dsems[k], self.dcounts[k])


def build_program(stop_after=99, dbg=False):
    nc = bass.Bass("TRN2", target_bir_lowering=False)

    def din(name, shape, dt=F32):
        return nc.dram_tensor(name, list(shape), dt, kind="ExternalInput").ap()

    x_ext = din("x_ext", [E, D])
    mem = din("mem", [NMEM, D])
    tabC = din("tabC", [128, E])
    tabS = din("tabS", [128, E])
    cst = din("cst", [128, 128 * 3 + 1024])
    kbias_d = din("kbias", [128, 72])
    vrep_d = din("vrep", [128, 69 * 64])
    gvec = din("gvec", [5, D])
    convp = din("convp", [128, 4 * 31 + 12])
    w_in = din("w_in", [D, DIN])
    w_out = din("w_out", [D, D])
    w_xq = din("w_xq", [D, D])
    w_xk = din("w_xk", [D, D])
    w_xv = din("w_xv", [D, D])
    w_xo = din("w_xo", [D, D])
    w_up = din("w_up", [D, DFF])
    w_down = din("w_down", [DFF, D])
    out = nc.dram_tensor("out", [T, D], F32, kind="ExternalOutput").ap()
    dbg_out = None
    if dbg:
        dbg_out = nc.dram_tensor("dbg", [128, 20 * 1024], F32, kind="ExternalOutput").ap()

    stack = ExitStack()
    with stack:
        arena = stack.enter_context(nc.sbuf_tensor("arena", [128, ARENA_BYTES // 2], BF16))
        psf = [stack.enter_context(nc.psum_tensor("psf%d" % i, [128, 512], F32)) for i in range(7)]
        psb_h = stack.enter_context(nc.psum_tensor("psb", [128, 1024], BF16))
        psb = psb_h[:, :]
        psf = [p[:, :] for p in psf]
        cx = Ctx(nc, stack)

        def A(off, dt, n):
            assert off % 4 == 0
            if dt == BF16:
                assert off + 2 * n <= ARENA_BYTES, (off, n)
                return arena[:, off // 2: off // 2 + n]
            assert off + 4 * n <= ARENA_BYTES, (off, n)
            return arena[:, off // 2: off // 2 + 2 * n].bitcast(F32)

        KB = 1024
        psum_res = [cx.res("psf%d" % i) for i in range(7)]
        psb_res = cx.res("psb")
        bank_rr = [0]

        def bank():
            i = bank_rr[0] % 6
            bank_rr[0] += 1
            return psf[i], psum_res[i]

        psb2 = psf[6].bitcast(BF16)
        trbufs = [(psb, psb_res), (psb2, psum_res[6])]
        tr_ctr = [0]

        o = 0
        ident = A(o, BF16, 128); o += 256
        Rm = A(o, BF16, 128); o += 256
        onesm = A(o, BF16, 128); o += 256
        mask2 = A(o, BF16, 1024); o += 2048
        ones = A(o, BF16, 128); o += 256
        kbias = A(o, F32, 72); o += 288
        epsc = A(o, F32, 1); o += 4
        cprm = A(o, F32, 4 * 31 + 12); o += 544
        cw = cprm[:, 0:124].rearrange("p (c k) -> p c k", c=4)
        cb = cprm[:, 124:128]
        lng = cprm[:, 128:132]
        lnb = cprm[:, 132:136]
        r_cprm = cx.res("cprm")
        o = 4 * KB
        SMALL_END = o
        r_const = cx.res("const")

        cx.add("pool", lambda e: e.dma_start(out=arena[:, 0:1408], in_=cst), writes=[r_const], dma="cst")
        cx.add("sp", lambda e: e.dma_start(out=cprm, in_=convp), writes=[r_cprm], dma="cprm")
        r_ones = cx.res("ones")
        r_eps = cx.res("eps")
        cx.add("dve", lambda e: e.memset(ones, 1.0), writes=[r_ones])
        cx.add("dve", lambda e: e.memset(epsc, EPS), writes=[r_eps])

        def wload(dst, dst_res, src, key, nsplit=1):
            cx.add("pool", (lambda e: e.dma_start(out=dst, in_=src)), writes=list(dict.fromkeys(dst_res)), dma=key)

        def rms_stats(srcs, src_res, ss, sq, rstd, r_ss, junk, r_junk):
            n = len(srcs)
            for j, (sap, sr) in enumerate(zip(srcs, src_res)):
                cx.add("act", (lambda e, sap=sap, j=j: e.activation(junk, sap, AF.Square, scale=1.0 / 32.0,
                                                                   accum_out=ss[:, j:j + 1])),
                       reads=[sr], writes=[r_junk, r_ss])
            cx.add("act", lambda e: e.activation(sq[:, 0:n], ss[:, 0:n], AF.Sqrt, bias=epsc, scale=1.0),
                   reads=[r_ss, r_eps], writes=[r_ss])
            cx.add("dve", lambda e: e.reciprocal(rstd[:, 0:n], sq[:, 0:n]), reads=[r_ss], writes=[r_ss])

        def transpose_tile(xn_ap, r_xn, dstT, r_dstT, col0, evac_eng):
            tb, tres = trbufs[tr_ctr[0] % 2]
            tr_ctr[0] += 1
            for k in range(8):
                cx.add("pe", (lambda e, k=k: e.transpose(tb[:, k * 128:(k + 1) * 128], xn_ap[:, k * 128:(k + 1) * 128], ident)),
                       reads=[r_xn, r_const], writes=[tres])
            src = tb.rearrange("p (k t) -> p k t", k=8)
            dst = dstT[:, :, col0:col0 + 128]
            if evac_eng == "act":
                cx.add("act", lambda e: e.copy(dst, src), reads=[tres], writes=[r_dstT])
            else:
                cx.add("dve", lambda e: e.tensor_copy(dst, src), reads=[tres], writes=[r_dstT])

        o = SMALL_END
        QT = A(o, BF16, 4 * T).rearrange("p (c t) -> p c t", c=4); o += 16 * KB
        KT = A(o, BF16, 4 * E).rearrange("p (c t) -> p c t", c=4); o += 32 * KB
        VT = A(o, BF16, 4 * E).rearrange("p (c t) -> p c t", c=4); o += 32 * KB
        UW = T + 32
        uT = A(o, BF16, 4 * UW).rearrange("p (c t) -> p c t", c=4); o += 4 * UW * 2
        o = (o + 1023) // 1024 * 1024
        P1_BASE = o
        r_QT = [cx.res("QT%d" % b) for b in range(4)]
        r_KT = [cx.res("KT%d" % b) for b in range(8)]
        r_VT = [cx.res("VT%d" % b) for b in range(8)]
        r_uT = [cx.res("uT%d" % b) for b in range(6)]

        o = P1_BASE
        winb = A(o, BF16, 8 * DIN).rearrange("p (k n) -> p k n", k=8); o += 40 * KB
        xnT = [A(o + i * 8 * KB, BF16, 8 * 512).rearrange("p (k t) -> p k t", k=8) for i in range(2)]; o += 16 * KB
        r_xnT = [cx.res("xnT%d" % i) for i in range(2)]
        xt = [A(o + i * 4 * KB, F32, 1024) for i in range(4)]; o += 16 * KB
        r_xt = [cx.res("xt%d" % i) for i in range(4)]
        xn = [A(o + i * 2 * KB, BF16, 1024) for i in range(2)]; o += 4 * KB
        r_xn = [cx.res("xn%d" % i) for i in range(2)]
        tC = [A(o + i * 2 * KB, F32, 512) for i in range(2)]; o += 4 * KB
        tS = [A(o + i * 2 * KB, F32, 512) for i in range(2)]; o += 4 * KB
        r_tabC = [cx.res("tabC%d" % i) for i in range(2)]
        r_tabS = [cx.res("tabS%d" % i) for i in range(2)]
        qraw = [A(o + i * KB, BF16, 512) for i in range(2)]; o += 2 * KB
        r_qraw = [cx.res("qraw%d" % i) for i in range(2)]
        tmpA = [A(o + i * 2 * KB, F32, 512) for i in range(2)]; o += 4 * KB
        r_tmpA = [cx.res("tmpA%d" % i) for i in range(2)]
        tmpB = [A(o + i * 2 * KB, F32, 512) for i in range(2)]; o += 4 * KB
        r_tmpB = [cx.res("tmpB%d" % i) for i in range(2)]
        sg = [A(o + i * 2 * KB, F32, 512) for i in range(2)]; o += 4 * KB
        r_sg = [cx.res("sg%d" % i) for i in range(2)]
        grep = A(o, F32, 1024); o += 4 * KB
        r_grep = cx.res("grep")
        junk = A(o, BF16, 1024); o += 2 * KB
        r_junk = cx.res("junk")
        ss = A(o, F32, 32).rearrange("p (b j) -> p b j", b=8); o += 128
        sq = A(o, F32, 32).rearrange("p (b j) -> p b j", b=8); o += 128
        rstd = A(o, F32, 32).rearrange("p (b j) -> p b j", b=8); o += 128
        r_ss = [cx.res("ss%d" % b) for b in range(8)]
        assert o <= ARENA_BYTES, o

        w_in_v = w_in.rearrange("(k p) n -> p k n", p=128)
        r_win = [cx.res("win%d" % i) for i in range(5)]
        def load_win(g_, extra_reads=()):
            cx.add("pool", (lambda e: e.dma_start(out=winb[:, :, g_ * 512:(g_ + 1) * 512], in_=w_in_v[:, :, g_ * 512:(g_ + 1) * 512])),
                   reads=list(extra_reads), writes=[r_win[g_]], dma="win%d" % g_)

        load_win(1)
        load_win(2)
        cx.add("sp", lambda e: e.dma_start(out=grep, in_=gvec[0:1, :].partition_broadcast(128)),
               writes=[r_grep], dma="grep")

        xe_t = x_ext.rearrange("(n p) d -> n p d", p=128)
        tile_ctr = [0]
        ev_ctr = [0]

        def load_x_tile(n, dst, dres, key):
            cx.add("sp", (lambda e: e.dma_start(out=dst, in_=xe_t[n])), writes=[dres], dma=key)

        def norm_pre(b, half):
            idx = [2 * half, 2 * half + 1]
            for i in idx:
                load_x_tile(b * 4 + i, xt[i], r_xt[i], "xt%d" % i)
            rms_stats([xt[i] for i in idx], [r_xt[i] for i in idx],
                      ss[:, b, 2 * half:2 * half + 2], sq[:, b, 2 * half:2 * half + 2],
                      rstd[:, b, 2 * half:2 * half + 2], r_ss[b], junk, r_junk)
            for i in idx:
                cx.add("dve", (lambda e: e.scalar_tensor_tensor(xn[i % 2], xt[i], rstd[:, b, i:i + 1], grep, ALU.mult, ALU.mult)),
                       reads=[r_xt[i], r_ss[b], r_grep], writes=[r_xn[i % 2]])

        def norm_tr(b, half):
            bs = b % 2
            for i in (2 * half, 2 * half + 1):
                ev_ctr[0] += 1
                transpose_tile(xn[i % 2], r_xn[i % 2], xnT[bs], r_xnT[bs], i * 128, "act" if ev_ctr[0] % 2 else "dve")

        def proj_chunk(bs, c0, n0=0, n=512):
            pb, pr = bank()
            for k in range(8):
                cx.add("pe", (lambda e, k=k: e.matmul(pb[:, 0:n], winb[:, k, c0:c0 + 128], xnT[bs][:, k, n0:n0 + n],
                                                      start=(k == 0), stop=(k == 7))),
                       reads=[r_xnT[bs], r_win[c0 // 512]], writes=[pr])
            flush_rot()
            return pb, pr

        rot_ctr = [0]

        pending_rot = []

        def flush_rot():
            while pending_rot:
                pending_rot.pop(0)()

        def rotary_evac(pb, pr, ts_, dst, dres):
            s_ = rot_ctr[0] % 2
            rot_ctr[0] += 1
            cx.add("act", lambda e: e.copy(qraw[s_], pb), reads=[pr], writes=[r_qraw[s_]])

            def rest():
                qb, qr = bank()
                cx.add("pe", lambda e: e.matmul(qb, Rm, qraw[s_], start=True, stop=True), reads=[r_qraw[s_], r_const], writes=[qr])
                cx.add("dve", lambda e: e.tensor_tensor(tmpA[s_], qb, tS[ts_], ALU.mult), reads=[qr, r_tabS[ts_]], writes=[r_tmpA[s_]])
                cx.add("pool", lambda e: e.tensor_tensor(tmpB[s_], qraw[s_], tC[ts_], ALU.mult), reads=[r_qraw[s_], r_tabC[ts_]], writes=[r_tmpB[s_]])
                cx.add("dve", lambda e: e.tensor_tensor(dst, tmpA[s_], tmpB[s_], ALU.add), reads=[r_tmpA[s_], r_tmpB[s_]], writes=[dres])
            pending_rot.append(rest)

        glu_ctr = [0]
        vev = [0]

        def k_chunk(b, pr_):
            pb, pres = proj_chunk(b % 2, 512 + pr_ * 128)
            rotary_evac(pb, pres, b % 2, KT[:, pr_, b * 512:(b + 1) * 512], r_KT[b])

        def v_chunk(b, pr_):
            pb, pres = proj_chunk(b % 2, 1024 + pr_ * 128)
            dst = VT[:, pr_, b * 512:(b + 1) * 512]
            vev[0] += 1
            if vev[0] % 2:
                cx.add("act", (lambda e: e.copy(dst, pb)), reads=[pres], writes=[r_VT[b]])
            else:
                cx.add("dve", (lambda e: e.tensor_copy(dst, pb)), reads=[pres], writes=[r_VT[b]])

        def q_chunk(b, pr_):
            ob = b - 2
            pb, pres = proj_chunk(b % 2, pr_ * 128)
            rotary_evac(pb, pres, b % 2, QT[:, pr_, ob * 512:(ob + 1) * 512], r_QT[ob])

        def glu_chunk(b, c):
            if 2 <= b <= 5:
                n0, n, ucol, ures = 0, 512, 16 + (b - 2) * 512, r_uT[b - 2]
            elif b == 1:
                n0, n, ucol, ures = 496, 16, 0, r_uT[4]
            else:
                n0, n, ucol, ures = 0, 16, 16 + T, r_uT[5]
            pa, pares = proj_chunk(b % 2, 1536 + c * 128, n0, n)
            pg, pgres = proj_chunk(b % 2, 2048 + c * 128, n0, n)
            s_ = glu_ctr[0] % 2
            glu_ctr[0] += 1
            cx.add("act", (lambda e: e.activation(sg[s_][:, 0:n], pg[:, 0:n], AF.Sigmoid)), reads=[pgres], writes=[r_sg[s_]])
            dst = uT[:, c, ucol:ucol + n]
            cx.add("dve", (lambda e: e.tensor_tensor(dst, pa[:, 0:n], sg[s_][:, 0:n], ALU.mult)),
                   reads=[pares, r_sg[s_]], writes=[ures])

        def load_tables(b):
            ts_ = b % 2
            cx.add("sp", (lambda e: e.dma_start(out=tC[ts_], in_=tabC[:, b * 512:(b + 1) * 512])), writes=[r_tabC[ts_]], dma="tabC%d" % ts_)
            cx.add("sp", (lambda e: e.dma_start(out=tS[ts_], in_=tabS[:, b * 512:(b + 1) * 512])), writes=[r_tabS[ts_]], dma="tabS%d" % ts_)

        load_tables(0)
        norm_pre(0, 0)
        norm_tr(0, 0)
        norm_pre(0, 1)
        norm_tr(0, 1)
        norm_pre(1, 0)
        for g_ in (0, 3, 4):
            load_win(g_, extra_reads=[r_xnT[0]])
        for b in range(8):
            own = 2 <= b <= 5
            work = [(k_chunk, pr_) for pr_ in range(4)] + [(v_chunk, pr_) for pr_ in range(4)]
            if own:
                work += [(q_chunk, pr_) for pr_ in range(4)]
            if own or b == 1 or b == 6:
                work += [(glu_chunk, c) for c in range(4)]
            n_w = len(work)
            marks = {n_w // 3: 0, (2 * n_w) // 3: 1}
            for wi, (fn_, arg_) in enumerate(work):
                if b + 1 < 8 and wi in marks:
                    if marks[wi] == 0:
                        load_tables(b + 1)
                        norm_tr(b + 1, 0)
                        norm_pre(b + 1, 1)
                    else:
                        norm_tr(b + 1, 1)
                        if b + 2 < 8:
                            norm_pre(b + 2, 0)
                fn_(b, arg_)
        flush_rot()

        def dump(ap_list):
            col = 0
            rr = cx.res("dbgstg")
            for ap in ap_list:
                n = ap.shape[1]
                stg = A(ARENA_BYTES - 8 * KB, F32, 2048)
                for c0 in range(0, n, 2048):
                    m = min(2048, n - c0)
                    cx.add("dve", (lambda e, ap=ap, c0=c0, m=m: e.tensor_copy(stg[:, 0:m], ap[:, c0:c0 + m])), writes=[rr])
                    cx.add("sp", (lambda e, col=col, c0=c0, m=m: e.dma_start(out=dbg_out[:, col + c0:col + c0 + m], in_=stg[:, 0:m])),
                           reads=[rr], dma="dbg")
                    cx.emit_phase()
                col += n

        cx.emit_phase()
        if stop_after == 1:
            if dbg:
                dump([QT[:, 0, :], KT[:, 0, :], VT[:, 0, :], uT[:, 0, :], QT[:, 3, :], KT[:, 3, :]])
                cx.final_wait_all_dma(["dbg"])
            return nc

        o = P1_BASE
        attT = A(o, BF16, 4 * T).rearrange("p (c t) -> p c t", c=4); o += 16 * KB
        convT = A(o, BF16, 4 * T).rearrange("p (c t) -> p c t", c=4); o += 16 * KB
        P2_BASE = o
        r_attT = cx.res("attT")
        acc = A(o, F32, 2 * 2 * T).rearrange("p (a c t) -> p a c t", a=2, c=2); o += 32 * KB
        r_acc = cx.res("acc")
        Vt = A(o, BF16, 32 * 256).rearrange("p (k f) -> p k f", k=32); o += 16 * KB
        r_Vt = [cx.res("Vt%d" % i) for i in range(8)]
        pexp = [A(o + i * KB, BF16, 512) for i in range(6)]; o += 6 * KB
        r_pexp = [cx.res("pexp%d" % i) for i in range(6)]
        pm = [A(o + i * KB, BF16, 512) for i in range(6)]; o += 6 * KB
        r_pm = [cx.res("pm%d" % i) for i in range(6)]
        obanks = [(psf[6], psum_res[6]), (psb.bitcast(F32), psb_res)]
        vrep = A(o, BF16, 69 * 64).rearrange("p (k f) -> p k f", k=69); o += 69 * 128
        r_vrep = cx.res("vrep")
        cx.add("pool", lambda e: e.dma_start(out=vrep, in_=vrep_d.rearrange("p (k f) -> p k f", k=69)), writes=[r_vrep], dma="vrep")
        assert o <= ARENA_BYTES
        r_KTh = [cx.res("KTlo"), cx.res("KThi")]
        r_VTh = [cx.res("VTlo"), cx.res("VThi")]
        r_QTall = r_QT
        diagA = A(20 * KB, BF16, 2 * 31 * 128).rearrange("p (c k m) -> p c k m", c=2, k=31)
        diagB = A(52 * KB, BF16, 2 * 31 * 128).rearrange("p (c k m) -> p c k m", c=2, k=31)
        r_diag = [cx.res("diag%d" % c) for c in range(4)]

        def dg(c, k):
            return (diagA if c < 2 else diagB)[:, c % 2, k, :]

        def diag_ops():
            for c in range(4):
                region = r_KTh[0] if c < 2 else r_VTh[0]
                dfull = diagA if c < 2 else diagB
                for k0 in range(0, 31, 8):
                    nk = min(8, 31 - k0)
                    dst = dfull[:, c % 2, k0:k0 + nk, :]
                    i0 = bass.AP(ident.tensor, ident.offset, [list(ident.ap[0]), [0, nk], [1, 128]])
                    cwv = cw[:, c, k0:k0 + nk]
                    i1 = bass.AP(cwv.tensor, cwv.offset, [list(cwv.ap[0]), [1, nk], [0, 128]])
                    cx.add("pool", (lambda e: e.tensor_tensor(dst, i0, i1, ALU.mult)), reads=[r_const, r_cprm], writes=[r_diag[c], region])
                    yield None

        diag_gen = diag_ops()
        dq_ctr = [0]

        def sap(base3, row0, nrow, c, start, step, n):
            v = base3[row0:row0 + nrow, c, start:start + (n - 1) * step + 1]
            return bass.AP(v.tensor, v.offset, [list(v.ap[0]), [step, n]])

        kt_col = {}
        col = 0
        for d in PATTERNS:
            for r in range(d):
                for m in range(NKT[d]):
                    kt_col[(d, r, m)] = col
                    col += 1
        assert col == 69

        pe_ctr = [0]
        pm_ctr = [0]
        qb_ctr = [0]
        vtb = [(psb, psb_res), (psf[6].bitcast(BF16), psum_res[6])]
        vtb_ctr = [0]
        VtB = A(117 * KB, BF16, 20 * 256).rearrange("p (k f) -> p k f", k=20)
        r_VtB = [cx.res("VtB%d" % i) for i in range(5)]
        vt_bufs = {"A": (Vt, r_Vt), "B": (VtB, r_VtB)}
        plan = {0: [(16, "A"), (1, "B"), (4, "A")], 1: [(1, "B"), (16, "A"), (4, "B")]}

        def build_vtiles(hg, d, vname):
            Vb, r_Vb = vt_bufs[vname]
            nkt = NKT[d]
            tiles = [(r, m) for r in range(d) for m in range(nkt)]
            for g0 in range(0, len(tiles), 4):
                grp = tiles[g0:g0 + 4]
                vb, vres = vtb[vtb_ctr[0] % 2]
                vtb_ctr[0] += 1
                for gi, (r, m) in enumerate(grp):
                    start = HALO + r + d * (128 * m - 64)
                    for pl in range(2):
                        in_ = sap(VT, 0, 128, 2 * hg + pl, start, d, 128)
                        dstp = vb[:, gi * 256 + pl * 128: gi * 256 + (pl + 1) * 128]
                        cx.add("pe", (lambda e: e.transpose(dstp, in_, ident)), reads=[r_VTh[hg], r_const], writes=[vres])
                ng = len(grp)
                dst = Vb[:, g0:g0 + ng, :]
                src = vb[:, 0:ng * 256].rearrange("p (k f) -> p k f", k=ng)
                cx.add("act", (lambda e: e.copy(dst, src)), reads=[vres], writes=[r_Vb[g0 // 4]])

        def stage_a(hg, d, r, n):
            qstart = d * 128 * n + r
            qsel = qb_ctr[0] % 3
            osel = qb_ctr[0] % 2
            qb_ctr[0] += 1
            banks = [(psf[2 * qsel], psum_res[2 * qsel]), (psf[2 * qsel + 1], psum_res[2 * qsel + 1])]
            for e_ in range(2):
                sb, sr = banks[e_]
                for kk in range(2):
                    kstart = HALO + r + d * (128 * (n + kk) - 64)
                    for hl in range(2):
                        lhsT = sap(KT, 64 * e_, 64, 2 * hg + hl, kstart, d, 128)
                        rhs = sap(QT, 64 * e_, 64, 2 * hg + hl, qstart, d, 128)
                        c0 = (kk * 2 + hl) * 128
                        cx.add("pe", (lambda e: e.matmul(sb[:, c0:c0 + 128], lhsT, rhs, start=True, stop=True)),
                               reads=[r_KTh[hg]] + r_QTall, writes=[sr])
            pmi = []
            for e_ in range(2):
                sb, sr = banks[e_]
                s_ = pe_ctr[0] % 6
                pe_ctr[0] += 1
                cx.add("act", (lambda e: e.activation(pexp[s_], sb, AF.Exp, scale=0.125)), reads=[sr], writes=[r_pexp[s_]])
                ps_ = pm_ctr[0] % 6
                pm_ctr[0] += 1
                cx.add("dve" if e_ == 0 else "pool", (lambda e: e.tensor_tensor(pm[ps_], pexp[s_], mask2[:, 256:768], ALU.mult)),
                       reads=[r_pexp[s_], r_const], writes=[r_pm[ps_]])
                pmi.append(ps_)
            return (osel, pmi)

        def stage_b(hg, d, vname, first, r, n, st):
            osel, pmi = st
            Vb, r_Vb = vt_bufs[vname]
            nkt = NKT[d]
            qstart = d * 128 * n + r
            ob, orr = obanks[osel]
            ov = ob.rearrange("p (a c q) -> p a c q", a=2, c=2)
            for hh in range(4):
                pl, e_ = hh // 2, hh % 2
                ps_ = pmi[e_]
                for a_ in range(2):
                    for kk in range(2):
                        vti = r * nkt + n + kk
                        kc = kt_col[(d, r, n + kk)]
                        lhsT = Vb[:, vti, hh * 64:(hh + 1) * 64] if a_ == 0 else vrep[:, kc, :]
                        rhs = pm[ps_][:, (kk * 2 + pl) * 128:(kk * 2 + pl + 1) * 128]
                        dsto = ov[64 * e_:64 * e_ + 64, a_, pl, :]
                        st_, sp_ = (kk == 0), (kk == 1)
                        cx.add("pe", (lambda e: e.matmul(dsto, lhsT, rhs, start=st_, stop=sp_)),
                               reads=[r_pm[ps_], r_Vb[vti // 4], r_vrep], writes=[orr])
            av = acc[:, :, :, qstart:qstart + (127 * d) + 1]
            av = bass.AP(av.tensor, av.offset, [list(av.ap[0]), list(av.ap[1]), list(av.ap[2]), [d, 128]])
            if first:
                cx.add("dve", (lambda e: e.tensor_copy(av, ov)), reads=[orr], writes=[r_acc])
            else:
                cx.add("dve", (lambda e: e.tensor_tensor(av, av, ov, ALU.add)), reads=[orr, r_acc], writes=[r_acc])

        def normalise_hg(hg_):
            for pl in range(2):
                cx.add("act", (lambda e: e.activation(acc[:, 1, pl, :], acc[:, 1, pl, :], AF.Ln)), reads=[r_acc], writes=[r_acc])
                cx.add("act", (lambda e: e.activation(acc[:, 1, pl, :], acc[:, 1, pl, :], AF.Exp, scale=-1.0)), reads=[r_acc], writes=[r_acc])
            for pl in range(2):
                cx.add("dve", (lambda e: e.tensor_tensor(attT[:, 2 * hg_ + pl, :], acc[:, 0, pl, :], acc[:, 1, pl, :], ALU.mult)),
                       reads=[r_acc], writes=[r_attT])

        hg0_norm_pending = [False]
        build_vtiles(0, *plan[0][0])
        for hg in range(2):
            inflight = []
            for pi, (d, vname) in enumerate(plan[hg]):
                nqb = T // d // 128
                cnt = 0
                for r in range(d):
                    for n in range(nqb):
                        if hg == 1:
                            dq_ctr[0] += 1
                            if dq_ctr[0] % 3 == 1:
                                next(diag_gen, None)
                        inflight.append((d, vname, pi == 0, r, n, stage_a(hg, d, r, n)))
                        if len(inflight) > 2:
                            if hg0_norm_pending[0]:
                                normalise_hg(0)
                                hg0_norm_pending[0] = False
                            stage_b(hg, *inflight.pop(0))
                        cnt += 1
                        if cnt == 3:
                            if pi + 1 < 3:
                                build_vtiles(hg, *plan[hg][pi + 1])
                            elif hg == 0:
                                build_vtiles(1, *plan[1][0])
            while inflight:
                stage_b(hg, *inflight.pop(0))
            if hg == 0:
                hg0_norm_pending[0] = True
        for _ in diag_gen:
            pass
        cx.emit_phase()
        if stop_after == 2:
            if dbg:
                dump([attT[:, 0, :], attT[:, 1, :], attT[:, 2, :], attT[:, 3, :]])
                cx.final_wait_all_dma(["dbg"])
            return nc

        woutb = A(4 * KB, BF16, 8 * 1024).rearrange("p (k n) -> p k n", k=8)
        r_wout = [cx.res("wout")] * 4
        wxkb = A(36 * KB, BF16, 8 * 1024).rearrange("p (k n) -> p k n", k=8)
        r_wxk = [cx.res("wxk")] * 4
        wxvb = A(68 * KB, BF16, 8 * 1024).rearrange("p (k n) -> p k n", k=8)
        r_wxv = [cx.res("wxv")] * 4
        wload(woutb, r_wout, w_out.rearrange("(k p) n -> p k n", p=128), "wout")
        wload(wxkb, r_wxk, w_xk.rearrange("(k p) n -> p k n", p=128), "wxk")
        wload(wxvb, r_wxv, w_xv.rearrange("(k p) n -> p k n", p=128), "wxv")
        normalise_hg(1)
        o = P2_BASE + 32 * KB
        yb = [A(o + i * 8 * KB, F32, 2048).rearrange("p (c t) -> p c t", c=4) for i in range(2)]; o += 16 * KB
        r_yb = [cx.res("yb%d" % i) for i in range(2)]
        ybf = [A(o + i * 4 * KB, BF16, 2048).rearrange("p (c t) -> p c t", c=4) for i in range(2)]; o += 8 * KB
        r_ybf = [cx.res("ybf%d" % i) for i in range(2)]
        ysq = [A(o + i * 4 * KB, BF16, 2048).rearrange("p (c t) -> p c t", c=4) for i in range(2)]; o += 8 * KB
        r_ysq = [cx.res("ysq%d" % i) for i in range(2)]
        mean_sb = A(o, F32, 512); o += 2 * KB
        var_sb = A(o, F32, 512); o += 2 * KB
        rs_sb = A(o, F32, 512); o += 2 * KB
        r_st = cx.res("lnstat")
        tt = [A(o + i * 2 * KB, F32, 512) for i in range(2)]; o += 4 * KB
        r_tt = [cx.res("tt%d" % i) for i in range(2)]
        r_convT = cx.res("convT")
        r_uTall = r_uT
        assert o <= ARENA_BYTES, o

        tctr = [0]

        conv_banks = {}

        def conv_mm(nb):
            conv_banks[nb] = []
            for c in range(4):
                pb, pr = bank()
                conv_banks[nb].append((pb, pr))
                for k in range(31):
                    st_, sp_ = (k == 0), (k == 30)
                    cx.add("pe", (lambda e: e.matmul(pb, dg(c, k), uT[:, c, nb * 512 + k + 1: nb * 512 + k + 513], start=st_, stop=sp_)),
                           reads=r_uTall + [r_diag[c]], writes=[pr])

        def conv_evac(nb):
            s_ = nb % 2
            for c in range(4):
                pb, pr = conv_banks[nb][c]
                cx.add("act", (lambda e: e.activation(yb[s_][:, c, :], pb, AF.Identity, bias=cb[:, c:c + 1], scale=1.0)),
                       reads=[pr, r_cprm], writes=[r_yb[s_]])
                cx.add("act", (lambda e: e.activation(ysq[s_][:, c, :], pb, AF.Square, bias=cb[:, c:c + 1], scale=1.0)),
                       reads=[pr, r_cprm], writes=[r_ysq[s_]])
                cx.add("dve", (lambda e: e.tensor_copy(ybf[s_][:, c, :], yb[s_][:, c, :])), reads=[r_yb[s_]], writes=[r_ybf[s_]])

        def ln_stage(nb):
            s_ = nb % 2
            mb, mr = bank()
            for c in range(4):
                st_, sp_ = (c == 0), (c == 3)
                cx.add("pe", (lambda e: e.matmul(mb, onesm, ybf[s_][:, c, :], start=st_, stop=sp_)), reads=[r_ybf[s_], r_const], writes=[mr])
            qb_, qr_ = bank()
            for c in range(4):
                st_, sp_ = (c == 0), (c == 3)
                cx.add("pe", (lambda e: e.matmul(qb_, onesm, ysq[s_][:, c, :], start=st_, stop=sp_)), reads=[r_ysq[s_], r_const], writes=[qr_])
            cx.add("act", (lambda e: e.copy(mean_sb, mb)), reads=[mr], writes=[r_st])
            cx.add("dve", lambda e: e.tensor_tensor(var_sb, mean_sb, mean_sb, ALU.mult), reads=[r_st], writes=[r_st])
            cx.add("dve", (lambda e: e.tensor_tensor(var_sb, qb_, var_sb, ALU.subtract)), reads=[qr_, r_st], writes=[r_st])
            cx.add("act", lambda e: e.activation(var_sb, var_sb, AF.Sqrt, bias=epsc, scale=1.0), reads=[r_st, r_eps], writes=[r_st])
            cx.add("dve", lambda e: e.reciprocal(rs_sb, var_sb), reads=[r_st], writes=[r_st])
            for c in range(4):
                t_ = tctr[0] % 2
                tctr[0] += 1
                cx.add("dve", (lambda e: e.tensor_tensor(tt[t_], yb[s_][:, c, :], mean_sb, ALU.subtract)), reads=[r_yb[s_], r_st], writes=[r_tt[t_]])
                cx.add("dve", (lambda e: e.tensor_tensor(tt[t_], tt[t_], rs_sb, ALU.mult)), reads=[r_tt[t_], r_st], writes=[r_tt[t_]])
                cx.add("act", (lambda e: e.activation(convT[:, c, nb * 512:(nb + 1) * 512], tt[t_], AF.Silu, bias=lnb[:, c:c + 1], scale=lng[:, c:c + 1])),
                       reads=[r_tt[t_], r_cprm], writes=[r_convT])

        conv_mm(0)
        conv_evac(0)
        conv_mm(1)
        conv_evac(1)
        ln_stage(0)
        conv_mm(2)
        conv_evac(2)
        ln_stage(1)
        ln_stage(2)
        conv_mm(3)
        conv_evac(3)
        ln_stage(3)
        cx.emit_phase()
        if stop_after == 3:
            if dbg:
                dump([convT[:, 0, :], convT[:, 1, :], convT[:, 2, :], convT[:, 3, :]])
                cx.final_wait_all_dma(["dbg"])
            return nc

        o = P2_BASE
        hbuf = A(o, F32, 16 * 1024).rearrange("p (n d) -> p n d", n=16); o += 64 * KB
        r_h = [cx.res("h%d" % i) for i in range(16)]
        XKT = A(o, BF16, 8 * 256).rearrange("p (c m) -> p c m", c=8); o += 4 * KB
        XV = A(o, BF16, 2 * 1024).rearrange("p (t n) -> p t n", t=2); o += 4 * KB
        r_XKT = cx.res("XKT")
        r_XV = cx.res("XV")
        assert o <= ARENA_BYTES
        P6_BASE = 4 * KB
        o = 52 * KB
        xt4 = [A(o + i * 4 * KB, F32, 1024) for i in range(2)]; o += 8 * KB
        r_xt4 = [cx.res("xt4_%d" % i) for i in range(2)]
        mnb = A(o, BF16, 1024); o += 2 * KB
        r_mnb = cx.res("mnb")
        mnT = A(o, BF16, 8 * 256).rearrange("p (k m) -> p k m", k=8); o += 4 * KB
        r_mnT = cx.res("mnT")
        junk4 = A(o, BF16, 1024); o += 2 * KB
        r_junk4 = cx.res("junk4")
        assert o <= 68 * KB, o
        o = 20 * KB
        grep4 = A(o, F32, 1024); o += 4 * KB
        r_grep4 = cx.res("grep4")
        st4 = A(o, F32, 12).rearrange("p (a j) -> p a j", a=3); o += 64
        r_st4 = cx.res("st4")
        wxqb = A(84 * KB, BF16, 8 * 1024).rearrange("p (k n) -> p k n", k=8)
        r_wxq = [cx.res("wxq")] * 4
        wload(wxqb, r_wxq, w_xq.rearrange("(k p) n -> p k n", p=128), "wxq")

        cx.add("sp", lambda e: e.dma_start(out=grep4, in_=gvec[2:3, :].partition_broadcast(128)), writes=[r_grep4], dma="grep")
        memt = [A(26 * KB + i * 4 * KB, F32, 1024) for i in range(2)]
        r_memt = [cx.res("memt%d" % i) for i in range(2)]
        mem_t = mem.rearrange("(n p) d -> n p d", p=128)
        ev = [0]

        def evac_copy(dst, src, sres, dres):
            ev[0] += 1
            if ev[0] % 2:
                cx.add("act", (lambda e: e.copy(dst, src)), reads=[sres], writes=[dres])
            else:
                cx.add("dve", (lambda e: e.tensor_copy(dst, src)), reads=[sres], writes=[dres])

        def mem_pre():
            for i in range(2):
                cx.add("sp", (lambda e: e.dma_start(out=memt[i], in_=mem_t[i])), writes=[r_memt[i]], dma="memt%d" % i)
            rms_stats([memt[0], memt[1]], [r_memt[0], r_memt[1]], st4[:, 0, :], st4[:, 1, :], st4[:, 2, :], r_st4, junk4, r_junk4)

        def mem_mm():
            for i in range(2):
                cx.add("dve", (lambda e: e.scalar_tensor_tensor(mnb, memt[i], st4[:, 2, i:i + 1], grep4, ALU.mult, ALU.mult)),
                       reads=[r_memt[i], r_st4, r_grep4], writes=[r_mnb])
                transpose_tile(mnb, r_mnb, mnT, r_mnT, i * 128, "act")
            for c in range(8):
                pb, pr = bank()
                for k in range(8):
                    st_, sp_ = (k == 0), (k == 7)
                    cx.add("pe", (lambda e: e.matmul(pb[:, 0:256], wxkb[:, k, c * 128:(c + 1) * 128], mnT[:, k, :], start=st_, stop=sp_)),
                           reads=r_wxk + [r_mnT], writes=[pr])
                evac_copy(XKT[:, c, :], pb[:, 0:256], pr, r_XKT)
            for t_ in range(2):
                for half in range(2):
                    pb, pr = bank()
                    for k in range(8):
                        st_, sp_ = (k == 0), (k == 7)
                        cx.add("pe", (lambda e: e.matmul(pb, mnT[:, k, t_ * 128:(t_ + 1) * 128], wxvb[:, k, half * 512:(half + 1) * 512],
                                                         start=st_, stop=sp_)),
                               reads=r_wxv + [r_mnT], writes=[pr])
                    evac_copy(XV[:, t_, half * 512:(half + 1) * 512], pb, pr, r_XV)

        mem_pre()
        for i in range(16):
            s_ = i % 2
            cx.add("sp", (lambda e: e.dma_start(out=xt4[s_], in_=xe_t[8 + i])), writes=[r_xt4[s_]], dma="xt4_%d" % s_)
            for half in range(2):
                pb, pr = bank()
                for k in range(8):
                    lhsT = attT[:, k, i * 128:(i + 1) * 128] if k < 4 else convT[:, k - 4, i * 128:(i + 1) * 128]
                    st_, sp_ = (k == 0), (k == 7)
                    cx.add("pe", (lambda e: e.matmul(pb, lhsT, woutb[:, k, half * 512:(half + 1) * 512], start=st_, stop=sp_)),
                           reads=[r_attT, r_convT] + r_wout, writes=[pr])
                cx.add("dve", (lambda e: e.tensor_tensor(hbuf[:, i, half * 512:(half + 1) * 512], pb,
                                                         xt4[s_][:, half * 512:(half + 1) * 512], ALU.add)),
                       reads=[pr, r_xt4[s_]], writes=[r_h[i]])
            if i == 9:
                mem_mm()
        cx.emit_phase()
        if stop_after == 4:
            if dbg:
                dump([hbuf[:, i, :] for i in range(16)])
                cx.final_wait_all_dma(["dbg"])
            return nc

        W0_BASE = 100 * KB
        W1_BASE = P6_BASE + 32 * KB
        wup = [A(W0_BASE, BF16, 8 * 1024).rearrange("p (k n) -> p k n", k=8), A(W1_BASE, BF16, 8 * 1024).rearrange("p (k n) -> p k n", k=8)]
        wdn = [A(W0_BASE + 16 * KB, BF16, 8 * 1024).rearrange("p (k n) -> p k n", k=8),
               A(W1_BASE + 16 * KB, BF16, 8 * 1024).rearrange("p (k n) -> p k n", k=8)]
        r_wup = [[cx.res("wup%d" % i)] for i in range(2)]
        r_wdn = [[cx.res("wdn%d" % i)] for i in range(2)]
        wup_src = w_up.rearrange("(k p) f -> p k f", p=128)
        wdn_src = w_down.rearrange("(f p) n -> p f n", p=128)

        def load_group(fg):
            s_ = fg % 2
            cx.add("pool", (lambda e: e.dma_start(out=wup[s_], in_=wup_src[:, :, fg * 1024:(fg + 1) * 1024])),
                   writes=[r_wup[s_][0]], dma="wup%d" % s_)
            cx.add("pool", (lambda e: e.dma_start(out=wdn[s_], in_=wdn_src[:, fg * 8:fg * 8 + 8, :])),
                   writes=[r_wdn[s_][0]], dma="wdn%d" % s_)

        o = 4 * KB
        wxob = A(o, BF16, 8 * 1024).rearrange("p (k n) -> p k n", k=8); o += 16 * KB
        r_wxo = [cx.res("wxo")] * 4
        hnT5 = [A(o + i * 8 * KB, BF16, 8 * 512).rearrange("p (k t) -> p k t", k=8) for i in range(2)]; o += 16 * KB
        r_hnT5 = [cx.res("hnT5_%d" % i) for i in range(2)]
        xqT = A(o, BF16, 8 * 512).rearrange("p (k t) -> p k t", k=8); o += 8 * KB
        r_xqT = [cx.res("xqT%d" % i) for i in range(4)]
        xoT2 = [A(o + i * 8 * KB, BF16, 8 * 512).rearrange("p (k t) -> p k t", k=8) for i in range(2)]; o += 16 * KB
        r_xoT2 = [[cx.res("xoT%d_%d" % (j, i)) for i in range(4)] for j in range(2)]
        px = [A(o + i * KB, BF16, 512) for i in range(4)]; o += 4 * KB
        r_px = [cx.res("px%d" % i) for i in range(4)]
        rd5 = [A(o + i * 2 * KB, F32, 512) for i in range(2)]; o += 4 * KB
        r_rd5 = [cx.res("rd5_%d" % i) for i in range(2)]
        hn5 = [A(o + i * 2 * KB, BF16, 1024) for i in range(2)]; o += 4 * KB
        r_hn5 = [cx.res("hn5_%d" % i) for i in range(2)]
        grep5 = A(o, F32, 1024); o += 4 * KB
        r_grep5 = cx.res("grep5")
        junk5 = A(o, BF16, 1024); o += 2 * KB
        r_junk5 = cx.res("junk5")
        st5 = A(o, F32, 48).rearrange("p (a b j) -> p a b j", a=3, b=4); o += 192
        r_st5 = [cx.res("st5_%d" % i) for i in range(4)]
        assert o <= 84 * KB, o

        wload(wxob, r_wxo, w_xo.rearrange("(k p) n -> p k n", p=128), "wxo")
        load_group(0)
        cx.add("sp", lambda e: e.dma_start(out=grep5, in_=gvec[1:2, :].partition_broadcast(128)), writes=[r_grep5], dma="grep")

        def norm_h_stats(nb, st, r_st, junk_, r_junk_):
            tiles = [nb * 4 + i for i in range(4)]
            rms_stats([hbuf[:, t_, :] for t_ in tiles], [r_h[t_] for t_ in tiles], st[:, 0, nb, :], st[:, 1, nb, :], st[:, 2, nb, :],
                      r_st[nb], junk_, r_junk_)

        def norm_h_apply(nb, g_ap, r_g, dstT, r_dstT, col_base, hn, r_hn, st, r_st):
            tiles = [nb * 4 + i for i in range(4)]

            def nrm(i):
                t_, s_ = tiles[i], i % 2
                cx.add("dve", (lambda e: e.scalar_tensor_tensor(hn[s_], hbuf[:, t_, :], st[:, 2, nb, i:i + 1], g_ap, ALU.mult, ALU.mult)),
                       reads=[r_h[t_], r_st[nb], r_g], writes=[r_hn[s_]])

            nrm(0)
            for i in range(4):
                if i + 1 < 4:
                    nrm(i + 1)
                transpose_tile(hn[i % 2], r_hn[i % 2], dstT, r_dstT, col_base + i * 128, "act")

        pxc = [0]
        rdc = [0]

        def xq_stage(nb):
            bs = nb % 2
            for c in range(8):
                pb, pr = bank()
                for k in range(8):
                    st_, sp_ = (k == 0), (k == 7)
                    cx.add("pe", (lambda e: e.matmul(pb, wxqb[:, k, c * 128:(c + 1) * 128], hnT5[bs][:, k, :], start=st_, stop=sp_)),
                           reads=r_wxq + [r_hnT5[bs]], writes=[pr])
                evac_copy(xqT[:, c, :], pb, pr, r_xqT[c // 2])

        def o_part(nb, i):
            xs = nb % 2
            t_ = nb * 4 + i
            for half in range(2):
                pb, pr = bank()
                for k in range(8):
                    st_, sp_ = (k == 0), (k == 7)
                    cx.add("pe", (lambda e: e.matmul(pb, xoT2[xs][:, k, i * 128:(i + 1) * 128], wxob[:, k, half * 512:(half + 1) * 512],
                                                     start=st_, stop=sp_)),
                           reads=r_xoT2[xs] + r_wxo, writes=[pr])
                hv = hbuf[:, t_, half * 512:(half + 1) * 512]
                cx.add("dve", (lambda e: e.tensor_tensor(hv, hv, pb, ALU.add)), reads=[pr, r_h[t_]], writes=[r_h[t_]])

        def head_scores(hx):
            pxs = []
            for kt in range(2):
                sb, sr = bank()
                for cc in range(2):
                    st_, sp_ = (cc == 0), (cc == 1)
                    cx.add("pe", (lambda e: e.matmul(sb, XKT[:, 2 * hx + cc, kt * 128:(kt + 1) * 128], xqT[:, 2 * hx + cc, :], start=st_, stop=sp_)),
                           reads=[r_XKT, r_xqT[hx]], writes=[sr])
                p_ = pxc[0] % 4
                pxc[0] += 1
                cx.add("act", (lambda e: e.activation(px[p_], sb, AF.Exp, scale=1.0 / 16.0)), reads=[sr], writes=[r_px[p_]])
                pxs.append(p_)
            return pxs

        def head_pv(nb, hx, pxs):
            xs = nb % 2
            db, dr = bank()
            for kt in range(2):
                st_, sp_ = (kt == 0), (kt == 1)
                p_ = pxs[kt]
                cx.add("pe", (lambda e: e.matmul(db, ones, px[p_], start=st_, stop=sp_)), reads=[r_px[p_], r_ones], writes=[dr])
            rd_ = rdc[0] % 2
            rdc[0] += 1
            cx.add("act", (lambda e: e.activation(rd5[rd_], db, AF.Ln)), reads=[dr], writes=[r_rd5[rd_]])
            cx.add("act", (lambda e: e.activation(rd5[rd_], rd5[rd_], AF.Exp, scale=-1.0)), reads=[r_rd5[rd_]], writes=[r_rd5[rd_]])
            for cc in range(2):
                ob, orr = bank()
                for kt in range(2):
                    st_, sp_ = (kt == 0), (kt == 1)
                    p_ = pxs[kt]
                    cx.add("pe", (lambda e: e.matmul(ob, XV[:, kt, (2 * hx + cc) * 128:(2 * hx + cc + 1) * 128], px[p_], start=st_, stop=sp_)),
                           reads=[r_px[p_], r_XV], writes=[orr])
                cx.add("dve", (lambda e: e.tensor_tensor(xoT2[xs][:, 2 * hx + cc, :], ob, rd5[rd_], ALU.mult)),
                       reads=[orr, r_rd5[rd_]], writes=[r_xoT2[xs][hx]])

        norm_h_stats(0, st5, r_st5, junk5, r_junk5)
        norm_h_apply(0, grep5, r_grep5, hnT5[0], r_hnT5[0], 0, hn5, r_hn5, st5, r_st5)
        norm_h_stats(1, st5, r_st5, junk5, r_junk5)
        for nb in range(4):
            xq_stage(nb)
            if nb + 1 < 4:
                norm_h_apply(nb + 1, grep5, r_grep5, hnT5[(nb + 1) % 2], r_hnT5[(nb + 1) % 2], 0, hn5, r_hn5, st5, r_st5)
                if nb + 2 < 4:
                    norm_h_stats(nb + 2, st5, r_st5, junk5, r_junk5)
            for hx in range(4):
                pxs = head_scores(hx)
                if nb > 0:
                    o_part(nb - 1, hx)
                head_pv(nb, hx, pxs)
        for i in range(4):
            o_part(3, i)
        cx.emit_phase()
        if stop_after == 5:
            if dbg:
                dump([hbuf[:, i, :] for i in range(16)])
                cx.final_wait_all_dma(["dbg"])
            return nc

        o = P6_BASE
        hnT6 = A(o, BF16, 8 * T).rearrange("p (k t) -> p k t", k=8); o += 32 * KB
        r_hnT6 = [cx.res("hnT6_%d" % i) for i in range(4)]
        o += 32 * KB
        HT = [A(o + i * 8 * KB, BF16, 8 * 512).rearrange("p (f t) -> p f t", f=8) for i in range(2)]; o += 16 * KB
        r_HT = [cx.res("HT%d" % i) for i in range(2)]
        rt = [A(o + i * 2 * KB, F32, 512) for i in range(2)]; o += 4 * KB
        r_rt = [cx.res("rt%d" % i) for i in range(2)]
        ostg = [A(o + i * 4 * KB, F32, 1024) for i in range(2)]; o += 8 * KB
        r_ostg = [cx.res("ostg%d" % i) for i in range(2)]
        hn6 = [A(o + i * 2 * KB, BF16, 1024) for i in range(2)]; o += 4 * KB
        r_hn6 = [cx.res("hn6_%d" % i) for i in range(2)]
        assert o <= W0_BASE
        o = W0_BASE + 32 * KB
        st6 = A(o, F32, 96).rearrange("p (z a b j) -> p z a b j", z=2, a=3, b=4); o += 384
        r_st6 = [[cx.res("st6_%d_%d" % (z, i)) for i in range(4)] for z in range(2)]
        assert o <= P2_BASE, o
        o = P2_BASE + 64 * KB
        grep6 = A(o, F32, 1024); o += 4 * KB
        grepf = A(o, F32, 1024); o += 4 * KB
        r_grep6 = cx.res("grep6")
        junk6 = A(o, BF16, 1024); o += 2 * KB
        r_junk6 = cx.res("junk6")
        assert o <= ARENA_BYTES, o

        load_group(1)
        cx.add("sp", lambda e: e.dma_start(out=grep6, in_=gvec[3:4, :].partition_broadcast(128)), writes=[r_grep6], dma="grep")
        r_grepf = cx.res("grepf")
        cx.add("sp", lambda e: e.dma_start(out=grepf, in_=gvec[4:5, :].partition_broadcast(128)), writes=[r_grepf], dma="grepf")
        htc = [0]
        rtc = [0]
        out_t = out.rearrange("(n p) d -> n p d", p=128)

        def up_stage(fg, nb, hs):
            s_ = fg % 2
            for f in range(8):
                pb, pr = bank()
                for k in range(8):
                    st_, sp_ = (k == 0), (k == 7)
                    cx.add("pe", (lambda e: e.matmul(pb, wup[s_][:, k, f * 128:(f + 1) * 128], hnT6[:, k, nb * 512:(nb + 1) * 512],
                                                     start=st_, stop=sp_)),
                           reads=r_wup[s_] + [r_hnT6[nb]], writes=[pr])
                r_ = rtc[0] % 2
                rtc[0] += 1
                cx.add("act", (lambda e: e.activation(rt[r_], pb, AF.Relu)), reads=[pr], writes=[r_rt[r_]])
                cx.add("pool", (lambda e: e.tensor_tensor(HT[hs][:, f, :], rt[r_], rt[r_], ALU.mult)), reads=[r_rt[r_]], writes=[r_HT[hs]])

        def down_stage(fg, nb, hs):
            s_ = fg % 2
            for i in range(4):
                t_ = nb * 4 + i
                for half in range(2):
                    pb, pr = bank()
                    for f in range(8):
                        st_, sp_ = (f == 0), (f == 7)
                        cx.add("pe", (lambda e: e.matmul(pb, HT[hs][:, f, i * 128:(i + 1) * 128], wdn[s_][:, f, half * 512:(half + 1) * 512],
                                                         start=st_, stop=sp_)),
                               reads=[r_HT[hs]] + r_wdn[s_], writes=[pr])
                    hv = hbuf[:, t_, half * 512:(half + 1) * 512]
                    cx.add("dve", (lambda e: e.tensor_tensor(hv, hv, pb, ALU.add)), reads=[pr, r_h[t_]], writes=[r_h[t_]])
            if fg == 3:
                tiles = [nb * 4 + i for i in range(4)]
                rms_stats([hbuf[:, t_, :] for t_ in tiles], [r_h[t_] for t_ in tiles], st6[:, 1, 0, nb, :], st6[:, 1, 1, nb, :],
                          st6[:, 1, 2, nb, :], r_st6[1][nb], junk6, r_junk6)
                for i, t_ in enumerate(tiles):
                    os_ = i % 2
                    cx.add("dve", (lambda e: e.scalar_tensor_tensor(ostg[os_], hbuf[:, t_, :], st6[:, 1, 2, nb, i:i + 1], grepf,
                                                                    ALU.mult, ALU.mult)),
                           reads=[r_h[t_], r_st6[1][nb], r_grepf], writes=[r_ostg[os_]])
                    cx.add("sp", (lambda e: e.dma_start(out=out_t[t_], in_=ostg[os_])), reads=[r_ostg[os_]], dma="out%d" % os_)
            if nb == 3 and fg + 2 < 4:
                load_group(fg + 2)

        items = [(fg, nb) for fg in range(4) for nb in range(4)]

        def norm6(nb):
            norm_h_apply(nb, grep6, r_grep6, hnT6, r_hnT6[nb], nb * 512, hn6, r_hn6, st6[:, 0], r_st6[0])
            if nb + 1 < 4:
                norm_h_stats(nb + 1, st6[:, 0], r_st6[0], junk6, r_junk6)

        norm_h_stats(0, st6[:, 0], r_st6[0], junk6, r_junk6)
        norm6(0)
        up_stage(0, 0, 0)
        for j in range(len(items)):
            if j + 1 < len(items):
                fg1, nb1 = items[j + 1]
                if fg1 == 0:
                    norm6(nb1)
                up_stage(fg1, nb1, (j + 1) % 2)
            down_stage(items[j][0], items[j][1], j % 2)
        cx.emit_phase()
        cx.final_wait_all_dma(["out0", "out1"])
    return nc


def _host_constants():
    ident = np.eye(128, dtype=np.float32)
    rm = np.zeros((128, 128), np.float32)
    for e in range(2):
        for i in range(8):
            rm[64 * e + 8 + i, 64 * e + i] = -1.0
            rm[64 * e + i, 64 * e + 8 + i] = 1.0
    onesm = np.full((128, 128), 1.0 / 512.0, np.float32)
    p = np.arange(128)[:, None]
    j = np.arange(128)[None, :]
    mask2 = np.concatenate([np.tile((p >= j), (1, 4)), np.tile((p <= j), (1, 4))], axis=1).astype(np.float32)
    return np.concatenate([ident, rm, onesm, mask2], axis=1)


def _tables(t0):
    pos = np.arange(t0 - HALO, t0 - HALO + E).astype(np.float32)
    pos = np.where((pos >= 0) & (pos < S), pos, np.float32(0.0)).astype(np.float32)
    rot = 16
    freqs = (np.float32(500000.0) ** (-np.arange(0, rot, 2, dtype=np.float32) / np.float32(rot))).astype(np.float32)
    ang = (pos[:, None] * freqs[None, :]).astype(np.float32)
    cos = np.cos(ang).astype(np.float32)
    sin = np.sin(ang).astype(np.float32)
    tc = np.ones((128, E), np.float32)
    ts = np.zeros((128, E), np.float32)
    for e in range(2):
        for i in range(16):
            tc[64 * e + i] = cos[:, i % 8]
            ts[64 * e + i] = sin[:, i % 8]
    return tc, ts


def _kbias(t0):
    kb = np.zeros((128, 72), np.float32)
    col = 0
    for d in PATTERNS:
        for r in range(d):
            for m in range(NKT[d]):
                start = t0 + r + d * (128 * m - 64)
                pos = start + d * np.arange(128)
                kb[:, col] = np.where((pos >= 0) & (pos < S), 0.0, -30000.0)
                col += 1
    return kb


_NC_CACHE = {}


def _get_nc(stop_after=99, dbg=False):
    key = (stop_after, dbg)
    if key not in _NC_CACHE:
        _NC_CACHE[key] = build_program(stop_after, dbg)
    return _NC_CACHE[key]


def make_in_maps(x, mem, norm_mix_g, w_in, conv_w, conv_b, conv_ln_g, conv_ln_b, w_out,
                 norm_x_g, norm_mem_g, w_xq, w_xk, w_xv, w_xo, norm_mlp_g, w_up, w_down, norm_final_g):
    f = lambda a: np.ascontiguousarray(np.asarray(a, dtype=np.float32))
    x = f(x)
    mem = f(mem)
    cst = _host_constants()
    gvec = np.stack([f(norm_mix_g)[0], f(norm_x_g)[0], f(norm_mem_g)[0], f(norm_mlp_g)[0], f(norm_final_g)], axis=0)
    cw = f(conv_w)[0]
    cwT = cw.T.reshape(4, 128, 31).transpose(1, 0, 2).reshape(128, 124)
    lay = lambda v: f(v)[0].reshape(4, 128).T
    convp = np.ascontiguousarray(np.concatenate([cwT, lay(conv_b), lay(conv_ln_g), lay(conv_ln_b)], axis=1))
    shared = {
        "cst": cst, "gvec": np.ascontiguousarray(gvec), "convp": convp,
        "w_in": f(w_in)[0], "w_out": f(w_out)[0], "w_xq": f(w_xq)[0], "w_xk": f(w_xk)[0], "w_xv": f(w_xv)[0],
        "w_xo": f(w_xo)[0], "w_up": f(w_up)[0], "w_down": f(w_down)[0],
    }
    in_maps = []
    for c in range(NCORES):
        b, t0 = c // 4, (c % 4) * T
        xe = np.zeros((E, D), np.float32)
        lo, hi = t0 - HALO, t0 + T + HALO
        slo, shi = max(lo, 0), min(hi, S)
        xe[slo - lo: shi - lo] = x[b, slo:shi]
        tc, ts = _tables(t0)
        m = dict(shared)
        kb = _kbias(t0)
        vr = np.ascontiguousarray(np.repeat((kb[:, :69] == 0).astype(np.float32), 64, axis=1))
        m.update({"x_ext": xe, "mem": mem[b], "tabC": tc, "tabS": ts, "kbias": kb, "vrep": vr})
        in_maps.append(m)
    return in_maps


def kernel(**inputs):
    in_maps = make_in_maps(**inputs)
    nc = _get_nc()
    res = run_bass_kernel_spmd(nc, in_maps, core_ids=list(range(NCORES)))
    outp = np.zeros((2, S, D), np.float32)
    for c in range(NCORES):
        b, t0 = c // 4, (c % 4) * T
        outp[b, t0:t0 + T] = res.results[c]["out"]
    return outp
```
